# Optimizing a Trainium2 kernel written in Bass

```python
import jax, jax.numpy as jnp
from jax import lax
import numpy as np

D_MODEL = 1024
BATCH = 2
SEQ = 16384
DEPTH = 4

GRID_W = 64
CTX_LEN = 256
HEAD_DIM = 64
A_Q_HEADS = 8
A_KV_HEADS = 2
C_Q_HEADS = 8
C_KV_HEADS = 2
WINDOW = 128
BLOCK = 128
CONV_CH = D_MODEL // 2
CONV_K = 3
D_FF = 2816
N_BRANCH = 3
N_MOD = 9
ROPE_THETA = 10000.0
EPS = 1e-6
NEG_INF = -1e30
SCALE = HEAD_DIM ** -0.5
A_Q = A_Q_HEADS * HEAD_DIM
A_KV = A_KV_HEADS * HEAD_DIM
C_Q = C_Q_HEADS * HEAD_DIM
C_KV = C_KV_HEADS * HEAD_DIM
KV_SIZES = (A_KV, A_KV, C_KV, C_KV)
IN_SIZES = KV_SIZES + (A_Q, C_Q, CONV_CH, CONV_CH, CONV_CH)
KV_COLS = sum(KV_SIZES)
IN_COLS = sum(IN_SIZES)

kernel_name = 'hybrid_dit_parallel_mixers'


def _split(z, sizes):
    return jnp.split(z, [int(v) for v in np.cumsum(sizes)[:-1]], axis=-1)


def _rmsnorm(x, g):
    xf = x.astype(jnp.float32)
    y = xf * lax.rsqrt(jnp.mean(xf * xf, axis=-1, keepdims=True) + EPS)
    return (y * g.astype(jnp.float32)).astype(x.dtype)


def _modulate(x, g, shift, scale):
    return _rmsnorm(x, g) * (1 + scale) + shift


def _ada(cvec, w, b):
    m = jax.nn.silu(cvec) @ w + b
    m = m.reshape(m.shape[:-1] + (N_MOD, 1, D_MODEL))
    return [m[..., j, :, :] for j in range(N_MOD)]


def _swiglu(h, wg, wu, wd):
    return (jax.nn.silu(h @ wg) * (h @ wu)) @ wd


def _heads(z, n):
    return z.reshape(z.shape[:-1] + (n, HEAD_DIM))


def _rope_tables(rows, dtype):
    row = jnp.repeat(jnp.arange(rows), GRID_W).astype(jnp.float32)
    col = jnp.tile(jnp.arange(GRID_W), rows).astype(jnp.float32)
    half = HEAD_DIM // 2
    inv = ROPE_THETA ** (-jnp.arange(0, half, 2, dtype=jnp.float32) / half)
    ang_r = row[:, None] * inv
    ang_c = col[:, None] * inv
    tabs = (jnp.cos(ang_r), jnp.sin(ang_r), jnp.cos(ang_c), jnp.sin(ang_c))
    return tuple(t.astype(dtype)[None, :, None, :] for t in tabs)


def _rot_half(u, cos, sin):
    u1, u2 = jnp.split(u, 2, axis=-1)
    return jnp.concatenate([u1 * cos - u2 * sin, u2 * cos + u1 * sin], axis=-1)


def _rope2d(x, tabs):
    cr, sr, cc, sc = tabs
    xr, xc = jnp.split(x, 2, axis=-1)
    return jnp.concatenate([_rot_half(xr, cr, sr), _rot_half(xc, cc, sc)], axis=-1)


def _qk(z, n, g, tabs):
    y = _rmsnorm(_heads(z, n), g)
    return y if tabs is None else _rope2d(y, tabs)


def _softmax(s, sink_g):
    if sink_g is None:
        return jax.nn.softmax(s, axis=-1)
    col = jnp.broadcast_to(sink_g.astype(jnp.float32)[None, :, :, None, None], s.shape[:-1] + (1,))
    return jax.nn.softmax(jnp.concatenate([s, col], axis=-1), axis=-1)[..., :-1]


def _ctx_attn(q, k, v, sink):
    B, L, HQ, _ = q.shape
    KV = k.shape[2]
    G = HQ // KV
    qg = q.reshape(B, L, KV, G, HEAD_DIM)
    s = jnp.einsum('bqkgd,bnkd->bkgqn', qg, k).astype(jnp.float32) * SCALE
    p = _softmax(s, None if sink is None else sink.reshape(KV, G))
    o = jnp.einsum('bkgqn,bnkd->bqkgd', p.astype(v.dtype), v)
    return o.reshape(B, L, HQ * HEAD_DIM)


def _window_attn(q, k, v, k_ctx, v_ctx, sink):
    B, S, HQ, _ = q.shape
    KV = k.shape[2]
    G = HQ // KV
    nb = S // BLOCK
    span = BLOCK + 2 * WINDOW
    qb = jnp.moveaxis(q.reshape(B, nb, BLOCK, KV, G, HEAD_DIM), 1, 0)
    pad = ((0, 0), (WINDOW, WINDOW), (0, 0), (0, 0))
    kp = jnp.pad(k, pad)
    vp = jnp.pad(v, pad)
    sink_g = sink.reshape(KV, G)
    ctx_ok = jnp.ones((BLOCK, k_ctx.shape[1]), dtype=bool)

    def block(args):
        i, qi = args
        start = i * BLOCK
        keys = jnp.concatenate([k_ctx, lax.dynamic_slice_in_dim(kp, start, span, axis=1)], axis=1)
        vals = jnp.concatenate([v_ctx, lax.dynamic_slice_in_dim(vp, start, span, axis=1)], axis=1)
        qpos = start + jnp.arange(BLOCK)
        kpos = start - WINDOW + jnp.arange(span)
        ok = (jnp.abs(qpos[:, None] - kpos[None, :]) <= WINDOW) & (kpos >= 0)[None, :] & (kpos < S)[None, :]
        ok = jnp.concatenate([ctx_ok, ok], axis=1)
        s = jnp.einsum('bqkgd,bnkd->bkgqn', qi, keys).astype(jnp.float32) * SCALE
        p = _softmax(jnp.where(ok, s, NEG_INF), sink_g)
        return jnp.einsum('bkgqn,bnkd->bqkgd', p.astype(vals.dtype), vals)

    o = lax.map(block, (jnp.arange(nb), qb))
    return jnp.moveaxis(o, 0, 1).reshape(B, S, HQ * HEAD_DIM)


def _global_attn(q, k, v, k_ctx, v_ctx):
    B, S, HQ, _ = q.shape
    KV = k.shape[2]
    G = HQ // KV
    nb = S // BLOCK
    keys = jnp.concatenate([k_ctx, k], axis=1)
    vals = jnp.concatenate([v_ctx, v], axis=1)
    qb = jnp.moveaxis(q.reshape(B, nb, BLOCK, KV, G, HEAD_DIM), 1, 0)

    def block(qi):
        s = jnp.einsum('bqkgd,bnkd->bkgqn', qi, keys).astype(jnp.float32) * SCALE
        p = jax.nn.softmax(s, axis=-1)
        return jnp.einsum('bkgqn,bnkd->bqkgd', p.astype(vals.dtype), vals)

    o = lax.map(block, qb)
    return jnp.moveaxis(o, 0, 1).reshape(B, S, HQ * HEAD_DIM)


def _gated_conv(b, c, u, w):
    y = lax.conv_general_dilated(c * u, w.astype(u.dtype)[:, None, :], window_strides=(1,),
                                 padding=((CONV_K // 2, CONV_K // 2),),
                                 dimension_numbers=('NWC', 'WIO', 'NWC'),
                                 feature_group_count=CONV_CH)
    return b * y


def _merge(h, o_a, o_b, o_c, w_pa, w_pb, w_pc, w_gate, b_gate, w_o):
    g_a, g_b, g_c = jnp.split(jax.nn.sigmoid(h @ w_gate + b_gate), N_BRANCH, axis=-1)
    return (g_a * (o_a @ w_pa) + g_b * (o_b @ w_pb) + g_c * (o_c @ w_pc)) @ w_o


def setup_inputs(seed: int = 0) -> dict:
    key = jax.random.key(seed)
    ks = jax.random.split(key, 20)

    def nrm(k, shape, scale):
        return jax.random.normal(k, shape, jnp.float32) * scale

    return {
        'x': nrm(ks[0], (BATCH, SEQ, D_MODEL), 1.0),
        'c': nrm(ks[1], (BATCH, D_MODEL), 1.0),
        'ctx': nrm(ks[2], (BATCH, CTX_LEN, D_MODEL), 1.0),
        'c_ctx': nrm(ks[3], (D_MODEL,), 1.0),
        'w_ada': nrm(ks[4], (DEPTH, D_MODEL, N_MOD * D_MODEL), 0.5 * D_MODEL ** -0.5),
        'b_ada': nrm(ks[5], (DEPTH, N_MOD * D_MODEL), 0.02),
        'norm_g': 1.0 + nrm(ks[6], (DEPTH, 3, D_MODEL), 0.05),
        'ffn_w_gate': nrm(ks[7], (DEPTH, 2, D_MODEL, D_FF), D_MODEL ** -0.5),
        'ffn_w_up': nrm(ks[8], (DEPTH, 2, D_MODEL, D_FF), D_MODEL ** -0.5),
        'ffn_w_down': nrm(ks[9], (DEPTH, 2, D_FF, D_MODEL), D_FF ** -0.5),
        'w_in': nrm(ks[10], (DEPTH, D_MODEL, IN_COLS), D_MODEL ** -0.5),
        'qk_g': 1.0 + nrm(ks[11], (DEPTH, 4, HEAD_DIM), 0.05),
        'sink_a': nrm(ks[12], (DEPTH, A_Q_HEADS), 0.5),
        'conv_w': nrm(ks[13], (DEPTH, CONV_K, CONV_CH), CONV_K ** -0.5),
        'w_pa': nrm(ks[14], (DEPTH, A_Q, D_MODEL), A_Q ** -0.5),
        'w_pb': nrm(ks[15], (DEPTH, CONV_CH, D_MODEL), CONV_CH ** -0.5),
        'w_pc': nrm(ks[16], (DEPTH, C_Q, D_MODEL), C_Q ** -0.5),
        'w_gate': nrm(ks[17], (DEPTH, D_MODEL, N_BRANCH * D_MODEL), D_MODEL ** -0.5),
        'b_gate': nrm(ks[18], (DEPTH, N_BRANCH * D_MODEL), 0.02),
        'w_o': nrm(ks[19], (DEPTH, D_MODEL, D_MODEL), D_MODEL ** -0.5),
    }


def reference(x, c, ctx, c_ctx, w_ada, b_ada, norm_g, ffn_w_gate, ffn_w_up, ffn_w_down, w_in, qk_g,
              sink_a, conv_w, w_pa, w_pb, w_pc, w_gate, b_gate, w_o):
    S = x.shape[1]
    ROWS = S // GRID_W
    tabs = _rope_tables(ROWS, x.dtype)
    xc = ctx
    for i in range(DEPTH):
        last = i == DEPTH - 1
        m = _ada(c, w_ada[i], b_ada[i])
        mc = _ada(c_ctx, w_ada[i], b_ada[i])

        x = x + 0.5 * m[2] * _swiglu(_modulate(x, norm_g[i, 0], m[0], m[1]),
                                     ffn_w_gate[i, 0], ffn_w_up[i, 0], ffn_w_down[i, 0])
        xc = xc + 0.5 * mc[2] * _swiglu(_modulate(xc, norm_g[i, 0], mc[0], mc[1]),
                                        ffn_w_gate[i, 0], ffn_w_up[i, 0], ffn_w_down[i, 0])

        h = _modulate(x, norm_g[i, 1], m[3], m[4])
        hc = _modulate(xc, norm_g[i, 1], mc[3], mc[4])
        if last:
            zc = _split(hc @ w_in[i][:, :KV_COLS], KV_SIZES)
        else:
            zc = _split(hc @ w_in[i], IN_SIZES)
        k_a_c = _qk(zc[0], A_KV_HEADS, qk_g[i, 1], None)
        v_a_c = _heads(zc[1], A_KV_HEADS)
        k_c_c = _qk(zc[2], C_KV_HEADS, qk_g[i, 3], None)
        v_c_c = _heads(zc[3], C_KV_HEADS)

        z = _split(h @ w_in[i], IN_SIZES)
        k_a = _qk(z[0], A_KV_HEADS, qk_g[i, 1], tabs)
        v_a = _heads(z[1], A_KV_HEADS)
        k_c = _qk(z[2], C_KV_HEADS, qk_g[i, 3], tabs)
        v_c = _heads(z[3], C_KV_HEADS)
        q_a = _qk(z[4], A_Q_HEADS, qk_g[i, 0], tabs)
        q_c = _qk(z[5], C_Q_HEADS, qk_g[i, 2], tabs)
        o_a = _window_attn(q_a, k_a, v_a, k_a_c, v_a_c, sink_a[i])
        o_c = _global_attn(q_c, k_c, v_c, k_c_c, v_c_c)
        o_b = _gated_conv(z[6], z[7], z[8], conv_w[i])
        x = x + m[5] * _merge(h, o_a, o_b, o_c, w_pa[i], w_pb[i], w_pc[i], w_gate[i], b_gate[i], w_o[i])

        if not last:
            q_a_c = _qk(zc[4], A_Q_HEADS, qk_g[i, 0], None)
            q_c_c = _qk(zc[5], C_Q_HEADS, qk_g[i, 2], None)
            o_a_c = _ctx_attn(q_a_c, k_a_c, v_a_c, sink_a[i])
            o_c_c = _ctx_attn(q_c_c, k_c_c, v_c_c, None)
            o_b_c = _gated_conv(zc[6], zc[7], zc[8], conv_w[i])
            xc = xc + mc[5] * _merge(hc, o_a_c, o_b_c, o_c_c, w_pa[i], w_pb[i], w_pc[i],
                                     w_gate[i], b_gate[i], w_o[i])

        x = x + 0.5 * m[8] * _swiglu(_modulate(x, norm_g[i, 2], m[6], m[7]),
                                     ffn_w_gate[i, 1], ffn_w_up[i, 1], ffn_w_down[i, 1])
        if not last:
            xc = xc + 0.5 * mc[8] * _swiglu(_modulate(xc, norm_g[i, 2], mc[6], mc[7]),
                                            ffn_w_gate[i, 1], ffn_w_up[i, 1], ffn_w_down[i, 1])
    return x
```

```python
import numpy as np
import ml_dtypes
from contextlib import ExitStack
import concourse.bass as bass
import concourse.mybir as mybir
from concourse.bass_utils import run_bass_kernel_spmd

F32 = mybir.dt.float32
BF16 = mybir.dt.bfloat16
AF = mybir.ActivationFunctionType
ALU = mybir.AluOpType

D = 1024
DFF = 2816
NJ = DFF // 128
DEPTH = 4
SEQ = 16384
NCORE = 8
TOK = 4096
CTX = 256
TT = 256
NLT = TOK // TT
NTILE = NLT + 1
TCOLS = TOK + CTX
EPS = 1e-6
SCALE = 0.125
NEG = -1e30
KH = CTX + 128 + TOK + 128
KALL = CTX + SEQ
NKC = KALL // 128


class Res:
    __slots__ = ("name", "w", "r", "dsem")

    def __init__(self, name):
        self.name = name
        self.w = {}
        self.r = {}
        self.dsem = None


class _Eng:
    def __init__(self, name, e, sem):
        self.name, self.e, self.sem, self.cnt, self.seen = name, e, sem, 0, {}


class _DSem:
    def __init__(self, key, sem):
        self.key, self.sem, self.cnt = key, sem, 0


class KB:
    def __init__(self, nc, es):
        self.nc, self.es = nc, es
        self.engs = {}
        for name, e in (("pe", nc.tensor), ("act", nc.scalar), ("dve", nc.vector),
                        ("pool", nc.gpsimd), ("sp", nc.sync)):
            self.engs[name] = _Eng(name, e, es.enter_context(nc.semaphore("s_" + name)))
        self.dsems = []
        self.free_dsems = []
        self.phase_dsems = []
        self.nres = 0

    def res(self, name="r"):
        self.nres += 1
        return Res(f"{name}{self.nres}")

    def _dsem(self, r):
        if r.dsem is None:
            if self.free_dsems:
                r.dsem = self.free_dsems.pop()
            else:
                r.dsem = _DSem("d_" + r.name, self.es.enter_context(self.nc.semaphore("d_" + r.name)))
                self.dsems.append(r.dsem)
            self.phase_dsems.append(r.dsem)
        return r.dsem

    def phase_end(self):
        self.barrier()
        self.free_dsems.extend(self.phase_dsems)
        self.phase_dsems = []

    def _deps(self, E, reads, writes):
        toks = {}
        for r in reads:
            for k, t in r.w.items():
                if k not in toks or toks[k][1] < t[1]:
                    toks[k] = t
        for w in writes:
            for dct in (w.w, w.r):
                for k, t in dct.items():
                    if k not in toks or toks[k][1] < t[1]:
                        toks[k] = t
        for k, (sem, val) in toks.items():
            if k == E.name or E.seen.get(k, 0) >= val:
                continue
            E.e.wait_ge(sem, val)
            E.seen[k] = val

    def op(self, eng, fn, reads=(), writes=()):
        E = self.engs[eng]
        self._deps(E, reads, writes)
        ins = fn(E.e)
        E.cnt += 1
        ins.then_inc(E.sem, 1)
        tok = (E.sem, E.cnt)
        for r in reads:
            r.r[E.name] = tok
        for w in writes:
            w.w[E.name] = tok
        return ins

    def dma(self, q, out, in_, reads=(), writes=(), own=None):
        E = self.engs[q]
        self._deps(E, reads, writes)
        ds = self._dsem(own)
        ins = E.e.dma_start(out=out, in_=in_)
        ds.cnt += 16
        ins.then_inc(ds.sem, 16)
        tok = (ds.sem, ds.cnt)
        for r in reads:
            r.r[ds.key] = tok
        for w in writes:
            w.w[ds.key] = tok

    def barrier(self):
        toks = {E.name: (E.sem, E.cnt) for E in self.engs.values() if E.cnt > 0}
        for ds in self.dsems:
            if ds.cnt > 0:
                toks[ds.key] = (ds.sem, ds.cnt)
        for E in self.engs.values():
            for k, (sem, val) in toks.items():
                if k == E.name or E.seen.get(k, 0) >= val:
                    continue
                E.e.wait_ge(sem, val)
                E.seen[k] = val


class T:
    def __init__(self, kb, es, name, shape, dtype, psum=False):
        nc = kb.nc
        kb.nres += 1
        self.t = es.enter_context((nc.psum_tensor if psum else nc.sbuf_tensor)(f"{name}_{kb.nres}", shape, dtype))
        self.r = kb.res(name)

    def __getitem__(self, idx):
        return self.t[idx]


class Prog:
    def __init__(self, phases, fused=False, nlayers=1):
        self.phases = phases
        self.fused = fused
        self.nl = nlayers
        self.nc = bass.Bass("TRN2", target_bir_lowering=False)
        self.in_names = []
        self.out_names = []

    def din(self, name, shape, dt=F32):
        self.in_names.append(name)
        return self.nc.dram_tensor(name, list(shape), dt, kind="ExternalInput").ap()

    def dout(self, name, shape, dt=F32):
        self.out_names.append(name)
        return self.nc.dram_tensor(name, list(shape), dt, kind="ExternalOutput").ap()

    def dint(self, name, shape, dt=F32):
        return self.nc.dram_tensor(name, list(shape), dt, kind="Internal").ap()

    def build(self):
        nc = self.nc
        L = self.nl
        P = self.phases
        with ExitStack() as es:
            kb = self.kb = KB(nc, es)
            d = self.d = {}
            d["cvec"] = self.din("cvec", [128, 8, 2])
            d["w_ada"] = self.din("w_ada", [L, D, 9 * D])
            d["b_ada"] = self.din("b_ada", [L, 128, 72])
            d["norm_g"] = self.din("norm_g", [L, 128, 3, 8])
            d["ident"] = self.din("ident", [128, 128])
            d["permm"] = self.din("permm", [128, 128])
            d["ropeC"] = self.din("ropeC", [128, TOK])
            d["ropeS"] = self.din("ropeS", [128, TOK])
            d["qk_g"] = self.din("qk_g", [L, 128, 4])
            d["w_in"] = self.din("w_in", [L, D, 3072])
            self.ffn_which = [w for w, p in ((0, "A"), (1, "C")) if p in P]
            if self.ffn_which:
                nw = len(self.ffn_which)
                d["wg"] = self.din("ffn_w_gate", [L, nw, D, DFF])
                d["wu"] = self.din("ffn_w_up", [L, nw, D, DFF])
                d["wd"] = self.din("ffn_w_down", [L, nw, DFF, D])
            if "B1" in P or "B2" in P:
                d["sink"] = self.din("sink_a", [L, 1, 8])
                d["conv_w"] = self.din("conv_w", [L, 128, 4, 3])
                d["w_pa"] = self.din("w_pa", [L, 512, D])
                d["w_pb"] = self.din("w_pb", [L, 512, D])
                d["w_pc"] = self.din("w_pc", [L, 512, D])
                d["w_gate"] = self.din("w_gate", [L, D, 3 * D])
                d["b_gate"] = self.din("b_gate", [L, 128, 24])
                d["w_o"] = self.din("w_o", [L, D, D])
                d["wmask"] = self.din("wmask", [128, 4, 512])
                d["emask"] = self.din("emask", [128, 2])
            d["xin"] = self.din("xin", [D, TCOLS])
            d["xw"] = self.dout("xw", [D, TCOLS])
            if self.fused:
                d["sel"] = self.din("sel", [128, 8])
                d["xedge_dummy"] = None
                for par in range(2):
                    d[f"kaT{par}"] = self.dint(f"kaT{par}", [128, TCOLS], BF16)
                    d[f"va{par}"] = self.dint(f"va{par}", [TCOLS, 128], BF16)
                    d[f"kae{par}"] = self.dint(f"kae{par}", [128, 256], BF16)
                    d[f"vae{par}"] = self.dint(f"vae{par}", [256, 128], BF16)
                    d[f"kcl{par}"] = self.dint(f"kcl{par}", [128, TOK], BF16)
                    d[f"vcl{par}"] = self.dint(f"vcl{par}", [TOK, 128], BF16)
                    d[f"kcc{par}"] = self.dint(f"kcc{par}", [128, CTX], BF16)
                    d[f"vcc{par}"] = self.dint(f"vcc{par}", [CTX, 128], BF16)
                    d[f"xe{par}"] = self.dint(f"xe{par}", [128, 16])
                    d[f"gkae{par}"] = self.dint(f"gkae{par}", [4 * 128, 256], BF16)
                    d[f"gvae{par}"] = self.dint(f"gvae{par}", [4 * 256, 128], BF16)
                    d[f"gkc{par}"] = self.dint(f"gkc{par}", [4 * 128, TOK], BF16)
                    d[f"gvc{par}"] = self.dint(f"gvc{par}", [4 * TOK, 128], BF16)
                    d[f"gxe{par}"] = self.dint(f"gxe{par}", [4 * 128, 16])
                    d[f"syn{par}"] = self.dint(f"syn{par}", [128, 16])
                    d[f"gsyn{par}"] = self.dint(f"gsyn{par}", [4 * 128, 16])
                for nm, shp in (("wg", [L, 2, D, DFF]), ("wu", [L, 2, D, DFF]), ("wd", [L, 2, DFF, D]), ("w_in", [L, D, 3072]),
                                ("w_gate", [L, D, 3 * D]), ("w_pa", [L, 512, D]), ("w_pb", [L, 512, D]), ("w_pc", [L, 512, D]),
                                ("w_o", [L, D, D])):
                    d["S_" + nm] = self.dint("S_" + nm, shp, BF16)
                d["QC"] = self.dint("QC", [512, TCOLS], BF16)
                d["ACC"] = self.dint("ACC", [D, TCOLS])
                d["GC"] = self.dint("GC", [D, TCOLS])

            c = self.c = {}
            c["ones"] = T(kb, es, "ones", [128, 128], BF16)
            c["bones"] = T(kb, es, "bones", [128, 128], BF16)
            c["onesf"] = T(kb, es, "onesf", [128, 128], F32)
            c["ident"] = T(kb, es, "identb", [128, 128], BF16)
            c["permm"] = T(kb, es, "permb", [128, 128], BF16)
            c["qkg"] = T(kb, es, "qkg", [128, L, 4], F32)
            c["epsc"] = T(kb, es, "epsc", [128, 1], F32)
            c["modA"] = T(kb, es, "modA", [128, L * 3 * 8 * 2], F32)
            c["modB"] = T(kb, es, "modB", [128, L * 3 * 8 * 2], F32)
            c["modG"] = T(kb, es, "modG", [128, L * 3 * 8 * 2], F32)
            kb.op("dve", lambda e: e.memset(c["ones"][:], 1.0), writes=[c["ones"].r])
            kb.op("dve", lambda e: e.memset(c["bones"][:], 0.0), writes=[c["bones"].r])
            kb.op("dve", lambda e: e.memset(c["bones"][0:64, 0:64], 1.0), writes=[c["bones"].r])
            kb.op("dve", lambda e: e.memset(c["bones"][64:128, 64:128], 1.0), writes=[c["bones"].r])
            kb.op("dve", lambda e: e.memset(c["onesf"][:], 1.0), writes=[c["onesf"].r])
            kb.op("dve", lambda e: e.memset(c["epsc"][:], EPS), writes=[c["epsc"].r])
            kb.dma("pool", c["ident"][:], d["ident"][:, :], writes=[c["ident"].r], own=c["ident"].r)
            kb.dma("pool", c["permm"][:], d["permm"][:, :], writes=[c["permm"].r], own=c["permm"].r)
            kb.dma("sp", c["qkg"][:], d["qk_g"].rearrange("l p f -> p l f"), writes=[c["qkg"].r], own=c["qkg"].r)

            kb.barrier()
            kb.phase_dsems = []
            self.precast((0, "A"))
            for l in range(L):
                self.phase_M(l)
            xsrc = d["xin"]
            for l in range(L):
                if "A" in P:
                    self.phase_ffn(l, 0, xsrc, d["xw"], with_kv=True)
                    xsrc = d["xw"]
                if "X" in P:
                    self.phase_X(l)
                if "B1" in P:
                    self.phase_B1(l, xsrc)
                if "B2" in P:
                    self.phase_B2(l, xsrc, d["xw"])
                    xsrc = d["xw"]
                if "C" in P:
                    self.phase_ffn(l, 1, xsrc, d["xw"], with_kv=False)
                    xsrc = d["xw"]
            kb.phase_end()
        return self

    def precast(self, key):
        if key is None:
            return
        l, p = key
        kb, d = self.kb, self.d
        if p == "A":
            items = [("wg", (l, 0)), ("wu", (l, 0)), ("wd", (l, 0)), ("w_in", (l,))]
        elif p == "B1":
            items = [("w_gate", (l,)), ("w_pa", (l,)), ("w_pb", (l,))]
        elif p == "B2":
            items = [("w_pc", (l,)), ("w_o", (l,))]
        else:
            items = [("wg", (l, 1)), ("wu", (l, 1)), ("wd", (l, 1))]
        for nm, idx in items:
            src, dst = d[nm], d["S_" + nm]
            for i in idx:
                src, dst = src[i], dst[i]
            r = kb.res("cast")
            rows = src.shape[0]
            h = rows // 2
            for a, b in ((0, h), (h, rows)):
                kb.dma("pool", dst[a:b, :], src[a:b, :], own=r)

    def next_key(self, l, p):
        seq = [(ll, pp) for ll in range(self.nl) for pp in ("A", "B1", "B2", "C")]
        i = seq.index((l, p))
        return seq[i + 1] if i + 1 < len(seq) else None

    def mod(self, which, l, n, ch, col):
        i = ((l * 3 + n) * 8 + ch) * 2 + col
        return self.c[which][:, i:i + 1]

    def phase_M(self, l):
        nc, kb, d, c = self.nc, self.kb, self.d, self.c
        with ExitStack() as ph:
            cv = T(kb, ph, "cv", [128, 8, 2], F32)
            sc = T(kb, ph, "sc", [128, 8, 2], F32)
            sg = T(kb, ph, "sgm", [128, 8, 2], F32)
            ba = T(kb, ph, "ba", [128, 72], F32)
            ng = T(kb, ph, "ng", [128, 3, 8], F32)
            mt = T(kb, ph, "mt", [128, 72, 2], F32)
            st = [T(kb, ph, f"wst{i}", [128, 8, 1024], F32) for i in range(2)]
            pm = T(kb, ph, "pm", [128, 72, 2], F32, psum=True)
            kb.dma("sp", cv[:], d["cvec"][:, :, :], writes=[cv.r], own=cv.r)
            kb.dma("sp", ba[:], d["b_ada"][l], writes=[ba.r], own=ba.r)
            kb.dma("sp", ng[:], d["norm_g"][l], writes=[ng.r], own=ng.r)
            kb.op("act", lambda e: e.activation(out=sg[:], in_=cv[:], func=AF.Sigmoid), reads=[cv.r], writes=[sg.r])
            kb.op("dve", lambda e: e.tensor_tensor(sc[:], cv[:], sg[:], op=ALU.mult), reads=[cv.r, sg.r], writes=[sc.r])
            wsrc = d["w_ada"][l].rearrange("(kc p) n -> p kc n", p=128)
            for cb in range(9):
                s = st[cb % 2]
                q = "sp" if cb % 2 == 0 else "act"
                kb.dma(q, s[:], wsrc[:, :, cb * 1024:(cb + 1) * 1024], writes=[s.r], own=s.r)
                for o in range(8):
                    oc = cb * 8 + o
                    for kc in range(8):
                        kb.op("pe", lambda e, kc=kc, o=o, oc=oc: e.matmul(
                            pm[:, oc, :], s[:, kc, o * 128:(o + 1) * 128], sc[:, kc, :],
                            start=(kc == 0), stop=(kc == 7)), reads=[s.r, sc.r], writes=[pm.r])
            for col in range(2):
                kb.op("dve", lambda e, col=col: e.tensor_tensor(mt[:, :, col], pm[:, :, col], ba[:], op=ALU.add),
                      reads=[pm.r, ba.r], writes=[mt.r])
            for n in range(3):
                for col in range(2):
                    base = (l * 3 + n) * 16
                    A = c["modA"][:, base + col: base + 16: 2]
                    B = c["modB"][:, base + col: base + 16: 2]
                    G = c["modG"][:, base + col: base + 16: 2]
                    sh = mt[:, (3 * n) * 8:(3 * n + 1) * 8, col]
                    scl = mt[:, (3 * n + 1) * 8:(3 * n + 2) * 8, col]
                    gt = mt[:, (3 * n + 2) * 8:(3 * n + 3) * 8, col]
                    kb.op("dve", lambda e, A=A, scl=scl, n=n: e.scalar_tensor_tensor(
                        out=A, in0=scl, scalar=1.0, in1=ng[:, n, :], op0=ALU.add, op1=ALU.mult),
                        reads=[mt.r, ng.r], writes=[c["modA"].r])
                    kb.op("dve", lambda e, B=B, sh=sh: e.tensor_copy(B, sh), reads=[mt.r], writes=[c["modB"].r])
                    kb.op("dve", lambda e, G=G, gt=gt, n=n: e.tensor_scalar(
                        G, gt, 1.0 if n == 1 else 0.5, None, op0=ALU.mult), reads=[mt.r], writes=[c["modG"].r])
            kb.phase_end()

    def emit_norm(self, ph_t, xt, xr, width, l, n, col, hout, hr, xoff=0):
        kb, c = self.kb, self.c
        sq, pss, rs, tmp = ph_t["sq"], ph_t["pss"], ph_t["rs"], ph_t["tmp"]
        W = width
        kb.op("act", lambda e: e.activation(out=sq[:, :, 0:W], in_=xt[:, :, xoff:xoff + W], func=AF.Square),
              reads=[xr], writes=[sq.r])
        for ch in range(8):
            kb.op("pe", lambda e, ch=ch: e.matmul(pss[:, 0:W], c["ones"][:], sq[:, ch, 0:W],
                                                 start=(ch == 0), stop=(ch == 7)),
                  reads=[sq.r, c["ones"].r], writes=[pss.r])
        kb.op("act", lambda e: e.activation(out=rs[:, 0:W], in_=pss[:, 0:W], func=AF.Sqrt,
                                            bias=c["epsc"][:, 0:1], scale=1.0 / D),
              reads=[pss.r, c["epsc"].r], writes=[rs.r])
        kb.op("dve", lambda e: e.reciprocal(rs[:, 0:W], rs[:, 0:W]), reads=[rs.r], writes=[rs.r])
        for ch in range(8):
            tb = tmp[ch % 2]
            kb.op("dve", lambda e, ch=ch, tb=tb: e.scalar_tensor_tensor(
                out=tb[:, 0:W], in0=xt[:, ch, xoff:xoff + W], scalar=self.mod("modA", l, n, ch, col),
                in1=rs[:, 0:W], op0=ALU.mult, op1=ALU.mult),
                reads=[xr, rs.r, c["modA"].r], writes=[tb.r])
            kb.op("act", lambda e, ch=ch, tb=tb: e.activation(
                out=hout[:, ch, 0:W], in_=tb[:, 0:W], func=AF.Identity,
                bias=self.mod("modB", l, n, ch, col), scale=1.0),
                reads=[tb.r, c["modB"].r], writes=[hr])

    def norm_tiles(self, ph, wmax):
        kb = self.kb
        return {
            "sq": T(kb, ph, "sq", [128, 8, wmax], BF16),
            "pss": T(kb, ph, "pss", [128, 512], F32, psum=True),
            "rs": T(kb, ph, "rs", [128, wmax], F32),
            "tmp": [T(kb, ph, f"ntmp{i}", [128, wmax], F32) for i in range(2)],
        }

    def qk_tiles(self, ph):
        kb = self.kb
        return {
            "sqk": T(kb, ph, "sqk", [128, TT], BF16),
            "psk": T(kb, ph, "psk", [128, 512], F32, psum=True),
            "rk": T(kb, ph, "rk", [128, TT], F32),
            "yf": T(kb, ph, "yf", [128, TT], F32),
            "yb": T(kb, ph, "yb", [128, TT], BF16),
            "t1": T(kb, ph, "t1", [128, TT], F32),
            "t2": T(kb, ph, "t2", [128, TT], F32),
            "ropeC": T(kb, ph, "rpC", [128, TT], F32),
            "ropeS": T(kb, ph, "rpS", [128, TT], F32),
        }

    def load_rope(self, qt, t):
        kb, d = self.kb, self.d
        kb.dma("sp", qt["ropeC"][:], d["ropeC"][:, t * TT:(t + 1) * TT], writes=[qt["ropeC"].r], own=qt["ropeC"].r)
        kb.dma("sp", qt["ropeS"][:], d["ropeS"][:, t * TT:(t + 1) * TT], writes=[qt["ropeS"].r], own=qt["ropeS"].r)

    def emit_qknorm(self, qt, ps, psr, l, gi, rope, out_ap, out_r):
        kb, c = self.kb, self.c
        g = c["qkg"][:, l, gi:gi + 1]
        kb.op("act", lambda e: e.activation(out=qt["sqk"][:], in_=ps, func=AF.Square), reads=[psr], writes=[qt["sqk"].r])
        kb.op("pe", lambda e: e.matmul(qt["psk"][:, 0:TT], c["bones"][:], qt["sqk"][:], start=True, stop=True),
              reads=[qt["sqk"].r, c["bones"].r], writes=[qt["psk"].r])
        kb.op("act", lambda e: e.activation(out=qt["rk"][:], in_=qt["psk"][:, 0:TT], func=AF.Sqrt,
                                            bias=c["epsc"][:, 0:1], scale=1.0 / 64),
              reads=[qt["psk"].r, c["epsc"].r], writes=[qt["rk"].r])
        kb.op("dve", lambda e: e.reciprocal(qt["rk"][:], qt["rk"][:]), reads=[qt["rk"].r], writes=[qt["rk"].r])
        if not rope:
            kb.op("dve", lambda e: e.scalar_tensor_tensor(out=out_ap, in0=ps, scalar=g, in1=qt["rk"][:],
                                                          op0=ALU.mult, op1=ALU.mult),
                  reads=[psr, qt["rk"].r, c["qkg"].r], writes=[out_r])
            return
        kb.op("dve", lambda e: e.scalar_tensor_tensor(out=qt["yf"][:], in0=ps, scalar=g, in1=qt["rk"][:],
                                                      op0=ALU.mult, op1=ALU.mult),
              reads=[psr, qt["rk"].r, c["qkg"].r], writes=[qt["yf"].r])
        kb.op("act", lambda e: e.copy(qt["yb"][:], qt["yf"][:]), reads=[qt["yf"].r], writes=[qt["yb"].r])
        kb.op("pe", lambda e: e.matmul(qt["psk"][:, TT:2 * TT], c["permm"][:], qt["yb"][:], start=True, stop=True),
              reads=[qt["yb"].r, c["permm"].r], writes=[qt["psk"].r])
        kb.op("pool", lambda e: e.tensor_tensor(qt["t1"][:], qt["yf"][:], qt["ropeC"][:], op=ALU.mult),
              reads=[qt["yf"].r, qt["ropeC"].r], writes=[qt["t1"].r])
        kb.op("dve", lambda e: e.tensor_tensor(qt["t2"][:], qt["psk"][:, TT:2 * TT], qt["ropeS"][:], op=ALU.mult),
              reads=[qt["psk"].r, qt["ropeS"].r], writes=[qt["t2"].r])
        kb.op("dve", lambda e: e.tensor_tensor(out_ap, qt["t1"][:], qt["t2"][:], op=ALU.add),
              reads=[qt["t1"].r, qt["t2"].r], writes=[out_r])

    def load_v(self, q, vt, ch0, nch, src_rows):
        for kv in range(2):
            self.kb.dma(q, vt[:, ch0:ch0 + nch, kv, 0:64],
                        src_rows[:, kv * 64:(kv + 1) * 64].rearrange("(ch p) dd -> p ch dd", p=128),
                        writes=[vt.r], own=vt.r)

    def load_w(self, dst, src_ap, nchunk, per=1):
        kb = self.kb
        for n_, k0 in enumerate(range(0, nchunk, per)):
            k1 = min(nchunk, k0 + per)
            kb.dma("sp" if n_ % 2 == 0 else "act", dst[:, k0:k1, :], src_ap[:, k0:k1, :], writes=[dst.r], own=dst.r)

    def phase_ffn(self, l, which, xsrc, xdst, with_kv):
        nc, kb, d, c = self.nc, self.kb, self.d, self.c
        n = 0 if which == 0 else 2
        xs = xsrc.rearrange("(c p) t -> p c t", p=128)
        xd = xdst.rearrange("(c p) t -> p c t", p=128)
        with ExitStack() as ph:
            self.precast(self.next_key(l, "A" if which == 0 else "C"))
            wg = T(kb, ph, "wg", [128, 8, DFF], BF16)
            wu = T(kb, ph, "wu", [128, 8, DFF], BF16)
            wd = T(kb, ph, "wd", [128, NJ, D], BF16)
            wi_ = self.ffn_which.index(which)
            self.load_w(wg, d["S_wg"][l, which].rearrange("(c p) n -> p c n", p=128), 8, 2)
            self.load_w(wu, d["S_wu"][l, which].rearrange("(c p) n -> p c n", p=128), 8, 2)
            self.load_w(wd, d["S_wd"][l, which].rearrange("(c p) n -> p c n", p=128), NJ, 6)
            xb = [T(kb, ph, f"xb{i}", [128, 8, TT], F32) for i in range(2)]
            hb = [T(kb, ph, f"hb{i}", [128, 8, TT], BF16) for i in range(3 if with_kv else 2)]
            act = T(kb, ph, "actb", [128, NJ, TT], BF16)
            sgb = [T(kb, ph, f"sgb{i}", [128, TT], F32) for i in range(2)]
            nt = self.norm_tiles(ph, TT)
            pgu = [T(kb, ph, f"pgu{i}", [128, 2, TT], F32, psum=True) for i in range(2)]
            py = [T(kb, ph, f"py{i}", [128, 512], F32, psum=True) for i in range(2)]
            if with_kv:
                wkv = T(kb, ph, "wkv", [128, 8, 512], BF16)
                self.load_w(wkv, d["S_w_in"][l].rearrange("(c p) n -> p c n", p=128)[:, :, 0:512], 8, 8)
                qt = self.qk_tiles(ph)
                kob = [T(kb, ph, f"kob{i}", [128, TT], BF16) for i in range(2)]
                vob = T(kb, ph, "vob", [128, 2, 2, 128], BF16)
                pkv = T(kb, ph, "pkv", [128, 512], F32, psum=True)
                pv = T(kb, ph, "pvv", [128, 512], F32, psum=True)
                xeb = T(kb, ph, "xeb", [128, 8, 2], F32)

            def load_x(t):
                b = xb[t % 2]
                kb.dma("sp", b[:], xs[:, :, t * TT:(t + 1) * TT], writes=[b.r], own=b.r)

            load_x(0)
            self.emit_norm(nt, xb[0], xb[0].r, TT, l, n, 0, hb[0], hb[0].r)
            for t in range(NTILE):
                col = 1 if t == NLT else 0
                if t + 1 < NTILE:
                    load_x(t + 1)
                x = xb[t % 2]
                h = hb[t % 2]
                for j in range(NJ):
                    p = pgu[j % 2]
                    for (wi, W_) in ((0, wg), (1, wu)):
                        for ch in range(8):
                            kb.op("pe", lambda e, ch=ch, W_=W_, wi=wi, p=p, j=j: e.matmul(
                                p[:, wi, :], W_[:, ch, j * 128:(j + 1) * 128], h[:, ch, :],
                                start=(ch == 0), stop=(ch == 7)), reads=[W_.r, h.r], writes=[p.r])
                    sg_ = sgb[j % 2]
                    kb.op("act", lambda e, p=p, sg_=sg_: e.activation(out=sg_[:], in_=p[:, 0, :], func=AF.Silu),
                          reads=[p.r], writes=[sg_.r])
                    kb.op("dve", lambda e, p=p, sg_=sg_, j=j: e.tensor_tensor(act[:, j, :], sg_[:], p[:, 1, :], op=ALU.mult),
                          reads=[p.r, sg_.r], writes=[act.r])
                for oc in range(8):
                    if oc == 2 and t + 1 < NTILE:
                        xn, hn = xb[(t + 1) % 2], hb[(t + 1) % 2]
                        self.emit_norm(nt, xn, xn.r, TT, l, n, 1 if t + 1 == NLT else 0, hn, hn.r)
                    p = py[oc % 2]
                    for j in range(NJ):
                        kb.op("pe", lambda e, p=p, j=j, oc=oc: e.matmul(
                            p[:, 0:TT], wd[:, j, oc * 128:(oc + 1) * 128], act[:, j, :],
                            start=(j == 0), stop=(j == NJ - 1)), reads=[wd.r, act.r], writes=[p.r])
                    kb.op("dve", lambda e, p=p, oc=oc: e.scalar_tensor_tensor(
                        out=x[:, oc, :], in0=p[:, 0:TT], scalar=self.mod("modG", l, n, oc, col), in1=x[:, oc, :],
                        op0=ALU.mult, op1=ALU.add), reads=[p.r, x.r, c["modG"].r], writes=[x.r])
                kb.dma("sp", xd[:, :, t * TT:(t + 1) * TT], x[:], reads=[x.r], own=x.r)
                if not with_kv:
                    continue
                if t == 0:
                    kb.op("pool", lambda e: e.tensor_copy(xeb[:, :, 0], x[:, :, 0]), reads=[x.r], writes=[xeb.r])
                if t == NLT - 1:
                    kb.op("pool", lambda e: e.tensor_copy(xeb[:, :, 1], x[:, :, TT - 1]), reads=[x.r], writes=[xeb.r])
                    kb.dma("sp", d[f"xe{l % 2}"][:, :], xeb[:].rearrange("p c j -> p (c j)"), reads=[xeb.r], own=xeb.r)
                h2 = hb[2]
                self.emit_norm(nt, x, x.r, TT, l, 1, col, h2, h2.r)
                if col == 0:
                    self.load_rope(qt, t)
                par = l % 2
                for ki, (c0, gi) in enumerate(((0, 1), (256, 3))):
                    for ch in range(8):
                        kb.op("pe", lambda e, ch=ch, c0=c0: e.matmul(
                            pkv[:, 0:TT], wkv[:, ch, c0:c0 + 128], h2[:, ch, :], start=(ch == 0), stop=(ch == 7)),
                            reads=[wkv.r, h2.r], writes=[pkv.r])
                    ko = kob[ki]
                    self.emit_qknorm(qt, pkv[:, 0:TT], pkv.r, l, gi, col == 0, ko[:], ko.r)
                    if ki == 0:
                        kb.dma("sp", d[f"kaT{par}"][:, t * TT:(t + 1) * TT], ko[:], reads=[ko.r], own=ko.r)
                        if t == 0:
                            kb.dma("sp", d[f"kae{par}"][:, 0:128], ko[:, 0:128], reads=[ko.r], own=ko.r)
                        if t == NLT - 1:
                            kb.dma("sp", d[f"kae{par}"][:, 128:256], ko[:, 128:256], reads=[ko.r], own=ko.r)
                    elif col == 0:
                        kb.dma("sp", d[f"kcl{par}"][:, t * TT:(t + 1) * TT], ko[:], reads=[ko.r], own=ko.r)
                    else:
                        kb.dma("sp", d[f"kcc{par}"][:, :], ko[:], reads=[ko.r], own=ko.r)
                for tb in range(2):
                    for vi, c0 in enumerate((128, 384)):
                        for ch in range(8):
                            kb.op("pe", lambda e, ch=ch, c0=c0, tb=tb, vi=vi: e.matmul(
                                pv[:, (tb * 2 + vi) * 128:(tb * 2 + vi + 1) * 128], h2[:, ch, tb * 128:(tb + 1) * 128],
                                wkv[:, ch, c0:c0 + 128], start=(ch == 0), stop=(ch == 7)),
                                reads=[wkv.r, h2.r], writes=[pv.r])
                kb.op("act", lambda e: e.copy(vob[:].rearrange("p a b c -> p (a b c)"), pv[:, :]), reads=[pv.r], writes=[vob.r])
                vdst = [d[f"va{par}"][t * TT:(t + 1) * TT, :],
                        d[f"vcl{par}"][t * TT:(t + 1) * TT, :] if col == 0 else d[f"vcc{par}"][:, :]]
                for vi, dst in enumerate(vdst):
                    kb.dma("sp", dst.rearrange("(tb p) f -> p tb f", p=128), vob[:, :, vi, :], reads=[vob.r], own=vob.r)
                if t == 0:
                    kb.dma("sp", d[f"vae{par}"][0:128, :], vob[:, 0, 0, :], reads=[vob.r], own=vob.r)
                if t == NLT - 1:
                    kb.dma("sp", d[f"vae{par}"][128:256, :], vob[:, 1, 0, :], reads=[vob.r], own=vob.r)
            kb.phase_end()

    def phase_X(self, l):
        nc, kb, d = self.nc, self.kb, self.d
        par = l % 2
        if not hasattr(self, "_xsems"):
            self._xsems = []
            for nm in ("xbig", "xsmall", "xsync"):
                ds = _DSem(nm, kb.es.enter_context(nc.semaphore(nm)))
                kb.dsems.append(ds)
                self._xsems.append(ds)
        xbig, xsmall, xsync = self._xsems
        kb.barrier()
        rg = [[0, 1, 2, 3], [4, 5, 6, 7]]

        def gather(a, b, ds):
            ins = nc.gpsimd.collective_compute("AllGather", ALU.bypass, replica_groups=rg,
                                               ins=[d[f"{a}{par}"][:, :]], outs=[d[f"{b}{par}"][:, :]])
            ds.cnt += 1
            ins.then_inc(ds.sem, 1)

        gather("kcl", "gkc", xbig)
        gather("vcl", "gvc", xbig)
        kb.barrier()
        for a, b in (("kae", "gkae"), ("vae", "gvae"), ("xe", "gxe")):
            gather(a, b, xsmall)
        kb.barrier()
        gather("syn", "gsyn", xsync)
        kb.barrier()

    def attn_tiles(self, ph, npt):
        kb = self.kb
        return {
            "pT": [T(kb, ph, f"pT{i}", [128, 512], BF16) for i in range(npt)],
            "dsb": T(kb, ph, "dsb", [128, 512], F32),
            "rbs": T(kb, ph, "rbs", [64, 512], F32),
            "pbb": T(kb, ph, "pbb", [128, 512], F32, psum=True),
            "cnt": 0,
        }

    def attn_group(self, at, pss, po, chunks, q_ap, q_r, out_ap, out_r, sinkrow=None, prev_epi=None):
        kb, c = self.kb, self.c
        n = len(chunks)
        LA = min(2, len(pss) - 1)
        bufs = []
        for i in range(n):
            bufs.append((pss[at["cnt"] % len(pss)], at["pT"][at["cnt"] % len(at["pT"])]))
            at["cnt"] += 1

        def emit_S(i):
            k_ap, k_r, v_ap, v_r, m_ap = chunks[i]
            ps = bufs[i][0]
            if m_ap is not None:
                kb.op("pe", lambda e: e.matmul(ps[:, :], c["ident"][:], m_ap, start=True, stop=False),
                      reads=[c["ident"].r, self._wm.r], writes=[ps.r])
            kb.op("pe", lambda e: e.matmul(ps[:, :].rearrange("p (j q) -> p j q", j=4), k_ap, q_ap,
                                           start=(m_ap is None), stop=True), reads=[k_r, q_r], writes=[ps.r])

        for i in range(min(LA, n)):
            emit_S(i)
        if prev_epi is not None:
            prev_epi()
        for i in range(n):
            ps, pT = bufs[i]
            kb.op("act", lambda e, ps=ps, pT=pT: e.activation(out=pT[:], in_=ps[:, :], func=AF.Exp, scale=SCALE),
                  reads=[ps.r], writes=[pT.r])
            if i + LA < n:
                emit_S(i + LA)
            v_ap, v_r = chunks[i][2], chunks[i][3]
            kb.op("pe", lambda e, pT=pT, v_ap=v_ap, i=i: e.matmul(po[0:65, :], v_ap, pT[:], start=(i == 0), stop=(i == n - 1)),
                  reads=[v_r, pT.r], writes=[po.r])

        def epilogue():
            dsb, rbs, pbb = at["dsb"], at["rbs"], at["pbb"]
            if sinkrow is not None:
                kb.op("dve", lambda e: e.tensor_tensor(dsb[64:65, :], po[64:65, :], sinkrow, op=ALU.add),
                      reads=[po.r, self._sinkrow.r], writes=[dsb.r])
            else:
                kb.op("dve", lambda e: e.tensor_copy(dsb[64:65, :], po[64:65, :]), reads=[po.r], writes=[dsb.r])
            kb.op("dve", lambda e: e.reciprocal(dsb[64:65, :], dsb[64:65, :]), reads=[dsb.r], writes=[dsb.r])
            kb.op("pe", lambda e: e.matmul(pbb[0:64, :], c["onesf"][64:65, 0:64], dsb[64:65, :], start=True, stop=True),
                  reads=[dsb.r, c["onesf"].r], writes=[pbb.r])
            kb.op("act", lambda e: e.copy(rbs[:, :], pbb[0:64, :]), reads=[pbb.r], writes=[rbs.r])
            kb.op("dve", lambda e: e.tensor_tensor(out_ap, po[0:64, :].rearrange("p (j q) -> p j q", j=4),
                                                   rbs[:, :].rearrange("p (j q) -> p j q", j=4), op=ALU.mult),
                  reads=[po.r, rbs.r], writes=[out_r])

        return epilogue

    def attn_pair(self, st, psp, pTp, po, kcall, vflat, nkc, qc, qb, ocT, prev_epi=None):
        kb, c = self.kb, self.c
        n = nkc
        bufs = []
        for i in range(n):
            bufs.append((psp[st["cnt"] % len(psp)], pTp[st["cnt"] % len(pTp)]))
            st["cnt"] += 1

        def emit_S(i):
            ps = bufs[i][0]
            for kv in range(2):
                ks = slice(kv * 64, (kv + 1) * 64)
                kb.op("pe", lambda e, kv=kv, ks=ks: e.matmul(
                    ps[:, kv, :].rearrange("p (j q) -> p j q", j=4), kcall[ks, i * 128:(i + 1) * 128],
                    qc[ks, :, qb * 128:(qb + 1) * 128], start=True, stop=True), reads=[kcall.r, qc.r], writes=[ps.r])

        emit_S(0)
        if prev_epi is not None:
            prev_epi()
        for i in range(n):
            ps, pT = bufs[i]
            kb.op("act", lambda e, ps=ps, pT=pT: e.activation(
                out=pT[:].rearrange("p a b -> p (a b)"), in_=ps[:].rearrange("p a b -> p (a b)"), func=AF.Exp, scale=SCALE),
                reads=[ps.r], writes=[pT.r])
            if i + 1 < n:
                emit_S(i + 1)
            for kv in range(2):
                o0 = i * 130 + kv * 65
                kb.op("pe", lambda e, pT=pT, kv=kv, o0=o0, i=i: e.matmul(
                    po[:, kv, :], vflat[:, o0:o0 + 128], pT[:, kv, :], start=(i == 0), stop=(i == n - 1)),
                    reads=[st["vr"], pT.r], writes=[po.r])

        def epilogue():
            dsb, rbs, pbb = st["dsb"], st["rbs"], st["pbb"]
            kb.op("dve", lambda e: e.tensor_copy(dsb[64:65, :, :], po[64:65, :, :]), reads=[po.r], writes=[dsb.r])
            kb.op("dve", lambda e: e.reciprocal(dsb[64:65, :, :], dsb[64:65, :, :]), reads=[dsb.r], writes=[dsb.r])
            for kv in range(2):
                kb.op("pe", lambda e, kv=kv: e.matmul(pbb[0:64, :], c["onesf"][64:65, 0:64], dsb[64:65, kv, :], start=True, stop=True),
                      reads=[dsb.r, c["onesf"].r], writes=[pbb.r])
                kb.op("act", lambda e: e.copy(rbs[:, :], pbb[0:64, :]), reads=[pbb.r], writes=[rbs.r])
                kb.op("dve", lambda e, kv=kv: e.tensor_tensor(
                    ocT[:, kv * 4:(kv + 1) * 4, qb * 128:(qb + 1) * 128], po[0:64, kv, :].rearrange("p (j q) -> p j q", j=4),
                    rbs[:, :].rearrange("p (j q) -> p j q", j=4), op=ALU.mult), reads=[po.r, rbs.r], writes=[ocT.r])

        return epilogue

    def phase_B1(self, l, xsrc):
        nc, kb, d, c = self.nc, self.kb, self.d, self.c
        xs = xsrc.rearrange("(c p) t -> p c t", p=128)
        W2 = TT + 2
        with ExitStack() as ph:
            self.precast(self.next_key(l, "B1"))
            win = T(kb, ph, "win", [128, 8, 2560], BF16)
            self.load_w(win, d["S_w_in"][l].rearrange("(c p) n -> p c n", p=128)[:, :, 512:3072], 8, 2)
            wgt = T(kb, ph, "wgt", [128, 8, 3072], BF16)
            self.load_w(wgt, d["S_w_gate"][l].rearrange("(c p) n -> p c n", p=128), 8, 2)
            wpa = T(kb, ph, "wpa", [64, 8, D], BF16)
            self.load_w(wpa, d["S_w_pa"][l].rearrange("(h dd) n -> dd h n", dd=64), 8, 4)
            wpb = T(kb, ph, "wpb", [128, 4, D], BF16)
            self.load_w(wpb, d["S_w_pb"][l].rearrange("(c p) n -> p c n", p=128), 4, 4)
            bgt = T(kb, ph, "bgt", [128, 24], F32)
            cw = T(kb, ph, "cw", [128, 4, 3], F32)
            wm = self._wm = T(kb, ph, "wm", [128, 4, 512], BF16)
            em = T(kb, ph, "em", [128, 2], F32)
            xedge = T(kb, ph, "xedge", [128, 8, 2], F32)
            sk = T(kb, ph, "sk", [128, 8], F32)
            ske = T(kb, ph, "ske", [128, 8], F32)
            sinkrow = self._sinkrow = T(kb, ph, "sinkrow", [128, 2, 512], F32)
            kactx = T(kb, ph, "kactx", [128, CTX], BF16)
            vactx = T(kb, ph, "vactx", [128, 2, 2, 65], BF16)
            kb.dma("sp", bgt[:], d["b_gate"][l], writes=[bgt.r], own=bgt.r)
            kb.dma("sp", cw[:], d["conv_w"][l], writes=[cw.r], own=cw.r)
            kb.dma("pool", wm[:], d["wmask"][:, :, :], writes=[wm.r], own=wm.r)
            kb.dma("sp", em[:], d["emask"][:, :], writes=[em.r], own=em.r)
            par = l % 2
            kaT, va_d = d[f"kaT{par}"], d[f"va{par}"]
            gkae, gvae, gxe = d[f"gkae{par}"], d[f"gvae{par}"], d[f"gxe{par}"]
            sel = T(kb, ph, "sel", [128, 8], F32)
            cand = T(kb, ph, "cand", [128, 4, 4, 128], BF16)
            cx = T(kb, ph, "cx", [128, 4, 16], F32)
            halo = T(kb, ph, "halo", [128, 4, 128], BF16)
            hacc = T(kb, ph, "hacc", [128, 128], F32)
            gk3 = gkae.rearrange("(i p) c -> p i c", p=128)
            gv3 = gvae.rearrange("(i t) f -> t i f", i=4)

            def prep_halos():
                kb.dma("sp", sel[:], d["sel"][:, :], writes=[sel.r], own=sel.r)
                kb.dma("sp", cand[:, 0, :, :], gk3[:, :, 128:256], writes=[cand.r], own=cand.r)
                kb.dma("sp", cand[:, 1, :, :], gk3[:, :, 0:128], writes=[cand.r], own=cand.r)
                kb.dma("sp", cand[:, 2, :, :], gv3[128:256, :, :], writes=[cand.r], own=cand.r)
                kb.dma("sp", cand[:, 3, :, :], gv3[0:128, :, :], writes=[cand.r], own=cand.r)
                kb.dma("sp", cx[:], gxe.rearrange("(i p) f -> p i f", p=128), writes=[cx.r], own=cx.r)
                for w in range(4):
                    so = 0 if w % 2 == 0 else 4
                    kb.op("dve", lambda e, w=w, so=so: e.tensor_scalar(hacc[:], cand[:, w, 0, :], sel[:, so:so + 1], None, op0=ALU.mult),
                          reads=[cand.r, sel.r], writes=[hacc.r])
                    for i in range(1, 4):
                        o_ap = halo[:, w, :] if i == 3 else hacc[:]
                        kb.op("dve", lambda e, w=w, so=so, i=i, o_ap=o_ap: e.scalar_tensor_tensor(
                            out=o_ap, in0=cand[:, w, i, :], scalar=sel[:, so + i:so + i + 1], in1=hacc[:], op0=ALU.mult, op1=ALU.add),
                            reads=[cand.r, sel.r, hacc.r], writes=[halo.r if i == 3 else hacc.r])
                for j_out, so, j_in in ((0, 0, 1), (1, 4, 0)):
                    kb.op("dve", lambda e, j_out=j_out, so=so, j_in=j_in: e.tensor_scalar(
                        xedge[:, :, j_out], cx[:, 0, j_in:16:2], sel[:, so:so + 1], None, op0=ALU.mult),
                        reads=[cx.r, sel.r], writes=[xedge.r])
                    for i in range(1, 4):
                        kb.op("dve", lambda e, j_out=j_out, so=so, j_in=j_in, i=i: e.scalar_tensor_tensor(
                            out=xedge[:, :, j_out], in0=cx[:, i, j_in:16:2], scalar=sel[:, so + i:so + i + 1], in1=xedge[:, :, j_out],
                            op0=ALU.mult, op1=ALU.add), reads=[cx.r, sel.r, xedge.r], writes=[xedge.r])

            kb.dma("sp", sk[64:65, :], d["sink"][l], writes=[sk.r], own=sk.r)
            kb.op("act", lambda e: e.activation(out=ske[64:65, :], in_=sk[64:65, :], func=AF.Exp), reads=[sk.r], writes=[ske.r])
            for hh in range(8):
                kb.op("dve", lambda e, hh=hh: e.tensor_scalar(
                    sinkrow[64:65, hh // 4, (hh % 4) * 128:(hh % 4 + 1) * 128], c["onesf"][64:65, 0:128],
                    ske[64:65, hh:hh + 1], None, op0=ALU.mult), reads=[ske.r, c["onesf"].r], writes=[sinkrow.r])
            kb.dma("sp", kactx[:], kaT[:, TOK:TOK + CTX], writes=[kactx.r], own=kactx.r)
            kb.op("dve", lambda e: e.memset(vactx[:].rearrange("p a b c -> p (a b c)"), 1.0), writes=[vactx.r])
            self.load_v("sp", vactx, 0, 2, va_d[TOK:TOK + CTX, :])

            xh = [T(kb, ph, f"xh{i}", [128, 8, W2], F32) for i in range(2)]
            h = T(kb, ph, "hB", [128, 8, W2], BF16)
            nt = self.norm_tiles(ph, W2)
            qt = self.qk_tiles(ph)
            qaT = T(kb, ph, "qaT", [128, 4, TT], BF16)
            qcT = [T(kb, ph, f"qcT{i}", [128, 4, TT], BF16) for i in range(2)]
            kwin = [T(kb, ph, f"kwin{i}", [128, 512], BF16) for i in range(2)]
            vwin = [T(kb, ph, f"vwin{i}", [128, 4, 2, 65], BF16) for i in range(2)]
            for v in vwin:
                kb.op("dve", lambda e, v=v: e.memset(v[:].rearrange("p a b c -> p (a b c)"), 1.0), writes=[v.r])
            csb = T(kb, ph, "csb", [128, W2], F32)
            cu = T(kb, ph, "cu", [128, W2], F32)
            yv = T(kb, ph, "yv", [128, TT], F32)
            ob = T(kb, ph, "ob", [128, 4, TT], BF16)
            oaT = T(kb, ph, "oaT", [64, 8, TT], BF16)
            gA = [T(kb, ph, f"gA{i}", [128, TT], F32) for i in range(2)]
            gB = [T(kb, ph, f"gB{i}", [128, TT], F32) for i in range(2)]
            gC = [T(kb, ph, f"gC{i}", [128, TT], F32) for i in range(2)]
            a1 = [T(kb, ph, f"a1{i}", [128, TT], F32) for i in range(2)]
            a2 = [T(kb, ph, f"a2{i}", [128, TT], F32) for i in range(2)]
            accb = [T(kb, ph, f"accb{i}", [128, TT], F32) for i in range(2)]
            at = self.attn_tiles(ph, 3)
            pp = [T(kb, ph, f"ppj{i}", [128, 512], F32, psum=True) for i in range(2)]
            pss = [T(kb, ph, f"pss{i}", [128, 512], F32, psum=True) for i in range(2)]
            po = T(kb, ph, "po", [128, 512], F32, psum=True)
            accd = d["ACC"].rearrange("(c p) t -> p c t", p=128)
            gcd = d["GC"].rearrange("(c p) t -> p c t", p=128)
            qcd = d["QC"].rearrange("(c p) t -> p c t", p=128)

            order = list(range(1, NLT - 1)) + [0, NLT - 1, NLT]
            slot = {t: si % 2 for si, t in enumerate(order)}

            def load_x(t):
                b = xh[slot[t]]
                if t == NLT:
                    kb.dma("sp", b[:, :, 1:TT + 1], xs[:, :, TOK:TOK + CTX], writes=[b.r], own=b.r)
                elif t == 0:
                    kb.dma("sp", b[:, :, 1:W2], xs[:, :, 0:TT + 1], writes=[b.r], own=b.r)
                    kb.op("pool", lambda e: e.tensor_copy(b[:, :, 0], xedge[:, :, 0]), reads=[xedge.r], writes=[b.r])
                elif t == NLT - 1:
                    kb.dma("sp", b[:, :, 0:TT + 1], xs[:, :, t * TT - 1:(t + 1) * TT], writes=[b.r], own=b.r)
                    kb.op("pool", lambda e: e.tensor_copy(b[:, :, TT + 1], xedge[:, :, 1]), reads=[xedge.r], writes=[b.r])
                else:
                    kb.dma("sp", b[:], xs[:, :, t * TT - 1:(t + 1) * TT + 1], writes=[b.r], own=b.r)

            def load_kv(t):
                if t >= NLT:
                    return
                kw, vw = kwin[slot[t]], vwin[slot[t]]
                lo, hi = t * TT - 128, t * TT + 384
                if t == 0:
                    kb.dma("sp", kw[:, 128:512], kaT[:, 0:hi], writes=[kw.r], own=kw.r)
                    self.load_v("sp", vw, 1, 3, va_d[0:hi, :])
                    kb.op("pool", lambda e: e.tensor_copy(kw[:, 0:128], halo[:, 0, :]), reads=[halo.r], writes=[kw.r])
                    kb.op("pool", lambda e: e.tensor_copy(vw[:, 0, :, 0:64], halo[:, 2, :].rearrange("p (kv dd) -> p kv dd", kv=2)),
                          reads=[halo.r], writes=[vw.r])
                elif t == NLT - 1:
                    kb.dma("sp", kw[:, 0:384], kaT[:, lo:TOK], writes=[kw.r], own=kw.r)
                    self.load_v("sp", vw, 0, 3, va_d[lo:TOK, :])
                    kb.op("pool", lambda e: e.tensor_copy(kw[:, 384:512], halo[:, 1, :]), reads=[halo.r], writes=[kw.r])
                    kb.op("pool", lambda e: e.tensor_copy(vw[:, 3, :, 0:64], halo[:, 3, :].rearrange("p (kv dd) -> p kv dd", kv=2)),
                          reads=[halo.r], writes=[vw.r])
                else:
                    kb.dma("sp", kw[:], kaT[:, lo:hi], writes=[kw.r], own=kw.r)
                    self.load_v("sp", vw, 0, 4, va_d[lo:hi, :])

            def proj(pt, half, wt, c0, rhs_lo, rhs_hi, hh=h):
                Wd = rhs_hi - rhs_lo
                o0 = half * 256
                for ch in range(8):
                    kb.op("pe", lambda e, ch=ch: e.matmul(pt[:, o0:o0 + Wd], wt[:, ch, c0:c0 + 128], hh[:, ch, rhs_lo:rhs_hi],
                                                         start=(ch == 0), stop=(ch == 7)),
                          reads=[wt.r, hh.r], writes=[pt.r])

            load_x(order[0])
            load_kv(order[0])
            for si, t in enumerate(order):
                col = 1 if t == NLT else 0
                lat = col == 0
                if si + 1 < NTILE:
                    if order[si + 1] == 0:
                        prep_halos()
                    load_x(order[si + 1])
                    load_kv(order[si + 1])
                x = xh[slot[t]]
                self.emit_norm(nt, x, x.r, W2, l, 1, col, h, h.r)
                if lat:
                    self.load_rope(qt, t)
                qc = qcT[slot[t]]
                for qi in range(8):
                    pt = pp[qi % 2]
                    proj(pt, 0, win, qi * 128, 1, TT + 1)
                    if qi < 4:
                        self.emit_qknorm(qt, pt[:, 0:TT], pt.r, l, 0, lat, qaT[:, qi, :], qaT.r)
                    else:
                        self.emit_qknorm(qt, pt[:, 0:TT], pt.r, l, 2, lat, qc[:, qi - 4, :], qc.r)
                kb.dma("sp", qcd[:, :, t * TT:(t + 1) * TT], qc[:], reads=[qc.r], own=qc.r)
                for cc in range(4):
                    proj(pp[0], 0, win, 1536 + cc * 128, 0, W2)
                    proj(pp[1], 0, win, 2048 + cc * 128, 0, W2)
                    kb.op("act", lambda e: e.copy(csb[:], pp[0][:, 0:W2]), reads=[pp[0].r], writes=[csb.r])
                    kb.op("dve", lambda e: e.tensor_tensor(cu[:], csb[:], pp[1][:, 0:W2], op=ALU.mult),
                          reads=[csb.r, pp[1].r], writes=[cu.r])
                    if not lat:
                        kb.op("dve", lambda e: e.memset(cu[:, 0:1], 0.0), writes=[cu.r])
                        kb.op("dve", lambda e: e.memset(cu[:, TT + 1:W2], 0.0), writes=[cu.r])
                    elif t == 0:
                        kb.op("dve", lambda e: e.tensor_scalar(cu[:, 0:1], cu[:, 0:1], em[:, 0:1], None, op0=ALU.mult),
                              reads=[em.r, cu.r], writes=[cu.r])
                    elif t == NLT - 1:
                        kb.op("dve", lambda e: e.tensor_scalar(cu[:, TT + 1:W2], cu[:, TT + 1:W2], em[:, 1:2], None, op0=ALU.mult),
                              reads=[em.r, cu.r], writes=[cu.r])
                    kb.op("dve", lambda e, cc=cc: e.tensor_scalar(yv[:], cu[:, 0:TT], cw[:, cc, 0:1], None, op0=ALU.mult),
                          reads=[cu.r, cw.r], writes=[yv.r])
                    for k in (1, 2):
                        kb.op("dve", lambda e, cc=cc, k=k: e.scalar_tensor_tensor(
                            out=yv[:], in0=cu[:, k:k + TT], scalar=cw[:, cc, k:k + 1], in1=yv[:], op0=ALU.mult, op1=ALU.add),
                            reads=[cu.r, cw.r, yv.r], writes=[yv.r])
                    proj(pp[0], 0, win, 1024 + cc * 128, 1, TT + 1)
                    kb.op("dve", lambda e, cc=cc: e.tensor_tensor(ob[:, cc, :], yv[:], pp[0][:, 0:TT], op=ALU.mult),
                          reads=[yv.r, pp[0].r], writes=[ob.r])
                kw, vw = kwin[slot[t]], vwin[slot[t]]
                for qb in range(2):
                    for kv in range(2):
                        ks = slice(kv * 64, (kv + 1) * 64)
                        chunks = [(kactx[ks, ci * 128:(ci + 1) * 128], kactx.r, vactx[:, ci, kv, :], vactx.r, None) for ci in range(2)]
                        if lat:
                            for wi in (qb, qb + 1, qb + 2):
                                m_ap = None
                                if wi == qb:
                                    m_ap = wm[:, 2 if (t == 0 and qb == 0) else 0, :]
                                elif wi == qb + 2:
                                    m_ap = wm[:, 3 if (t == NLT - 1 and qb == 1) else 1, :]
                                chunks.append((kw[ks, wi * 128:(wi + 1) * 128], kw.r, vw[:, wi, kv, :], vw.r, m_ap))
                        epi = self.attn_group(at, pss, po, chunks, qaT[ks, :, qb * 128:(qb + 1) * 128], qaT.r,
                                              oaT[:, kv * 4:(kv + 1) * 4, qb * 128:(qb + 1) * 128], oaT.r,
                                              sinkrow=sinkrow[64:65, kv, :])
                        epi()
                pool6 = [pp[0], pp[1], pss[0], pss[1], po, at["pbb"]]
                pc_ = [0]

                def nxt():
                    pc_[0] += 1
                    return pool6[pc_[0] % 6]

                for oc in range(8):
                    i2 = oc % 2
                    for gi_, gbuf in ((0, gA[i2]), (1, gB[i2]), (2, gC[i2])):
                        pt = nxt()
                        proj(pt, 0, wgt, (gi_ * 8 + oc) * 128, 1, TT + 1)
                        kb.op("act", lambda e, pt=pt, gbuf=gbuf, gi_=gi_, oc=oc: e.activation(
                            out=gbuf[:], in_=pt[:, 0:TT], func=AF.Sigmoid, bias=bgt[:, gi_ * 8 + oc: gi_ * 8 + oc + 1], scale=1.0),
                            reads=[pt.r, bgt.r], writes=[gbuf.r])
                    kb.dma("sp", gcd[:, oc, t * TT:(t + 1) * TT], gC[i2][:], reads=[gC[i2].r], own=gC[i2].r)
                    pa_ = nxt()
                    for hd in range(8):
                        kb.op("pe", lambda e, hd=hd, oc=oc, pa_=pa_: e.matmul(pa_[:, 0:TT], wpa[:, hd, oc * 128:(oc + 1) * 128], oaT[:, hd, :],
                                                                            start=(hd == 0), stop=(hd == 7)),
                              reads=[wpa.r, oaT.r], writes=[pa_.r])
                    pb_ = nxt()
                    for cc in range(4):
                        kb.op("pe", lambda e, cc=cc, oc=oc, pb_=pb_: e.matmul(pb_[:, 0:TT], wpb[:, cc, oc * 128:(oc + 1) * 128], ob[:, cc, :],
                                                                            start=(cc == 0), stop=(cc == 3)),
                              reads=[wpb.r, ob.r], writes=[pb_.r])
                    kb.op("dve", lambda e, i2=i2, pa_=pa_: e.tensor_tensor(a1[i2][:], pa_[:, 0:TT], gA[i2][:], op=ALU.mult),
                          reads=[pa_.r, gA[i2].r], writes=[a1[i2].r])
                    kb.op("dve", lambda e, i2=i2, pb_=pb_: e.tensor_tensor(a2[i2][:], pb_[:, 0:TT], gB[i2][:], op=ALU.mult),
                          reads=[pb_.r, gB[i2].r], writes=[a2[i2].r])
                    kb.op("pool", lambda e, i2=i2: e.tensor_tensor(accb[i2][:], a1[i2][:], a2[i2][:], op=ALU.add),
                          reads=[a1[i2].r, a2[i2].r], writes=[accb[i2].r])
                    kb.dma("sp", accd[:, oc, t * TT:(t + 1) * TT], accb[i2][:], reads=[accb[i2].r], own=accb[i2].r)
            kb.phase_end()

    def phase_B2(self, l, xsrc, xdst):
        nc, kb, d, c = self.nc, self.kb, self.d, self.c
        xs = xsrc.rearrange("(c p) t -> p c t", p=128)
        xd = xdst.rearrange("(c p) t -> p c t", p=128)
        accd = d["ACC"].rearrange("(c p) t -> p c t", p=128)
        gcd = d["GC"].rearrange("(c p) t -> p c t", p=128)
        qcd = d["QC"].rearrange("(c p) t -> p c t", p=128)
        with ExitStack() as ph:
            self.precast(self.next_key(l, "B2"))
            kcall = T(kb, ph, "kcall", [128, KALL], BF16)
            vcall = T(kb, ph, "vcall", [128, NKC + 1, 2, 65], BF16)
            kb.op("dve", lambda e: e.memset(vcall[:].rearrange("p a b c -> p (a b c)"), 1.0), writes=[vcall.r])
            par = l % 2
            gkc, gvc = d[f"gkc{par}"], d[f"gvc{par}"]
            kb.dma("sp", kcall[:, 0:CTX], d[f"kcc{par}"][:, :], writes=[kcall.r], own=kcall.r)
            self.load_v("act", vcall, 0, 2, d[f"vcc{par}"][:, :])
            for i in range(4):
                for hh in range(2):
                    c0 = hh * 2048
                    kb.dma("sp", kcall[:, CTX + i * TOK + c0: CTX + i * TOK + c0 + 2048], gkc[i * 128:(i + 1) * 128, c0:c0 + 2048],
                           writes=[kcall.r], own=kcall.r)
                for k0 in range(0, 32, 4):
                    self.load_v("act", vcall, 2 + i * 32 + k0, 4, gvc[i * TOK + k0 * 128: i * TOK + (k0 + 4) * 128, :])
            wpc = T(kb, ph, "wpc", [64, 8, D], BF16)
            self.load_w(wpc, d["S_w_pc"][l].rearrange("(h dd) n -> dd h n", dd=64), 8, 4)
            wo = T(kb, ph, "wo", [128, 8, D], BF16)
            self.load_w(wo, d["S_w_o"][l].rearrange("(c p) n -> p c n", p=128), 8, 4)
            xb = [T(kb, ph, f"xb2{i}", [128, 8, TT], F32) for i in range(2)]
            qcb = [T(kb, ph, f"qcb{i}", [128, 4, TT], BF16) for i in range(2)]
            acb = [T(kb, ph, f"acb{i}", [128, 8, TT], F32) for i in range(2)]
            gcb = [T(kb, ph, f"gcb{i}", [128, 8, TT], F32) for i in range(2)]
            ocT = T(kb, ph, "ocT", [64, 8, TT], BF16)
            m1 = [T(kb, ph, f"m1{i}", [128, TT], F32) for i in range(2)]
            mrg = T(kb, ph, "mrg", [128, 8, TT], BF16)
            st = {"cnt": 0, "vr": vcall.r,
                  "dsb": T(kb, ph, "dsb2", [128, 2, 512], F32), "rbs": T(kb, ph, "rbs2", [64, 512], F32),
                  "pbb": T(kb, ph, "pbb2", [128, 512], F32, psum=True)}
            vflat = vcall[:].rearrange("p a b c -> p (a b c)")
            pTp = [T(kb, ph, f"pTp{i}", [128, 2, 512], BF16) for i in range(3)]
            psp = [T(kb, ph, f"psp{i}", [128, 2, 512], F32, psum=True) for i in range(2)]
            po = T(kb, ph, "po2", [128, 2, 512], F32, psum=True)
            pp = [psp[0][:, 0, :], psp[1][:, 0, :]]
            ppr = [psp[0].r, psp[1].r]

            def load_t(t):
                i = t % 2
                kb.dma("sp", xb[i][:], xs[:, :, t * TT:(t + 1) * TT], writes=[xb[i].r], own=xb[i].r)
                kb.dma("sp", qcb[i][:], qcd[:, :, t * TT:(t + 1) * TT], writes=[qcb[i].r], own=qcb[i].r)
                kb.dma("sp", acb[i][:], accd[:, :, t * TT:(t + 1) * TT], writes=[acb[i].r], own=acb[i].r)
                kb.dma("sp", gcb[i][:], gcd[:, :, t * TT:(t + 1) * TT], writes=[gcb[i].r], own=gcb[i].r)

            load_t(0)
            g = 0
            pend = None
            for t in range(NTILE):
                col = 1 if t == NLT else 0
                if t + 1 < NTILE:
                    load_t(t + 1)
                i = t % 2
                x, qc, ac, gc = xb[i], qcb[i], acb[i], gcb[i]
                nkc = NKC if col == 0 else 2
                for qb in range(2):
                    pend = self.attn_pair(st, psp, pTp, po, kcall, vflat, nkc, qc, qb, ocT, prev_epi=pend)
                pend()
                pend = None
                for oc in range(8):
                    pt, ptr = pp[oc % 2], ppr[oc % 2]
                    for hd in range(8):
                        kb.op("pe", lambda e, hd=hd, oc=oc, pt=pt: e.matmul(pt[:, 0:TT], wpc[:, hd, oc * 128:(oc + 1) * 128], ocT[:, hd, :],
                                                                          start=(hd == 0), stop=(hd == 7)),
                              reads=[wpc.r, ocT.r], writes=[ptr])
                    mm = m1[oc % 2]
                    kb.op("dve", lambda e, pt=pt, mm=mm, oc=oc: e.tensor_tensor(mm[:], pt[:, 0:TT], gc[:, oc, :], op=ALU.mult),
                          reads=[ptr, gc.r], writes=[mm.r])
                    kb.op("pool", lambda e, mm=mm, oc=oc: e.tensor_tensor(mrg[:, oc, :], mm[:], ac[:, oc, :], op=ALU.add),
                          reads=[mm.r, ac.r], writes=[mrg.r])
                for oc in range(8):
                    pt, ptr = pp[oc % 2], ppr[oc % 2]
                    for ch in range(8):
                        kb.op("pe", lambda e, ch=ch, oc=oc, pt=pt: e.matmul(pt[:, 0:TT], wo[:, ch, oc * 128:(oc + 1) * 128], mrg[:, ch, :],
                                                                          start=(ch == 0), stop=(ch == 7)),
                              reads=[wo.r, mrg.r], writes=[ptr])
                    kb.op("dve", lambda e, pt=pt, oc=oc: e.scalar_tensor_tensor(
                        out=x[:, oc, :], in0=pt[:, 0:TT], scalar=self.mod("modG", l, 1, oc, col), in1=x[:, oc, :],
                        op0=ALU.mult, op1=ALU.add), reads=[ptr, x.r, c["modG"].r], writes=[x.r])
                kb.dma("sp", xd[:, :, t * TT:(t + 1) * TT], x[:], reads=[x.r], own=x.r)
            kb.phase_end()


def _rope_tables_np(pos0):
    pos = np.arange(pos0, pos0 + TOK)
    row = (pos // 64).astype(np.float32)
    colp = (pos % 64).astype(np.float32)
    half = 32
    inv = (np.float32(10000.0) ** (-np.arange(0, half, 2, dtype=np.float32) / np.float32(half))).astype(np.float32)
    C = np.zeros((128, TOK), np.float32)
    S = np.zeros((128, TOK), np.float32)
    for p in range(128):
        dd = p % 64
        blk, within = dd // 32, dd % 32
        i = within % 16
        first = within < 16
        ang = ((row if blk == 0 else colp) * inv[i]).astype(np.float32)
        C[p] = np.cos(ang)
        S[p] = -np.sin(ang) if first else np.sin(ang)
    return C, S


def _perm_matrix():
    Pm = np.zeros((128, 128), np.float32)
    for m in range(128):
        dd = m % 64
        within = dd % 32
        partner = m + 16 if within < 16 else m - 16
        Pm[partner, m] = 1.0
    return Pm


_QPERM = np.concatenate([np.concatenate([np.arange(j * 64, (j + 1) * 64), np.arange((4 + j) * 64, (5 + j) * 64)])
                         for j in range(4)])


def _prep_common(inp):
    f = lambda a: np.ascontiguousarray(np.asarray(a, dtype=np.float32))
    w_in = f(inp["w_in"]).copy()
    w_in[:, :, 512:1024] = w_in[:, :, 512 + _QPERM]
    w_in[:, :, 1024:1536] = w_in[:, :, 1024 + _QPERM]
    com = {
        "w_ada": f(inp["w_ada"]),
        "b_ada": f(np.asarray(inp["b_ada"]).reshape(DEPTH, 72, 128).transpose(0, 2, 1)),
        "norm_g": f(np.asarray(inp["norm_g"]).reshape(DEPTH, 3, 8, 128).transpose(0, 3, 1, 2)),
        "ident": np.eye(128, dtype=np.float32),
        "permm": _perm_matrix(),
        "qk_g": f(np.tile(np.asarray(inp["qk_g"]), (1, 1, 2)).transpose(0, 2, 1)),
        "w_in": w_in,
        "ffn_w_gate": f(inp["ffn_w_gate"]), "ffn_w_up": f(inp["ffn_w_up"]), "ffn_w_down": f(inp["ffn_w_down"]),
        "sink_a": f(np.asarray(inp["sink_a"]).reshape(DEPTH, 1, 8)),
        "conv_w": f(np.asarray(inp["conv_w"]).reshape(DEPTH, 3, 4, 128).transpose(0, 3, 2, 1)),
        "w_pa": f(inp["w_pa"]), "w_pb": f(inp["w_pb"]), "w_pc": f(inp["w_pc"]),
        "w_gate": f(inp["w_gate"]),
        "b_gate": f(np.asarray(inp["b_gate"]).reshape(DEPTH, 24, 128).transpose(0, 2, 1)),
        "w_o": f(inp["w_o"]),
    }
    return com


def _prep_core(inp, r):
    b, q = r // 4, r % 4
    x = np.asarray(inp["x"], dtype=np.float32)
    ctx = np.asarray(inp["ctx"], dtype=np.float32)
    xT = np.concatenate([x[b, q * TOK:(q + 1) * TOK].T, ctx[b].T], axis=1)
    cvec = np.stack([np.asarray(inp["c"], np.float32)[b], np.asarray(inp["c_ctx"], np.float32)], axis=1)
    cvec = cvec.reshape(8, 128, 2).transpose(1, 0, 2)
    C, S = _rope_tables_np(q * TOK)
    kk = np.arange(128)[:, None]
    qq = np.arange(128)[None, :]
    mprev = np.where(kk >= qq, 0.0, NEG).astype(np.float32)
    mnext = np.where(kk <= qq, 0.0, NEG).astype(np.float32)
    allneg = np.full((128, 128), NEG, np.float32)
    wm = np.stack([np.tile(m, (1, 4)) for m in (mprev, mnext, mprev if q > 0 else allneg, mnext if q < 3 else allneg)], axis=1)
    emask = np.zeros((128, 2), np.float32)
    emask[:, 0] = 1.0 if q > 0 else 0.0
    emask[:, 1] = 1.0 if q < 3 else 0.0
    sel = np.zeros((128, 8), np.float32)
    if q > 0:
        sel[:, q - 1] = 1.0
    if q < 3:
        sel[:, 4 + q + 1] = 1.0
    return {"xin": np.ascontiguousarray(xT), "cvec": np.ascontiguousarray(cvec), "ropeC": C, "ropeS": S,
            "wmask": np.ascontiguousarray(wm), "emask": emask, "sel": sel}


_PROGS = {}


def _get_prog(phases, fused=False, nlayers=1):
    key = (tuple(phases), fused, nlayers)
    if key not in _PROGS:
        _PROGS[key] = Prog(set(phases), fused, nlayers).build()
    return _PROGS[key]


def _run(prog, maps):
    maps = [{k: m[k] for k in prog.in_names} for m in maps]
    res = run_bass_kernel_spmd(prog.nc, maps, core_ids=list(range(NCORE)))
    return res.results


def _layer_slice(com, l, which=None):
    out = {}
    for k, v in com.items():
        if k in ("ident", "permm"):
            out[k] = v
        elif k.startswith("ffn_") and which is not None:
            out[k] = np.ascontiguousarray(v[l:l + 1, which:which + 1])
        else:
            out[k] = np.ascontiguousarray(v[l:l + 1])
    return out


def kernel(**inp):
    com = _prep_common(inp)
    prog = _get_prog(["A", "X", "B1", "B2", "C"], fused=True, nlayers=DEPTH)
    maps = []
    for r in range(NCORE):
        m = dict(com, **_prep_core(inp, r))
        maps.append(m)
    res = _run(prog, maps)
    out = np.zeros((2, SEQ, D), np.float32)
    for r in range(NCORE):
        b, q = r // 4, r % 4
        out[b, q * TOK:(q + 1) * TOK] = res[r]["xw"][:, :TOK].T
    return out
```

```python
import numpy as np
import ml_dtypes
from contextlib import ExitStack
import concourse.bass as bass
import concourse.mybir as mybir
from concourse.bass_utils import run_bass_kernel_spmd

F32 = mybir.dt.float32
BF16 = mybir.dt.bfloat16
AF = mybir.ActivationFunctionType
ALU = mybir.AluOpType

D = 1024
DFF = 2816
NJ = DFF // 128
DEPTH = 4
SEQ = 16384
NCORE = 8
TOK = 4096
CTX = 256
TT = 256
NLT = TOK // TT
NTILE = NLT + 1
TCOLS = TOK + CTX
EPS = 1e-6
SCALE = 0.125
NEG = -1e30
KH = CTX + 128 + TOK + 128
KALL = CTX + SEQ
NKC = KALL // 128


class Res:
    __slots__ = ("name", "w", "r", "dsem")

    def __init__(self, name):
        self.name = name
        self.w = {}
        self.r = {}
        self.dsem = None


class _Eng:
    def __init__(self, name, e, sem):
        self.name, self.e, self.sem, self.cnt, self.seen = name, e, sem, 0, {}


class _DSem:
    def __init__(self, key, sem):
        self.key, self.sem, self.cnt = key, sem, 0


class KB:
    def __init__(self, nc, es):
        self.nc, self.es = nc, es
        self.engs = {}
        for name, e in (("pe", nc.tensor), ("act", nc.scalar), ("dve", nc.vector),
                        ("pool", nc.gpsimd), ("sp", nc.sync)):
            self.engs[name] = _Eng(name, e, es.enter_context(nc.semaphore("s_" + name)))
        self.dsems = []
        self.free_dsems = []
        self.phase_dsems = []
        self.nres = 0

    def res(self, name="r"):
        self.nres += 1
        return Res(f"{name}{self.nres}")

    def _dsem(self, r):
        if r.dsem is None:
            if self.free_dsems:
                r.dsem = self.free_dsems.pop()
            else:
                r.dsem = _DSem("d_" + r.name, self.es.enter_context(self.nc.semaphore("d_" + r.name)))
                self.dsems.append(r.dsem)
            self.phase_dsems.append(r.dsem)
        return r.dsem

    def phase_end(self):
        self.barrier()
        self.free_dsems.extend(self.phase_dsems)
        self.phase_dsems = []

    def _deps(self, E, reads, writes):
        toks = {}
        for r in reads:
            for k, t in r.w.items():
                if k not in toks or toks[k][1] < t[1]:
                    toks[k] = t
        for w in writes:
            for dct in (w.w, w.r):
                for k, t in dct.items():
                    if k not in toks or toks[k][1] < t[1]:
                        toks[k] = t
        for k, (sem, val) in toks.items():
            if k == E.name or E.seen.get(k, 0) >= val:
                continue
            E.e.wait_ge(sem, val)
            E.seen[k] = val

    def op(self, eng, fn, reads=(), writes=()):
        E = self.engs[eng]
        self._deps(E, reads, writes)
        ins = fn(E.e)
        E.cnt += 1
        ins.then_inc(E.sem, 1)
        tok = (E.sem, E.cnt)
        for r in reads:
            r.r[E.name] = tok
        for w in writes:
            w.w[E.name] = tok
        return ins

    def dma(self, q, out, in_, reads=(), writes=(), own=None):
        E = self.engs[q]
        self._deps(E, reads, writes)
        ds = self._dsem(own)
        ins = E.e.dma_start(out=out, in_=in_)
        ds.cnt += 16
        ins.then_inc(ds.sem, 16)
        tok = (ds.sem, ds.cnt)
        for r in reads:
            r.r[ds.key] = tok
        for w in writes:
            w.w[ds.key] = tok

    def barrier(self):
        toks = {E.name: (E.sem, E.cnt) for E in self.engs.values() if E.cnt > 0}
        for ds in self.dsems:
            if ds.cnt > 0:
                toks[ds.key] = (ds.sem, ds.cnt)
        for E in self.engs.values():
            for k, (sem, val) in toks.items():
                if k == E.name or E.seen.get(k, 0) >= val:
                    continue
                E.e.wait_ge(sem, val)
                E.seen[k] = val


class T:
    def __init__(self, kb, es, name, shape, dtype, psum=False):
        nc = kb.nc
        kb.nres += 1
        self.t = es.enter_context((nc.psum_tensor if psum else nc.sbuf_tensor)(f"{name}_{kb.nres}", shape, dtype))
        self.r = kb.res(name)

    def __getitem__(self, idx):
        return self.t[idx]


class Prog:
    def __init__(self, phases, fused=False, nlayers=1):
        self.phases = phases
        self.fused = fused
        self.nl = nlayers
        self.nc = bass.Bass("TRN2", target_bir_lowering=False)
        self.in_names = []
        self.out_names = []

    def din(self, name, shape, dt=F32):
        self.in_names.append(name)
        return self.nc.dram_tensor(name, list(shape), dt, kind="ExternalInput").ap()

    def dout(self, name, shape, dt=F32):
        self.out_names.append(name)
        return self.nc.dram_tensor(name, list(shape), dt, kind="ExternalOutput").ap()

    def dint(self, name, shape, dt=F32):
        return self.nc.dram_tensor(name, list(shape), dt, kind="Internal").ap()

    def build(self):
        nc = self.nc
        L = self.nl
        P = self.phases
        with ExitStack() as es:
            kb = self.kb = KB(nc, es)
            d = self.d = {}
            d["cvec"] = self.din("cvec", [128, 8, 2])
            d["w_ada"] = self.din("w_ada", [L, D, 9 * D])
            d["b_ada"] = self.din("b_ada", [L, 128, 72])
            d["norm_g"] = self.din("norm_g", [L, 128, 3, 8])
            d["ident"] = self.din("ident", [128, 128])
            d["permm"] = self.din("permm", [128, 128])
            d["ropeC"] = self.din("ropeC", [128, TOK])
            d["ropeS"] = self.din("ropeS", [128, TOK])
            d["qk_g"] = self.din("qk_g", [L, 128, 4])
            d["w_in"] = self.din("w_in", [L, D, 3072])
            self.ffn_which = [w for w, p in ((0, "A"), (1, "C")) if p in P]
            if self.ffn_which:
                nw = len(self.ffn_which)
                d["wg"] = self.din("ffn_w_gate", [L, nw, D, DFF])
                d["wu"] = self.din("ffn_w_up", [L, nw, D, DFF])
                d["wd"] = self.din("ffn_w_down", [L, nw, DFF, D])
            if "B1" in P or "B2" in P:
                d["sink"] = self.din("sink_a", [L, 1, 8])
                d["conv_w"] = self.din("conv_w", [L, 128, 4, 3])
                d["w_pa"] = self.din("w_pa", [L, 512, D])
                d["w_pb"] = self.din("w_pb", [L, 512, D])
                d["w_pc"] = self.din("w_pc", [L, 512, D])
                d["w_gate"] = self.din("w_gate", [L, D, 3 * D])
                d["b_gate"] = self.din("b_gate", [L, 128, 24])
                d["w_o"] = self.din("w_o", [L, D, D])
                d["wmask"] = self.din("wmask", [128, 4, 512])
                d["emask"] = self.din("emask", [128, 2])
            d["xin"] = self.din("xin", [D, TCOLS])
            d["xw"] = self.dout("xw", [D, TCOLS])
            if self.fused:
                d["sel"] = self.din("sel", [128, 8])
                d["xedge_dummy"] = None
                for par in range(2):
                    d[f"kaT{par}"] = self.dint(f"kaT{par}", [128, TCOLS], BF16)
                    d[f"va{par}"] = self.dint(f"va{par}", [TCOLS, 128], BF16)
                    d[f"kae{par}"] = self.dint(f"kae{par}", [128, 256], BF16)
                    d[f"vae{par}"] = self.dint(f"vae{par}", [256, 128], BF16)
                    d[f"kcl{par}"] = self.dint(f"kcl{par}", [128, TOK], BF16)
                    d[f"vcl{par}"] = self.dint(f"vcl{par}", [TOK, 128], BF16)
                    d[f"kcc{par}"] = self.dint(f"kcc{par}", [128, CTX], BF16)
                    d[f"vcc{par}"] = self.dint(f"vcc{par}", [CTX, 128], BF16)
                    d[f"xe{par}"] = self.dint(f"xe{par}", [128, 16])
                    d[f"gkae{par}"] = self.dint(f"gkae{par}", [4 * 128, 256], BF16)
                    d[f"gvae{par}"] = self.dint(f"gvae{par}", [4 * 256, 128], BF16)
                    d[f"gkc{par}"] = self.dint(f"gkc{par}", [4 * 128, TOK], BF16)
                    d[f"gvc{par}"] = self.dint(f"gvc{par}", [4 * TOK, 128], BF16)
                    d[f"gxe{par}"] = self.dint(f"gxe{par}", [4 * 128, 16])
                    d[f"syn{par}"] = self.dint(f"syn{par}", [128, 16])
                    d[f"gsyn{par}"] = self.dint(f"gsyn{par}", [4 * 128, 16])
                for nm, shp in (("wg", [L, 2, D, DFF]), ("wu", [L, 2, D, DFF]), ("wd", [L, 2, DFF, D]), ("w_in", [L, D, 3072]),
                                ("w_gate", [L, D, 3 * D]), ("w_pa", [L, 512, D]), ("w_pb", [L, 512, D]), ("w_pc", [L, 512, D]),
                                ("w_o", [L, D, D])):
                    d["S_" + nm] = self.dint("S_" + nm, shp, BF16)
                d["QC"] = self.dint("QC", [512, TCOLS], BF16)
                d["ACC"] = self.dint("ACC", [D, TCOLS])
                d["GC"] = self.dint("GC", [D, TCOLS])

            c = self.c = {}
            c["ones"] = T(kb, es, "ones", [128, 128], BF16)
            c["bones"] = T(kb, es, "bones", [128, 128], BF16)
            c["onesf"] = T(kb, es, "onesf", [128, 128], F32)
            c["ident"] = T(kb, es, "identb", [128, 128], BF16)
            c["permm"] = T(kb, es, "permb", [128, 128], BF16)
            c["qkg"] = T(kb, es, "qkg", [128, L, 4], F32)
            c["epsc"] = T(kb, es, "epsc", [128, 1], F32)
            c["modA"] = T(kb, es, "modA", [128, L * 3 * 8 * 2], F32)
            c["modB"] = T(kb, es, "modB", [128, L * 3 * 8 * 2], F32)
            c["modG"] = T(kb, es, "modG", [128, L * 3 * 8 * 2], F32)
            kb.op("dve", lambda e: e.memset(c["ones"][:], 1.0), writes=[c["ones"].r])
            kb.op("dve", lambda e: e.memset(c["bones"][:], 0.0), writes=[c["bones"].r])
            kb.op("dve", lambda e: e.memset(c["bones"][0:64, 0:64], 1.0), writes=[c["bones"].r])
            kb.op("dve", lambda e: e.memset(c["bones"][64:128, 64:128], 1.0), writes=[c["bones"].r])
            kb.op("dve", lambda e: e.memset(c["onesf"][:], 1.0), writes=[c["onesf"].r])
            kb.op("dve", lambda e: e.memset(c["epsc"][:], EPS), writes=[c["epsc"].r])
            kb.dma("pool", c["ident"][:], d["ident"][:, :], writes=[c["ident"].r], own=c["ident"].r)
            kb.dma("pool", c["permm"][:], d["permm"][:, :], writes=[c["permm"].r], own=c["permm"].r)
            kb.dma("sp", c["qkg"][:], d["qk_g"].rearrange("l p f -> p l f"), writes=[c["qkg"].r], own=c["qkg"].r)

            kb.barrier()
            kb.phase_dsems = []
            self.precast((0, "A"))
            for l in range(L):
                self.phase_M(l)
            xsrc = d["xin"]
            for l in range(L):
                if "A" in P:
                    self.phase_ffn(l, 0, xsrc, d["xw"], with_kv=True)
                    xsrc = d["xw"]
                if "X" in P:
                    self.phase_X(l)
                if "B1" in P:
                    self.phase_B1(l, xsrc)
                if "B2" in P:
                    self.phase_B2(l, xsrc, d["xw"])
                    xsrc = d["xw"]
                if "C" in P:
                    self.phase_ffn(l, 1, xsrc, d["xw"], with_kv=False)
                    xsrc = d["xw"]
            kb.phase_end()
        return self

    def precast(self, key, after=()):
        if key is None:
            return
        after = [t.r for t in after]
        l, p = key
        kb, d = self.kb, self.d
        if p == "A":
            items = [("wg", (l, 0)), ("wu", (l, 0)), ("wd", (l, 0)), ("w_in", (l,))]
        elif p == "B1":
            items = [("w_gate", (l,)), ("w_pa", (l,)), ("w_pb", (l,))]
        elif p == "B2":
            items = [("w_pc", (l,)), ("w_o", (l,))]
        else:
            items = [("wg", (l, 1)), ("wu", (l, 1)), ("wd", (l, 1))]
        for nm, idx in items:
            src, dst = d[nm], d["S_" + nm]
            for i in idx:
                src, dst = src[i], dst[i]
            r = kb.res("cast")
            rows = src.shape[0]
            h = rows // 2
            for a, b in ((0, h), (h, rows)):
                kb.dma("pool", dst[a:b, :], src[a:b, :], reads=after, own=r)

    def next_key(self, l, p):
        seq = [(ll, pp) for ll in range(self.nl) for pp in ("A", "B1", "B2", "C")]
        i = seq.index((l, p))
        return seq[i + 1] if i + 1 < len(seq) else None

    def mod(self, which, l, n, ch, col):
        i = ((l * 3 + n) * 8 + ch) * 2 + col
        return self.c[which][:, i:i + 1]

    def phase_M(self, l):
        nc, kb, d, c = self.nc, self.kb, self.d, self.c
        with ExitStack() as ph:
            cv = T(kb, ph, "cv", [128, 8, 2], F32)
            sc = T(kb, ph, "sc", [128, 8, 2], F32)
            sg = T(kb, ph, "sgm", [128, 8, 2], F32)
            ba = T(kb, ph, "ba", [128, 72], F32)
            ng = T(kb, ph, "ng", [128, 3, 8], F32)
            mt = T(kb, ph, "mt", [128, 72, 2], F32)
            st = [T(kb, ph, f"wst{i}", [128, 8, 1024], F32) for i in range(2)]
            pm = T(kb, ph, "pm", [128, 72, 2], F32, psum=True)
            kb.dma("sp", cv[:], d["cvec"][:, :, :], writes=[cv.r], own=cv.r)
            kb.dma("sp", ba[:], d["b_ada"][l], writes=[ba.r], own=ba.r)
            kb.dma("sp", ng[:], d["norm_g"][l], writes=[ng.r], own=ng.r)
            kb.op("act", lambda e: e.activation(out=sg[:], in_=cv[:], func=AF.Sigmoid), reads=[cv.r], writes=[sg.r])
            kb.op("dve", lambda e: e.tensor_tensor(sc[:], cv[:], sg[:], op=ALU.mult), reads=[cv.r, sg.r], writes=[sc.r])
            wsrc = d["w_ada"][l].rearrange("(kc p) n -> p kc n", p=128)
            for cb in range(9):
                s = st[cb % 2]
                q = "sp" if cb % 2 == 0 else "act"
                kb.dma(q, s[:], wsrc[:, :, cb * 1024:(cb + 1) * 1024], writes=[s.r], own=s.r)
                for o in range(8):
                    oc = cb * 8 + o
                    for kc in range(8):
                        kb.op("pe", lambda e, kc=kc, o=o, oc=oc: e.matmul(
                            pm[:, oc, :], s[:, kc, o * 128:(o + 1) * 128], sc[:, kc, :],
                            start=(kc == 0), stop=(kc == 7)), reads=[s.r, sc.r], writes=[pm.r])
            for col in range(2):
                kb.op("dve", lambda e, col=col: e.tensor_tensor(mt[:, :, col], pm[:, :, col], ba[:], op=ALU.add),
                      reads=[pm.r, ba.r], writes=[mt.r])
            for n in range(3):
                for col in range(2):
                    base = (l * 3 + n) * 16
                    A = c["modA"][:, base + col: base + 16: 2]
                    B = c["modB"][:, base + col: base + 16: 2]
                    G = c["modG"][:, base + col: base + 16: 2]
                    sh = mt[:, (3 * n) * 8:(3 * n + 1) * 8, col]
                    scl = mt[:, (3 * n + 1) * 8:(3 * n + 2) * 8, col]
                    gt = mt[:, (3 * n + 2) * 8:(3 * n + 3) * 8, col]
                    kb.op("dve", lambda e, A=A, scl=scl, n=n: e.scalar_tensor_tensor(
                        out=A, in0=scl, scalar=1.0, in1=ng[:, n, :], op0=ALU.add, op1=ALU.mult),
                        reads=[mt.r, ng.r], writes=[c["modA"].r])
                    kb.op("dve", lambda e, B=B, sh=sh: e.tensor_copy(B, sh), reads=[mt.r], writes=[c["modB"].r])
                    kb.op("dve", lambda e, G=G, gt=gt, n=n: e.tensor_scalar(
                        G, gt, 1.0 if n == 1 else 0.5, None, op0=ALU.mult), reads=[mt.r], writes=[c["modG"].r])
            kb.phase_end()

    def emit_norm(self, ph_t, xt, xr, width, l, n, col, hout, hr, xoff=0):
        kb, c = self.kb, self.c
        sq, pss, rs, tmp = ph_t["sq"], ph_t["pss"], ph_t["rs"], ph_t["tmp"]
        W = width
        kb.op("act", lambda e: e.activation(out=sq[:, :, 0:W], in_=xt[:, :, xoff:xoff + W], func=AF.Square),
              reads=[xr], writes=[sq.r])
        for ch in range(8):
            kb.op("pe", lambda e, ch=ch: e.matmul(pss[:, 0:W], c["ones"][:], sq[:, ch, 0:W],
                                                 start=(ch == 0), stop=(ch == 7)),
                  reads=[sq.r, c["ones"].r], writes=[pss.r])
        kb.op("act", lambda e: e.activation(out=rs[:, 0:W], in_=pss[:, 0:W], func=AF.Ln,
                                            bias=c["epsc"][:, 0:1], scale=1.0 / D),
              reads=[pss.r, c["epsc"].r], writes=[rs.r])
        kb.op("act", lambda e: e.activation(out=rs[:, 0:W], in_=rs[:, 0:W], func=AF.Exp, scale=-0.5),
              reads=[rs.r], writes=[rs.r])
        for ch in range(8):
            tb = tmp[ch % 2]
            kb.op("dve", lambda e, ch=ch, tb=tb: e.scalar_tensor_tensor(
                out=tb[:, 0:W], in0=xt[:, ch, xoff:xoff + W], scalar=self.mod("modA", l, n, ch, col),
                in1=rs[:, 0:W], op0=ALU.mult, op1=ALU.mult),
                reads=[xr, rs.r, c["modA"].r], writes=[tb.r])
            kb.op("act", lambda e, ch=ch, tb=tb: e.activation(
                out=hout[:, ch, 0:W], in_=tb[:, 0:W], func=AF.Identity,
                bias=self.mod("modB", l, n, ch, col), scale=1.0),
                reads=[tb.r, c["modB"].r], writes=[hr])

    def norm_tiles(self, ph, wmax):
        kb = self.kb
        return {
            "sq": T(kb, ph, "sq", [128, 8, wmax], BF16),
            "pss": T(kb, ph, "pss", [128, 512], F32, psum=True),
            "rs": T(kb, ph, "rs", [128, wmax], F32),
            "tmp": [T(kb, ph, f"ntmp{i}", [128, wmax], F32) for i in range(2)],
        }

    def qk_tiles(self, ph):
        kb = self.kb
        return {
            "sqk": T(kb, ph, "sqk", [128, TT], BF16),
            "psk": T(kb, ph, "psk", [128, 512], F32, psum=True),
            "rk": T(kb, ph, "rk", [128, TT], F32),
            "yf": T(kb, ph, "yf", [128, TT], F32),
            "yb": T(kb, ph, "yb", [128, TT], BF16),
            "t1": T(kb, ph, "t1", [128, TT], F32),
            "t2": T(kb, ph, "t2", [128, TT], F32),
            "ropeC": T(kb, ph, "rpC", [128, TT], F32),
            "ropeS": T(kb, ph, "rpS", [128, TT], F32),
        }

    def load_rope(self, qt, t):
        kb, d = self.kb, self.d
        kb.dma("sp", qt["ropeC"][:], d["ropeC"][:, t * TT:(t + 1) * TT], writes=[qt["ropeC"].r], own=qt["ropeC"].r)
        kb.dma("sp", qt["ropeS"][:], d["ropeS"][:, t * TT:(t + 1) * TT], writes=[qt["ropeS"].r], own=qt["ropeS"].r)

    def emit_qknorm(self, qt, ps, psr, l, gi, rope, out_ap, out_r):
        kb, c = self.kb, self.c
        g = c["qkg"][:, l, gi:gi + 1]
        kb.op("act", lambda e: e.activation(out=qt["sqk"][:], in_=ps, func=AF.Square), reads=[psr], writes=[qt["sqk"].r])
        kb.op("pe", lambda e: e.matmul(qt["psk"][:, 0:TT], c["bones"][:], qt["sqk"][:], start=True, stop=True),
              reads=[qt["sqk"].r, c["bones"].r], writes=[qt["psk"].r])
        kb.op("act", lambda e: e.activation(out=qt["rk"][:], in_=qt["psk"][:, 0:TT], func=AF.Ln,
                                            bias=c["epsc"][:, 0:1], scale=1.0 / 64),
              reads=[qt["psk"].r, c["epsc"].r], writes=[qt["rk"].r])
        kb.op("act", lambda e: e.activation(out=qt["rk"][:], in_=qt["rk"][:], func=AF.Exp, scale=-0.5),
              reads=[qt["rk"].r], writes=[qt["rk"].r])
        if not rope:
            kb.op("dve", lambda e: e.scalar_tensor_tensor(out=out_ap, in0=ps, scalar=g, in1=qt["rk"][:],
                                                          op0=ALU.mult, op1=ALU.mult),
                  reads=[psr, qt["rk"].r, c["qkg"].r], writes=[out_r])
            return
        kb.op("dve", lambda e: e.scalar_tensor_tensor(out=qt["yf"][:], in0=ps, scalar=g, in1=qt["rk"][:],
                                                      op0=ALU.mult, op1=ALU.mult),
              reads=[psr, qt["rk"].r, c["qkg"].r], writes=[qt["yf"].r])
        kb.op("act", lambda e: e.copy(qt["yb"][:], qt["yf"][:]), reads=[qt["yf"].r], writes=[qt["yb"].r])
        kb.op("pe", lambda e: e.matmul(qt["psk"][:, TT:2 * TT], c["permm"][:], qt["yb"][:], start=True, stop=True),
              reads=[qt["yb"].r, c["permm"].r], writes=[qt["psk"].r])
        kb.op("pool", lambda e: e.tensor_tensor(qt["t1"][:], qt["yf"][:], qt["ropeC"][:], op=ALU.mult),
              reads=[qt["yf"].r, qt["ropeC"].r], writes=[qt["t1"].r])
        kb.op("dve", lambda e: e.tensor_tensor(qt["t2"][:], qt["psk"][:, TT:2 * TT], qt["ropeS"][:], op=ALU.mult),
              reads=[qt["psk"].r, qt["ropeS"].r], writes=[qt["t2"].r])
        kb.op("dve", lambda e: e.tensor_tensor(out_ap, qt["t1"][:], qt["t2"][:], op=ALU.add),
              reads=[qt["t1"].r, qt["t2"].r], writes=[out_r])

    def load_v(self, q, vt, ch0, nch, src_rows):
        for kv in range(2):
            self.kb.dma(q, vt[:, ch0:ch0 + nch, kv, 0:64],
                        src_rows[:, kv * 64:(kv + 1) * 64].rearrange("(ch p) dd -> p ch dd", p=128),
                        writes=[vt.r], own=vt.r)

    def load_w(self, dst, src_ap, nchunk, per=1):
        kb = self.kb
        for n_, k0 in enumerate(range(0, nchunk, per)):
            k1 = min(nchunk, k0 + per)
            kb.dma("sp" if n_ % 2 == 0 else "act", dst[:, k0:k1, :], src_ap[:, k0:k1, :], writes=[dst.r], own=dst.r)

    def phase_ffn(self, l, which, xsrc, xdst, with_kv):
        nc, kb, d, c = self.nc, self.kb, self.d, self.c
        n = 0 if which == 0 else 2
        xs = xsrc.rearrange("(c p) t -> p c t", p=128)
        xd = xdst.rearrange("(c p) t -> p c t", p=128)
        with ExitStack() as ph:
            wg = T(kb, ph, "wg", [128, 8, DFF], BF16)
            wu = T(kb, ph, "wu", [128, 8, DFF], BF16)
            wd = T(kb, ph, "wd", [128, NJ, D], BF16)
            wi_ = self.ffn_which.index(which)
            self.load_w(wg, d["S_wg"][l, which].rearrange("(c p) n -> p c n", p=128), 8, 2)
            self.load_w(wu, d["S_wu"][l, which].rearrange("(c p) n -> p c n", p=128), 8, 2)
            self.load_w(wd, d["S_wd"][l, which].rearrange("(c p) n -> p c n", p=128), NJ, 6)
            self.precast(self.next_key(l, "A" if which == 0 else "C"), after=[wg, wu, wd])
            xb = [T(kb, ph, f"xb{i}", [128, 8, TT], F32) for i in range(2)]
            hb = [T(kb, ph, f"hb{i}", [128, 8, TT], BF16) for i in range(3 if with_kv else 2)]
            act = T(kb, ph, "actb", [128, NJ, TT], BF16)
            sgb = [T(kb, ph, f"sgb{i}", [128, TT], F32) for i in range(2)]
            nt = self.norm_tiles(ph, TT)
            pgu = [T(kb, ph, f"pgu{i}", [128, 2, TT], F32, psum=True) for i in range(2)]
            py = [T(kb, ph, f"py{i}", [128, 512], F32, psum=True) for i in range(2)]
            if with_kv:
                wkv = T(kb, ph, "wkv", [128, 8, 512], BF16)
                self.load_w(wkv, d["S_w_in"][l].rearrange("(c p) n -> p c n", p=128)[:, :, 0:512], 8, 8)
                qt = self.qk_tiles(ph)
                kob = [T(kb, ph, f"kob{i}", [128, TT], BF16) for i in range(2)]
                vob = T(kb, ph, "vob", [128, 2, 2, 128], BF16)
                pkv = T(kb, ph, "pkv", [128, 512], F32, psum=True)
                pv = T(kb, ph, "pvv", [128, 512], F32, psum=True)
                xeb = T(kb, ph, "xeb", [128, 8, 2], F32)

            def load_x(t):
                b = xb[t % 2]
                kb.dma("sp", b[:], xs[:, :, t * TT:(t + 1) * TT], writes=[b.r], own=b.r)

            load_x(0)
            self.emit_norm(nt, xb[0], xb[0].r, TT, l, n, 0, hb[0], hb[0].r)
            for t in range(NTILE):
                col = 1 if t == NLT else 0
                if t + 1 < NTILE:
                    load_x(t + 1)
                x = xb[t % 2]
                h = hb[t % 2]
                for j in range(NJ):
                    p = pgu[j % 2]
                    for (wi, W_) in ((0, wg), (1, wu)):
                        for ch in range(8):
                            kb.op("pe", lambda e, ch=ch, W_=W_, wi=wi, p=p, j=j: e.matmul(
                                p[:, wi, :], W_[:, ch, j * 128:(j + 1) * 128], h[:, ch, :],
                                start=(ch == 0), stop=(ch == 7)), reads=[W_.r, h.r], writes=[p.r])
                    sg_ = sgb[j % 2]
                    kb.op("act", lambda e, p=p, sg_=sg_: e.activation(out=sg_[:], in_=p[:, 0, :], func=AF.Silu),
                          reads=[p.r], writes=[sg_.r])
                    kb.op("dve", lambda e, p=p, sg_=sg_, j=j: e.tensor_tensor(act[:, j, :], sg_[:], p[:, 1, :], op=ALU.mult),
                          reads=[p.r, sg_.r], writes=[act.r])
                for oc in range(8):
                    if oc == 2 and t + 1 < NTILE:
                        xn, hn = xb[(t + 1) % 2], hb[(t + 1) % 2]
                        self.emit_norm(nt, xn, xn.r, TT, l, n, 1 if t + 1 == NLT else 0, hn, hn.r)
                    p = py[oc % 2]
                    for j in range(NJ):
                        kb.op("pe", lambda e, p=p, j=j, oc=oc: e.matmul(
                            p[:, 0:TT], wd[:, j, oc * 128:(oc + 1) * 128], act[:, j, :],
                            start=(j == 0), stop=(j == NJ - 1)), reads=[wd.r, act.r], writes=[p.r])
                    kb.op("dve", lambda e, p=p, oc=oc: e.scalar_tensor_tensor(
                        out=x[:, oc, :], in0=p[:, 0:TT], scalar=self.mod("modG", l, n, oc, col), in1=x[:, oc, :],
                        op0=ALU.mult, op1=ALU.add), reads=[p.r, x.r, c["modG"].r], writes=[x.r])
                kb.dma("sp", xd[:, :, t * TT:(t + 1) * TT], x[:], reads=[x.r], own=x.r)
                if not with_kv:
                    continue
                if t == 0:
                    kb.op("pool", lambda e: e.tensor_copy(xeb[:, :, 0], x[:, :, 0]), reads=[x.r], writes=[xeb.r])
                if t == NLT - 1:
                    kb.op("pool", lambda e: e.tensor_copy(xeb[:, :, 1], x[:, :, TT - 1]), reads=[x.r], writes=[xeb.r])
                    kb.dma("sp", d[f"xe{l % 2}"][:, :], xeb[:].rearrange("p c j -> p (c j)"), reads=[xeb.r], own=xeb.r)
                h2 = hb[2]
                self.emit_norm(nt, x, x.r, TT, l, 1, col, h2, h2.r)
                if col == 0:
                    self.load_rope(qt, t)
                par = l % 2
                for ki, (c0, gi) in enumerate(((0, 1), (256, 3))):
                    for ch in range(8):
                        kb.op("pe", lambda e, ch=ch, c0=c0: e.matmul(
                            pkv[:, 0:TT], wkv[:, ch, c0:c0 + 128], h2[:, ch, :], start=(ch == 0), stop=(ch == 7)),
                            reads=[wkv.r, h2.r], writes=[pkv.r])
                    ko = kob[ki]
                    self.emit_qknorm(qt, pkv[:, 0:TT], pkv.r, l, gi, col == 0, ko[:], ko.r)
                    if ki == 0:
                        kb.dma("sp", d[f"kaT{par}"][:, t * TT:(t + 1) * TT], ko[:], reads=[ko.r], own=ko.r)
                        if t == 0:
                            kb.dma("sp", d[f"kae{par}"][:, 0:128], ko[:, 0:128], reads=[ko.r], own=ko.r)
                        if t == NLT - 1:
                            kb.dma("sp", d[f"kae{par}"][:, 128:256], ko[:, 128:256], reads=[ko.r], own=ko.r)
                    elif col == 0:
                        kb.dma("sp", d[f"kcl{par}"][:, t * TT:(t + 1) * TT], ko[:], reads=[ko.r], own=ko.r)
                    else:
                        kb.dma("sp", d[f"kcc{par}"][:, :], ko[:], reads=[ko.r], own=ko.r)
                for tb in range(2):
                    for vi, c0 in enumerate((128, 384)):
                        for ch in range(8):
                            kb.op("pe", lambda e, ch=ch, c0=c0, tb=tb, vi=vi: e.matmul(
                                pv[:, (tb * 2 + vi) * 128:(tb * 2 + vi + 1) * 128], h2[:, ch, tb * 128:(tb + 1) * 128],
                                wkv[:, ch, c0:c0 + 128], start=(ch == 0), stop=(ch == 7)),
                                reads=[wkv.r, h2.r], writes=[pv.r])
                kb.op("act", lambda e: e.copy(vob[:].rearrange("p a b c -> p (a b c)"), pv[:, :]), reads=[pv.r], writes=[vob.r])
                vdst = [d[f"va{par}"][t * TT:(t + 1) * TT, :],
                        d[f"vcl{par}"][t * TT:(t + 1) * TT, :] if col == 0 else d[f"vcc{par}"][:, :]]
                for vi, dst in enumerate(vdst):
                    kb.dma("sp", dst.rearrange("(tb p) f -> p tb f", p=128), vob[:, :, vi, :], reads=[vob.r], own=vob.r)
                if t == 0:
                    kb.dma("sp", d[f"vae{par}"][0:128, :], vob[:, 0, 0, :], reads=[vob.r], own=vob.r)
                if t == NLT - 1:
                    kb.dma("sp", d[f"vae{par}"][128:256, :], vob[:, 1, 0, :], reads=[vob.r], own=vob.r)
            kb.phase_end()

    def phase_X(self, l):
        nc, kb, d = self.nc, self.kb, self.d
        par = l % 2
        if not hasattr(self, "_xsems"):
            self._xsems = []
            for nm in ("xbig", "xsmall", "xsync"):
                ds = _DSem(nm, kb.es.enter_context(nc.semaphore(nm)))
                kb.dsems.append(ds)
                self._xsems.append(ds)
        xbig, xsmall, xsync = self._xsems
        kb.barrier()
        rg = [[0, 1, 2, 3], [4, 5, 6, 7]]

        def gather(a, b, ds):
            ins = nc.gpsimd.collective_compute("AllGather", ALU.bypass, replica_groups=rg,
                                               ins=[d[f"{a}{par}"][:, :]], outs=[d[f"{b}{par}"][:, :]])
            ds.cnt += 1
            ins.then_inc(ds.sem, 1)

        gather("kcl", "gkc", xbig)
        gather("vcl", "gvc", xbig)
        kb.barrier()
        for a, b in (("kae", "gkae"), ("vae", "gvae"), ("xe", "gxe")):
            gather(a, b, xsmall)
        kb.barrier()
        gather("syn", "gsyn", xsync)
        kb.barrier()

    def attn_tiles(self, ph, npt):
        kb = self.kb
        return {
            "pT": [T(kb, ph, f"pT{i}", [128, 512], BF16) for i in range(npt)],
            "dsb": T(kb, ph, "dsb", [128, 512], F32),
            "rbs": T(kb, ph, "rbs", [64, 512], F32),
            "pbb": T(kb, ph, "pbb", [128, 512], F32, psum=True),
            "cnt": 0,
        }

    def attn_group(self, at, pss, po, chunks, q_ap, q_r, out_ap, out_r, sinkrow=None, prev_epi=None):
        kb, c = self.kb, self.c
        n = len(chunks)
        LA = min(2, len(pss) - 1)
        bufs = []
        for i in range(n):
            bufs.append((pss[at["cnt"] % len(pss)], at["pT"][at["cnt"] % len(at["pT"])]))
            at["cnt"] += 1

        def emit_S(i):
            k_ap, k_r, v_ap, v_r, m_ap = chunks[i]
            ps = bufs[i][0]
            if m_ap is not None:
                kb.op("pe", lambda e: e.matmul(ps[:, :], c["ident"][:], m_ap, start=True, stop=False),
                      reads=[c["ident"].r, self._wm.r], writes=[ps.r])
            kb.op("pe", lambda e: e.matmul(ps[:, :].rearrange("p (j q) -> p j q", j=4), k_ap, q_ap,
                                           start=(m_ap is None), stop=True), reads=[k_r, q_r], writes=[ps.r])

        for i in range(min(LA, n)):
            emit_S(i)
        if prev_epi is not None:
            prev_epi()
        for i in range(n):
            ps, pT = bufs[i]
            kb.op("act", lambda e, ps=ps, pT=pT: e.activation(out=pT[:], in_=ps[:, :], func=AF.Exp, scale=SCALE),
                  reads=[ps.r], writes=[pT.r])
            if i + LA < n:
                emit_S(i + LA)
            v_ap, v_r = chunks[i][2], chunks[i][3]
            kb.op("pe", lambda e, pT=pT, v_ap=v_ap, i=i: e.matmul(po[0:65, :], v_ap, pT[:], start=(i == 0), stop=(i == n - 1)),
                  reads=[v_r, pT.r], writes=[po.r])

        def epilogue():
            dsb, rbs, pbb = at["dsb"], at["rbs"], at["pbb"]
            if sinkrow is not None:
                kb.op("dve", lambda e: e.tensor_tensor(dsb[64:65, :], po[64:65, :], sinkrow, op=ALU.add),
                      reads=[po.r, self._sinkrow.r], writes=[dsb.r])
            else:
                kb.op("dve", lambda e: e.tensor_copy(dsb[64:65, :], po[64:65, :]), reads=[po.r], writes=[dsb.r])
            kb.op("act", lambda e: e.activation(out=dsb[64:65, :], in_=dsb[64:65, :], func=AF.Ln), reads=[dsb.r], writes=[dsb.r])
            kb.op("act", lambda e: e.activation(out=dsb[64:65, :], in_=dsb[64:65, :], func=AF.Exp, scale=-1.0), reads=[dsb.r], writes=[dsb.r])
            kb.op("pe", lambda e: e.matmul(pbb[0:64, :], c["onesf"][64:65, 0:64], dsb[64:65, :], start=True, stop=True),
                  reads=[dsb.r, c["onesf"].r], writes=[pbb.r])
            kb.op("act", lambda e: e.copy(rbs[:, :], pbb[0:64, :]), reads=[pbb.r], writes=[rbs.r])
            kb.op("dve", lambda e: e.tensor_tensor(out_ap, po[0:64, :].rearrange("p (j q) -> p j q", j=4),
                                                   rbs[:, :].rearrange("p (j q) -> p j q", j=4), op=ALU.mult),
                  reads=[po.r, rbs.r], writes=[out_r])

        return epilogue

    def attn_pair(self, st, psp, pTp, po, kcall, vflat, nkc, qc, qb, ocT, prev_epi=None):
        kb, c = self.kb, self.c
        n = nkc
        bufs = []
        for i in range(n):
            bufs.append((psp[st["cnt"] % len(psp)], pTp[st["cnt"] % len(pTp)]))
            st["cnt"] += 1

        def emit_S(i):
            ps = bufs[i][0]
            for kv in range(2):
                ks = slice(kv * 64, (kv + 1) * 64)
                kb.op("pe", lambda e, kv=kv, ks=ks: e.matmul(
                    ps[:, kv, :].rearrange("p (j q) -> p j q", j=4), kcall[ks, i * 128:(i + 1) * 128],
                    qc[ks, :, qb * 128:(qb + 1) * 128], start=True, stop=True), reads=[kcall.r, qc.r], writes=[ps.r])

        emit_S(0)
        if prev_epi is not None:
            prev_epi()
        for i in range(n):
            ps, pT = bufs[i]
            kb.op("act", lambda e, ps=ps, pT=pT: e.activation(
                out=pT[:].rearrange("p a b -> p (a b)"), in_=ps[:].rearrange("p a b -> p (a b)"), func=AF.Exp, scale=SCALE),
                reads=[ps.r], writes=[pT.r])
            if i + 1 < n:
                emit_S(i + 1)
            for kv in range(2):
                o0 = i * 130 + kv * 65
                kb.op("pe", lambda e, pT=pT, kv=kv, o0=o0, i=i: e.matmul(
                    po[:, kv, :], vflat[:, o0:o0 + 128], pT[:, kv, :], start=(i == 0), stop=(i == n - 1)),
                    reads=[st["vr"], pT.r], writes=[po.r])

        def epilogue():
            dsb, rbs, pbb = st["dsb"], st["rbs"], st["pbb"]
            kb.op("dve", lambda e: e.tensor_copy(dsb[64:65, :, :], po[64:65, :, :]), reads=[po.r], writes=[dsb.r])
            dflat = dsb[64:65, :, :].rearrange("p a b -> p (a b)")
            kb.op("act", lambda e: e.activation(out=dflat, in_=dflat, func=AF.Ln), reads=[dsb.r], writes=[dsb.r])
            kb.op("act", lambda e: e.activation(out=dflat, in_=dflat, func=AF.Exp, scale=-1.0), reads=[dsb.r], writes=[dsb.r])
            for kv in range(2):
                kb.op("pe", lambda e, kv=kv: e.matmul(pbb[0:64, :], c["onesf"][64:65, 0:64], dsb[64:65, kv, :], start=True, stop=True),
                      reads=[dsb.r, c["onesf"].r], writes=[pbb.r])
                kb.op("act", lambda e: e.copy(rbs[:, :], pbb[0:64, :]), reads=[pbb.r], writes=[rbs.r])
                kb.op("dve", lambda e, kv=kv: e.tensor_tensor(
                    ocT[:, kv * 4:(kv + 1) * 4, qb * 128:(qb + 1) * 128], po[0:64, kv, :].rearrange("p (j q) -> p j q", j=4),
                    rbs[:, :].rearrange("p (j q) -> p j q", j=4), op=ALU.mult), reads=[po.r, rbs.r], writes=[ocT.r])

        return epilogue

    def phase_B1(self, l, xsrc):
        nc, kb, d, c = self.nc, self.kb, self.d, self.c
        xs = xsrc.rearrange("(c p) t -> p c t", p=128)
        W2 = TT + 2
        with ExitStack() as ph:
            win = T(kb, ph, "win", [128, 8, 2560], BF16)
            self.load_w(win, d["S_w_in"][l].rearrange("(c p) n -> p c n", p=128)[:, :, 512:3072], 8, 2)
            wgt = T(kb, ph, "wgt", [128, 8, 3072], BF16)
            self.load_w(wgt, d["S_w_gate"][l].rearrange("(c p) n -> p c n", p=128), 8, 2)
            wpa = T(kb, ph, "wpa", [64, 8, D], BF16)
            self.load_w(wpa, d["S_w_pa"][l].rearrange("(h dd) n -> dd h n", dd=64), 8, 4)
            wpb = T(kb, ph, "wpb", [128, 4, D], BF16)
            self.load_w(wpb, d["S_w_pb"][l].rearrange("(c p) n -> p c n", p=128), 4, 4)
            self.precast(self.next_key(l, "B1"), after=[win, wgt, wpa, wpb])
            bgt = T(kb, ph, "bgt", [128, 24], F32)
            cw = T(kb, ph, "cw", [128, 4, 3], F32)
            wm = self._wm = T(kb, ph, "wm", [128, 4, 512], BF16)
            em = T(kb, ph, "em", [128, 2], F32)
            xedge = T(kb, ph, "xedge", [128, 8, 2], F32)
            sk = T(kb, ph, "sk", [128, 8], F32)
            ske = T(kb, ph, "ske", [128, 8], F32)
            sinkrow = self._sinkrow = T(kb, ph, "sinkrow", [128, 2, 512], F32)
            kactx = T(kb, ph, "kactx", [128, CTX], BF16)
            vactx = T(kb, ph, "vactx", [128, 2, 2, 65], BF16)
            kb.dma("sp", bgt[:], d["b_gate"][l], writes=[bgt.r], own=bgt.r)
            kb.dma("sp", cw[:], d["conv_w"][l], writes=[cw.r], own=cw.r)
            kb.dma("pool", wm[:], d["wmask"][:, :, :], writes=[wm.r], own=wm.r)
            kb.dma("sp", em[:], d["emask"][:, :], writes=[em.r], own=em.r)
            par = l % 2
            kaT, va_d = d[f"kaT{par}"], d[f"va{par}"]
            gkae, gvae, gxe = d[f"gkae{par}"], d[f"gvae{par}"], d[f"gxe{par}"]
            sel = T(kb, ph, "sel", [128, 8], F32)
            cand = T(kb, ph, "cand", [128, 4, 4, 128], BF16)
            cx = T(kb, ph, "cx", [128, 4, 16], F32)
            halo = T(kb, ph, "halo", [128, 4, 128], BF16)
            hacc = T(kb, ph, "hacc", [128, 128], F32)
            gk3 = gkae.rearrange("(i p) c -> p i c", p=128)
            gv3 = gvae.rearrange("(i t) f -> t i f", i=4)

            def prep_halos():
                kb.dma("sp", sel[:], d["sel"][:, :], writes=[sel.r], own=sel.r)
                kb.dma("sp", cand[:, 0, :, :], gk3[:, :, 128:256], writes=[cand.r], own=cand.r)
                kb.dma("sp", cand[:, 1, :, :], gk3[:, :, 0:128], writes=[cand.r], own=cand.r)
                kb.dma("sp", cand[:, 2, :, :], gv3[128:256, :, :], writes=[cand.r], own=cand.r)
                kb.dma("sp", cand[:, 3, :, :], gv3[0:128, :, :], writes=[cand.r], own=cand.r)
                kb.dma("sp", cx[:], gxe.rearrange("(i p) f -> p i f", p=128), writes=[cx.r], own=cx.r)
                for w in range(4):
                    so = 0 if w % 2 == 0 else 4
                    kb.op("dve", lambda e, w=w, so=so: e.tensor_scalar(hacc[:], cand[:, w, 0, :], sel[:, so:so + 1], None, op0=ALU.mult),
                          reads=[cand.r, sel.r], writes=[hacc.r])
                    for i in range(1, 4):
                        o_ap = halo[:, w, :] if i == 3 else hacc[:]
                        kb.op("dve", lambda e, w=w, so=so, i=i, o_ap=o_ap: e.scalar_tensor_tensor(
                            out=o_ap, in0=cand[:, w, i, :], scalar=sel[:, so + i:so + i + 1], in1=hacc[:], op0=ALU.mult, op1=ALU.add),
                            reads=[cand.r, sel.r, hacc.r], writes=[halo.r if i == 3 else hacc.r])
                for j_out, so, j_in in ((0, 0, 1), (1, 4, 0)):
                    kb.op("dve", lambda e, j_out=j_out, so=so, j_in=j_in: e.tensor_scalar(
                        xedge[:, :, j_out], cx[:, 0, j_in:16:2], sel[:, so:so + 1], None, op0=ALU.mult),
                        reads=[cx.r, sel.r], writes=[xedge.r])
                    for i in range(1, 4):
                        kb.op("dve", lambda e, j_out=j_out, so=so, j_in=j_in, i=i: e.scalar_tensor_tensor(
                            out=xedge[:, :, j_out], in0=cx[:, i, j_in:16:2], scalar=sel[:, so + i:so + i + 1], in1=xedge[:, :, j_out],
                            op0=ALU.mult, op1=ALU.add), reads=[cx.r, sel.r, xedge.r], writes=[xedge.r])

            kb.dma("sp", sk[64:65, :], d["sink"][l], writes=[sk.r], own=sk.r)
            kb.op("act", lambda e: e.activation(out=ske[64:65, :], in_=sk[64:65, :], func=AF.Exp), reads=[sk.r], writes=[ske.r])
            for hh in range(8):
                kb.op("dve", lambda e, hh=hh: e.tensor_scalar(
                    sinkrow[64:65, hh // 4, (hh % 4) * 128:(hh % 4 + 1) * 128], c["onesf"][64:65, 0:128],
                    ske[64:65, hh:hh + 1], None, op0=ALU.mult), reads=[ske.r, c["onesf"].r], writes=[sinkrow.r])
            kb.dma("sp", kactx[:], kaT[:, TOK:TOK + CTX], writes=[kactx.r], own=kactx.r)
            kb.op("dve", lambda e: e.memset(vactx[:].rearrange("p a b c -> p (a b c)"), 1.0), writes=[vactx.r])
            self.load_v("sp", vactx, 0, 2, va_d[TOK:TOK + CTX, :])

            xh = [T(kb, ph, f"xh{i}", [128, 8, W2], F32) for i in range(2)]
            h = T(kb, ph, "hB", [128, 8, W2], BF16)
            nt = self.norm_tiles(ph, W2)
            qt = self.qk_tiles(ph)
            qaT = T(kb, ph, "qaT", [128, 4, TT], BF16)
            qcT = [T(kb, ph, f"qcT{i}", [128, 4, TT], BF16) for i in range(2)]
            kwin = [T(kb, ph, f"kwin{i}", [128, 512], BF16) for i in range(2)]
            vwin = [T(kb, ph, f"vwin{i}", [128, 4, 2, 65], BF16) for i in range(2)]
            for v in vwin:
                kb.op("dve", lambda e, v=v: e.memset(v[:].rearrange("p a b c -> p (a b c)"), 1.0), writes=[v.r])
            csb = T(kb, ph, "csb", [128, W2], F32)
            cu = T(kb, ph, "cu", [128, W2], F32)
            yv = T(kb, ph, "yv", [128, TT], F32)
            ob = T(kb, ph, "ob", [128, 4, TT], BF16)
            oaT = T(kb, ph, "oaT", [64, 8, TT], BF16)
            gA = [T(kb, ph, f"gA{i}", [128, TT], F32) for i in range(2)]
            gB = [T(kb, ph, f"gB{i}", [128, TT], F32) for i in range(2)]
            gC = [T(kb, ph, f"gC{i}", [128, TT], F32) for i in range(2)]
            a1 = [T(kb, ph, f"a1{i}", [128, TT], F32) for i in range(2)]
            a2 = [T(kb, ph, f"a2{i}", [128, TT], F32) for i in range(2)]
            accb = [T(kb, ph, f"accb{i}", [128, TT], F32) for i in range(2)]
            at = self.attn_tiles(ph, 3)
            pp = [T(kb, ph, f"ppj{i}", [128, 512], F32, psum=True) for i in range(2)]
            pss = [T(kb, ph, f"pss{i}", [128, 512], F32, psum=True) for i in range(2)]
            po = T(kb, ph, "po", [128, 512], F32, psum=True)
            accd = d["ACC"].rearrange("(c p) t -> p c t", p=128)
            gcd = d["GC"].rearrange("(c p) t -> p c t", p=128)
            qcd = d["QC"].rearrange("(c p) t -> p c t", p=128)

            order = list(range(1, NLT - 1)) + [0, NLT - 1, NLT]
            slot = {t: si % 2 for si, t in enumerate(order)}

            def load_x(t):
                b = xh[slot[t]]
                if t == NLT:
                    kb.dma("sp", b[:, :, 1:TT + 1], xs[:, :, TOK:TOK + CTX], writes=[b.r], own=b.r)
                elif t == 0:
                    kb.dma("sp", b[:, :, 1:W2], xs[:, :, 0:TT + 1], writes=[b.r], own=b.r)
                    kb.op("pool", lambda e: e.tensor_copy(b[:, :, 0], xedge[:, :, 0]), reads=[xedge.r], writes=[b.r])
                elif t == NLT - 1:
                    kb.dma("sp", b[:, :, 0:TT + 1], xs[:, :, t * TT - 1:(t + 1) * TT], writes=[b.r], own=b.r)
                    kb.op("pool", lambda e: e.tensor_copy(b[:, :, TT + 1], xedge[:, :, 1]), reads=[xedge.r], writes=[b.r])
                else:
                    kb.dma("sp", b[:], xs[:, :, t * TT - 1:(t + 1) * TT + 1], writes=[b.r], own=b.r)

            def load_kv(t):
                if t >= NLT:
                    return
                kw, vw = kwin[slot[t]], vwin[slot[t]]
                lo, hi = t * TT - 128, t * TT + 384
                if t == 0:
                    kb.dma("sp", kw[:, 128:512], kaT[:, 0:hi], writes=[kw.r], own=kw.r)
                    self.load_v("sp", vw, 1, 3, va_d[0:hi, :])
                    kb.op("pool", lambda e: e.tensor_copy(kw[:, 0:128], halo[:, 0, :]), reads=[halo.r], writes=[kw.r])
                    kb.op("pool", lambda e: e.tensor_copy(vw[:, 0, :, 0:64], halo[:, 2, :].rearrange("p (kv dd) -> p kv dd", kv=2)),
                          reads=[halo.r], writes=[vw.r])
                elif t == NLT - 1:
                    kb.dma("sp", kw[:, 0:384], kaT[:, lo:TOK], writes=[kw.r], own=kw.r)
                    self.load_v("sp", vw, 0, 3, va_d[lo:TOK, :])
                    kb.op("pool", lambda e: e.tensor_copy(kw[:, 384:512], halo[:, 1, :]), reads=[halo.r], writes=[kw.r])
                    kb.op("pool", lambda e: e.tensor_copy(vw[:, 3, :, 0:64], halo[:, 3, :].rearrange("p (kv dd) -> p kv dd", kv=2)),
                          reads=[halo.r], writes=[vw.r])
                else:
                    kb.dma("sp", kw[:], kaT[:, lo:hi], writes=[kw.r], own=kw.r)
                    self.load_v("sp", vw, 0, 4, va_d[lo:hi, :])

            def proj(pt, half, wt, c0, rhs_lo, rhs_hi, hh=h):
                Wd = rhs_hi - rhs_lo
                o0 = half * 256
                for ch in range(8):
                    kb.op("pe", lambda e, ch=ch: e.matmul(pt[:, o0:o0 + Wd], wt[:, ch, c0:c0 + 128], hh[:, ch, rhs_lo:rhs_hi],
                                                         start=(ch == 0), stop=(ch == 7)),
                          reads=[wt.r, hh.r], writes=[pt.r])

            load_x(order[0])
            load_kv(order[0])
            for si, t in enumerate(order):
                col = 1 if t == NLT else 0
                lat = col == 0
                if si + 1 < NTILE:
                    if order[si + 1] == 0:
                        prep_halos()
                    load_x(order[si + 1])
                    load_kv(order[si + 1])
                x = xh[slot[t]]
                self.emit_norm(nt, x, x.r, W2, l, 1, col, h, h.r)
                if lat:
                    self.load_rope(qt, t)
                qc = qcT[slot[t]]
                for qi in range(8):
                    pt = pp[qi % 2]
                    proj(pt, 0, win, qi * 128, 1, TT + 1)
                    if qi < 4:
                        self.emit_qknorm(qt, pt[:, 0:TT], pt.r, l, 0, lat, qaT[:, qi, :], qaT.r)
                    else:
                        self.emit_qknorm(qt, pt[:, 0:TT], pt.r, l, 2, lat, qc[:, qi - 4, :], qc.r)
                kb.dma("sp", qcd[:, :, t * TT:(t + 1) * TT], qc[:], reads=[qc.r], own=qc.r)
                for cc in range(4):
                    proj(pp[0], 0, win, 1536 + cc * 128, 0, W2)
                    proj(pp[1], 0, win, 2048 + cc * 128, 0, W2)
                    kb.op("act", lambda e: e.copy(csb[:], pp[0][:, 0:W2]), reads=[pp[0].r], writes=[csb.r])
                    kb.op("dve", lambda e: e.tensor_tensor(cu[:], csb[:], pp[1][:, 0:W2], op=ALU.mult),
                          reads=[csb.r, pp[1].r], writes=[cu.r])
                    if not lat:
                        kb.op("dve", lambda e: e.memset(cu[:, 0:1], 0.0), writes=[cu.r])
                        kb.op("dve", lambda e: e.memset(cu[:, TT + 1:W2], 0.0), writes=[cu.r])
                    elif t == 0:
                        kb.op("dve", lambda e: e.tensor_scalar(cu[:, 0:1], cu[:, 0:1], em[:, 0:1], None, op0=ALU.mult),
                              reads=[em.r, cu.r], writes=[cu.r])
                    elif t == NLT - 1:
                        kb.op("dve", lambda e: e.tensor_scalar(cu[:, TT + 1:W2], cu[:, TT + 1:W2], em[:, 1:2], None, op0=ALU.mult),
                              reads=[em.r, cu.r], writes=[cu.r])
                    kb.op("dve", lambda e, cc=cc: e.tensor_scalar(yv[:], cu[:, 0:TT], cw[:, cc, 0:1], None, op0=ALU.mult),
                          reads=[cu.r, cw.r], writes=[yv.r])
                    for k in (1, 2):
                        kb.op("dve", lambda e, cc=cc, k=k: e.scalar_tensor_tensor(
                            out=yv[:], in0=cu[:, k:k + TT], scalar=cw[:, cc, k:k + 1], in1=yv[:], op0=ALU.mult, op1=ALU.add),
                            reads=[cu.r, cw.r, yv.r], writes=[yv.r])
                    proj(pp[0], 0, win, 1024 + cc * 128, 1, TT + 1)
                    kb.op("dve", lambda e, cc=cc: e.tensor_tensor(ob[:, cc, :], yv[:], pp[0][:, 0:TT], op=ALU.mult),
                          reads=[yv.r, pp[0].r], writes=[ob.r])
                kw, vw = kwin[slot[t]], vwin[slot[t]]
                for qb in range(2):
                    for kv in range(2):
                        ks = slice(kv * 64, (kv + 1) * 64)
                        chunks = [(kactx[ks, ci * 128:(ci + 1) * 128], kactx.r, vactx[:, ci, kv, :], vactx.r, None) for ci in range(2)]
                        if lat:
                            for wi in (qb, qb + 1, qb + 2):
                                m_ap = None
                                if wi == qb:
                                    m_ap = wm[:, 2 if (t == 0 and qb == 0) else 0, :]
                                elif wi == qb + 2:
                                    m_ap = wm[:, 3 if (t == NLT - 1 and qb == 1) else 1, :]
                                chunks.append((kw[ks, wi * 128:(wi + 1) * 128], kw.r, vw[:, wi, kv, :], vw.r, m_ap))
                        epi = self.attn_group(at, pss, po, chunks, qaT[ks, :, qb * 128:(qb + 1) * 128], qaT.r,
                                              oaT[:, kv * 4:(kv + 1) * 4, qb * 128:(qb + 1) * 128], oaT.r,
                                              sinkrow=sinkrow[64:65, kv, :])
                        epi()
                pool6 = [pp[0], pp[1], pss[0], pss[1], po, at["pbb"]]
                pc_ = [0]

                def nxt():
                    pc_[0] += 1
                    return pool6[pc_[0] % 6]

                for oc in range(8):
                    i2 = oc % 2
                    for gi_, gbuf in ((0, gA[i2]), (1, gB[i2]), (2, gC[i2])):
                        pt = nxt()
                        proj(pt, 0, wgt, (gi_ * 8 + oc) * 128, 1, TT + 1)
                        kb.op("act", lambda e, pt=pt, gbuf=gbuf, gi_=gi_, oc=oc: e.activation(
                            out=gbuf[:], in_=pt[:, 0:TT], func=AF.Sigmoid, bias=bgt[:, gi_ * 8 + oc: gi_ * 8 + oc + 1], scale=1.0),
                            reads=[pt.r, bgt.r], writes=[gbuf.r])
                    kb.dma("sp", gcd[:, oc, t * TT:(t + 1) * TT], gC[i2][:], reads=[gC[i2].r], own=gC[i2].r)
                    pa_ = nxt()
                    for hd in range(8):
                        kb.op("pe", lambda e, hd=hd, oc=oc, pa_=pa_: e.matmul(pa_[:, 0:TT], wpa[:, hd, oc * 128:(oc + 1) * 128], oaT[:, hd, :],
                                                                            start=(hd == 0), stop=(hd == 7)),
                              reads=[wpa.r, oaT.r], writes=[pa_.r])
                    pb_ = nxt()
                    for cc in range(4):
                        kb.op("pe", lambda e, cc=cc, oc=oc, pb_=pb_: e.matmul(pb_[:, 0:TT], wpb[:, cc, oc * 128:(oc + 1) * 128], ob[:, cc, :],
                                                                            start=(cc == 0), stop=(cc == 3)),
                              reads=[wpb.r, ob.r], writes=[pb_.r])
                    kb.op("dve", lambda e, i2=i2, pa_=pa_: e.tensor_tensor(a1[i2][:], pa_[:, 0:TT], gA[i2][:], op=ALU.mult),
                          reads=[pa_.r, gA[i2].r], writes=[a1[i2].r])
                    kb.op("dve", lambda e, i2=i2, pb_=pb_: e.tensor_tensor(a2[i2][:], pb_[:, 0:TT], gB[i2][:], op=ALU.mult),
                          reads=[pb_.r, gB[i2].r], writes=[a2[i2].r])
                    kb.op("pool", lambda e, i2=i2: e.tensor_tensor(accb[i2][:], a1[i2][:], a2[i2][:], op=ALU.add),
                          reads=[a1[i2].r, a2[i2].r], writes=[accb[i2].r])
                    kb.dma("sp", accd[:, oc, t * TT:(t + 1) * TT], accb[i2][:], reads=[accb[i2].r], own=accb[i2].r)
            kb.phase_end()

    def phase_B2(self, l, xsrc, xdst):
        nc, kb, d, c = self.nc, self.kb, self.d, self.c
        xs = xsrc.rearrange("(c p) t -> p c t", p=128)
        xd = xdst.rearrange("(c p) t -> p c t", p=128)
        accd = d["ACC"].rearrange("(c p) t -> p c t", p=128)
        gcd = d["GC"].rearrange("(c p) t -> p c t", p=128)
        qcd = d["QC"].rearrange("(c p) t -> p c t", p=128)
        with ExitStack() as ph:
            kcall = T(kb, ph, "kcall", [128, KALL], BF16)
            vcall = T(kb, ph, "vcall", [128, NKC + 1, 2, 65], BF16)
            kb.op("dve", lambda e: e.memset(vcall[:].rearrange("p a b c -> p (a b c)"), 1.0), writes=[vcall.r])
            par = l % 2
            gkc, gvc = d[f"gkc{par}"], d[f"gvc{par}"]
            kb.dma("sp", kcall[:, 0:CTX], d[f"kcc{par}"][:, :], writes=[kcall.r], own=kcall.r)
            self.load_v("act", vcall, 0, 2, d[f"vcc{par}"][:, :])
            for i in range(4):
                for hh in range(2):
                    c0 = hh * 2048
                    kb.dma("sp", kcall[:, CTX + i * TOK + c0: CTX + i * TOK + c0 + 2048], gkc[i * 128:(i + 1) * 128, c0:c0 + 2048],
                           writes=[kcall.r], own=kcall.r)
                for k0 in range(0, 32, 4):
                    self.load_v("act", vcall, 2 + i * 32 + k0, 4, gvc[i * TOK + k0 * 128: i * TOK + (k0 + 4) * 128, :])
            wpc = T(kb, ph, "wpc", [64, 8, D], BF16)
            self.load_w(wpc, d["S_w_pc"][l].rearrange("(h dd) n -> dd h n", dd=64), 8, 4)
            wo = T(kb, ph, "wo", [128, 8, D], BF16)
            self.load_w(wo, d["S_w_o"][l].rearrange("(c p) n -> p c n", p=128), 8, 4)
            self.precast(self.next_key(l, "B2"), after=[kcall, vcall, wpc, wo])
            xb = [T(kb, ph, f"xb2{i}", [128, 8, TT], F32) for i in range(2)]
            qcb = [T(kb, ph, f"qcb{i}", [128, 4, TT], BF16) for i in range(2)]
            acb = [T(kb, ph, f"acb{i}", [128, 8, TT], F32) for i in range(2)]
            gcb = [T(kb, ph, f"gcb{i}", [128, 8, TT], F32) for i in range(2)]
            ocT = T(kb, ph, "ocT", [64, 8, TT], BF16)
            m1 = [T(kb, ph, f"m1{i}", [128, TT], F32) for i in range(2)]
            mrg = T(kb, ph, "mrg", [128, 8, TT], BF16)
            st = {"cnt": 0, "vr": vcall.r,
                  "dsb": T(kb, ph, "dsb2", [128, 2, 512], F32), "rbs": T(kb, ph, "rbs2", [64, 512], F32),
                  "pbb": T(kb, ph, "pbb2", [128, 512], F32, psum=True)}
            vflat = vcall[:].rearrange("p a b c -> p (a b c)")
            pTp = [T(kb, ph, f"pTp{i}", [128, 2, 512], BF16) for i in range(3)]
            psp = [T(kb, ph, f"psp{i}", [128, 2, 512], F32, psum=True) for i in range(2)]
            po = T(kb, ph, "po2", [128, 2, 512], F32, psum=True)
            pp = [psp[0][:, 0, :], psp[1][:, 0, :]]
            ppr = [psp[0].r, psp[1].r]

            def load_t(t):
                i = t % 2
                kb.dma("sp", xb[i][:], xs[:, :, t * TT:(t + 1) * TT], writes=[xb[i].r], own=xb[i].r)
                kb.dma("sp", qcb[i][:], qcd[:, :, t * TT:(t + 1) * TT], writes=[qcb[i].r], own=qcb[i].r)
                kb.dma("sp", acb[i][:], accd[:, :, t * TT:(t + 1) * TT], writes=[acb[i].r], own=acb[i].r)
                kb.dma("sp", gcb[i][:], gcd[:, :, t * TT:(t + 1) * TT], writes=[gcb[i].r], own=gcb[i].r)

            load_t(0)
            g = 0
            pend = None
            for t in range(NTILE):
                col = 1 if t == NLT else 0
                if t + 1 < NTILE:
                    load_t(t + 1)
                i = t % 2
                x, qc, ac, gc = xb[i], qcb[i], acb[i], gcb[i]
                nkc = NKC if col == 0 else 2
                for qb in range(2):
                    pend = self.attn_pair(st, psp, pTp, po, kcall, vflat, nkc, qc, qb, ocT, prev_epi=pend)
                pend()
                pend = None
                for oc in range(8):
                    pt, ptr = pp[oc % 2], ppr[oc % 2]
                    for hd in range(8):
                        kb.op("pe", lambda e, hd=hd, oc=oc, pt=pt: e.matmul(pt[:, 0:TT], wpc[:, hd, oc * 128:(oc + 1) * 128], ocT[:, hd, :],
                                                                          start=(hd == 0), stop=(hd == 7)),
                              reads=[wpc.r, ocT.r], writes=[ptr])
                    mm = m1[oc % 2]
                    kb.op("dve", lambda e, pt=pt, mm=mm, oc=oc: e.tensor_tensor(mm[:], pt[:, 0:TT], gc[:, oc, :], op=ALU.mult),
                          reads=[ptr, gc.r], writes=[mm.r])
                    kb.op("pool", lambda e, mm=mm, oc=oc: e.tensor_tensor(mrg[:, oc, :], mm[:], ac[:, oc, :], op=ALU.add),
                          reads=[mm.r, ac.r], writes=[mrg.r])
                for oc in range(8):
                    pt, ptr = pp[oc % 2], ppr[oc % 2]
                    for ch in range(8):
                        kb.op("pe", lambda e, ch=ch, oc=oc, pt=pt: e.matmul(pt[:, 0:TT], wo[:, ch, oc * 128:(oc + 1) * 128], mrg[:, ch, :],
                                                                          start=(ch == 0), stop=(ch == 7)),
                              reads=[wo.r, mrg.r], writes=[ptr])
                    kb.op("dve", lambda e, pt=pt, oc=oc: e.scalar_tensor_tensor(
                        out=x[:, oc, :], in0=pt[:, 0:TT], scalar=self.mod("modG", l, 1, oc, col), in1=x[:, oc, :],
                        op0=ALU.mult, op1=ALU.add), reads=[ptr, x.r, c["modG"].r], writes=[x.r])
                kb.dma("sp", xd[:, :, t * TT:(t + 1) * TT], x[:], reads=[x.r], own=x.r)
            kb.phase_end()


def _rope_tables_np(pos0):
    pos = np.arange(pos0, pos0 + TOK)
    row = (pos // 64).astype(np.float32)
    colp = (pos % 64).astype(np.float32)
    half = 32
    inv = (np.float32(10000.0) ** (-np.arange(0, half, 2, dtype=np.float32) / np.float32(half))).astype(np.float32)
    C = np.zeros((128, TOK), np.float32)
    S = np.zeros((128, TOK), np.float32)
    for p in range(128):
        dd = p % 64
        blk, within = dd // 32, dd % 32
        i = within % 16
        first = within < 16
        ang = ((row if blk == 0 else colp) * inv[i]).astype(np.float32)
        C[p] = np.cos(ang)
        S[p] = -np.sin(ang) if first else np.sin(ang)
    return C, S


def _perm_matrix():
    Pm = np.zeros((128, 128), np.float32)
    for m in range(128):
        dd = m % 64
        within = dd % 32
        partner = m + 16 if within < 16 else m - 16
        Pm[partner, m] = 1.0
    return Pm


_QPERM = np.concatenate([np.concatenate([np.arange(j * 64, (j + 1) * 64), np.arange((4 + j) * 64, (5 + j) * 64)])
                         for j in range(4)])


def _prep_common(inp):
    f = lambda a: np.ascontiguousarray(np.asarray(a, dtype=np.float32))
    w_in = f(inp["w_in"]).copy()
    w_in[:, :, 512:1024] = w_in[:, :, 512 + _QPERM]
    w_in[:, :, 1024:1536] = w_in[:, :, 1024 + _QPERM]
    com = {
        "w_ada": f(inp["w_ada"]),
        "b_ada": f(np.asarray(inp["b_ada"]).reshape(DEPTH, 72, 128).transpose(0, 2, 1)),
        "norm_g": f(np.asarray(inp["norm_g"]).reshape(DEPTH, 3, 8, 128).transpose(0, 3, 1, 2)),
        "ident": np.eye(128, dtype=np.float32),
        "permm": _perm_matrix(),
        "qk_g": f(np.tile(np.asarray(inp["qk_g"]), (1, 1, 2)).transpose(0, 2, 1)),
        "w_in": w_in,
        "ffn_w_gate": f(inp["ffn_w_gate"]), "ffn_w_up": f(inp["ffn_w_up"]), "ffn_w_down": f(inp["ffn_w_down"]),
        "sink_a": f(np.asarray(inp["sink_a"]).reshape(DEPTH, 1, 8)),
        "conv_w": f(np.asarray(inp["conv_w"]).reshape(DEPTH, 3, 4, 128).transpose(0, 3, 2, 1)),
        "w_pa": f(inp["w_pa"]), "w_pb": f(inp["w_pb"]), "w_pc": f(inp["w_pc"]),
        "w_gate": f(inp["w_gate"]),
        "b_gate": f(np.asarray(inp["b_gate"]).reshape(DEPTH, 24, 128).transpose(0, 2, 1)),
        "w_o": f(inp["w_o"]),
    }
    return com


def _prep_core(inp, r):
    b, q = r // 4, r % 4
    x = np.asarray(inp["x"], dtype=np.float32)
    ctx = np.asarray(inp["ctx"], dtype=np.float32)
    xT = np.concatenate([x[b, q * TOK:(q + 1) * TOK].T, ctx[b].T], axis=1)
    cvec = np.stack([np.asarray(inp["c"], np.float32)[b], np.asarray(inp["c_ctx"], np.float32)], axis=1)
    cvec = cvec.reshape(8, 128, 2).transpose(1, 0, 2)
    C, S = _rope_tables_np(q * TOK)
    kk = np.arange(128)[:, None]
    qq = np.arange(128)[None, :]
    mprev = np.where(kk >= qq, 0.0, NEG).astype(np.float32)
    mnext = np.where(kk <= qq, 0.0, NEG).astype(np.float32)
    allneg = np.full((128, 128), NEG, np.float32)
    wm = np.stack([np.tile(m, (1, 4)) for m in (mprev, mnext, mprev if q > 0 else allneg, mnext if q < 3 else allneg)], axis=1)
    emask = np.zeros((128, 2), np.float32)
    emask[:, 0] = 1.0 if q > 0 else 0.0
    emask[:, 1] = 1.0 if q < 3 else 0.0
    sel = np.zeros((128, 8), np.float32)
    if q > 0:
        sel[:, q - 1] = 1.0
    if q < 3:
        sel[:, 4 + q + 1] = 1.0
    return {"xin": np.ascontiguousarray(xT), "cvec": np.ascontiguousarray(cvec), "ropeC": C, "ropeS": S,
            "wmask": np.ascontiguousarray(wm), "emask": emask, "sel": sel}


_PROGS = {}


def _get_prog(phases, fused=False, nlayers=1):
    key = (tuple(phases), fused, nlayers)
    if key not in _PROGS:
        _PROGS[key] = Prog(set(phases), fused, nlayers).build()
    return _PROGS[key]


def _run(prog, maps):
    maps = [{k: m[k] for k in prog.in_names} for m in maps]
    res = run_bass_kernel_spmd(prog.nc, maps, core_ids=list(range(NCORE)))
    return res.results


def _layer_slice(com, l, which=None):
    out = {}
    for k, v in com.items():
        if k in ("ident", "permm"):
            out[k] = v
        elif k.startswith("ffn_") and which is not None:
            out[k] = np.ascontiguousarray(v[l:l + 1, which:which + 1])
        else:
            out[k] = np.ascontiguousarray(v[l:l + 1])
    return out


def kernel(**inp):
    com = _prep_common(inp)
    prog = _get_prog(["A", "X", "B1", "B2", "C"], fused=True, nlayers=DEPTH)
    maps = []
    for r in range(NCORE):
        m = dict(com, **_prep_core(inp, r))
        maps.append(m)
    res = _run(prog, maps)
    out = np.zeros((2, SEQ, D), np.float32)
    for r in range(NCORE):
        b, q = r // 4, r % 4
        out[b, q * TOK:(q + 1) * TOK] = res[r]["xw"][:, :TOK].T
    return out
```

```python
import numpy as np
import ml_dtypes
from contextlib import ExitStack
import concourse.bass as bass
import concourse.mybir as mybir
from concourse.bass_utils import run_bass_kernel_spmd

F32 = mybir.dt.float32
BF16 = mybir.dt.bfloat16
AF = mybir.ActivationFunctionType
ALU = mybir.AluOpType

D = 1024
DFF = 2816
NJ = DFF // 128
DEPTH = 4
SEQ = 16384
NCORE = 8
TOK = 4096
CTX = 256
TT = 256
NLT = TOK // TT
NTILE = NLT + 1
TCOLS = TOK + CTX
EPS = 1e-6
SCALE = 0.125
NEG = -1e30
KH = CTX + 128 + TOK + 128
KALL = CTX + SEQ
NKC = KALL // 128


class Res:
    __slots__ = ("name", "w", "r", "dsem")

    def __init__(self, name):
        self.name = name
        self.w = {}
        self.r = {}
        self.dsem = None


class _Eng:
    def __init__(self, name, e, sem):
        self.name, self.e, self.sem, self.cnt, self.seen = name, e, sem, 0, {}


class _DSem:
    def __init__(self, key, sem):
        self.key, self.sem, self.cnt = key, sem, 0


class KB:
    def __init__(self, nc, es):
        self.nc, self.es = nc, es
        self.engs = {}
        for name, e in (("pe", nc.tensor), ("act", nc.scalar), ("dve", nc.vector),
                        ("pool", nc.gpsimd), ("sp", nc.sync)):
            self.engs[name] = _Eng(name, e, es.enter_context(nc.semaphore("s_" + name)))
        self.dsems = []
        self.free_dsems = []
        self.phase_dsems = []
        self.nres = 0

    def res(self, name="r"):
        self.nres += 1
        return Res(f"{name}{self.nres}")

    def _dsem(self, r):
        if r.dsem is None:
            if self.free_dsems:
                r.dsem = self.free_dsems.pop()
            else:
                r.dsem = _DSem("d_" + r.name, self.es.enter_context(self.nc.semaphore("d_" + r.name)))
                self.dsems.append(r.dsem)
            self.phase_dsems.append(r.dsem)
        return r.dsem

    def phase_end(self):
        self.barrier()
        self.free_dsems.extend(self.phase_dsems)
        self.phase_dsems = []

    def _deps(self, E, reads, writes):
        toks = {}
        for r in reads:
            for k, t in r.w.items():
                if k not in toks or toks[k][1] < t[1]:
                    toks[k] = t
        for w in writes:
            for dct in (w.w, w.r):
                for k, t in dct.items():
                    if k not in toks or toks[k][1] < t[1]:
                        toks[k] = t
        for k, (sem, val) in toks.items():
            if k == E.name or E.seen.get(k, 0) >= val:
                continue
            E.e.wait_ge(sem, val)
            E.seen[k] = val

    def op(self, eng, fn, reads=(), writes=()):
        E = self.engs[eng]
        self._deps(E, reads, writes)
        ins = fn(E.e)
        E.cnt += 1
        ins.then_inc(E.sem, 1)
        tok = (E.sem, E.cnt)
        for r in reads:
            r.r[E.name] = tok
        for w in writes:
            w.w[E.name] = tok
        return ins

    def dma(self, q, out, in_, reads=(), writes=(), own=None):
        E = self.engs[q]
        self._deps(E, reads, writes)
        ds = self._dsem(own)
        ins = E.e.dma_start(out=out, in_=in_)
        ds.cnt += 16
        ins.then_inc(ds.sem, 16)
        tok = (ds.sem, ds.cnt)
        for r in reads:
            r.r[ds.key] = tok
        for w in writes:
            w.w[ds.key] = tok

    def barrier(self):
        toks = {E.name: (E.sem, E.cnt) for E in self.engs.values() if E.cnt > 0}
        for ds in self.dsems:
            if ds.cnt > 0:
                toks[ds.key] = (ds.sem, ds.cnt)
        for E in self.engs.values():
            for k, (sem, val) in toks.items():
                if k == E.name or E.seen.get(k, 0) >= val:
                    continue
                E.e.wait_ge(sem, val)
                E.seen[k] = val


class T:
    def __init__(self, kb, es, name, shape, dtype, psum=False):
        nc = kb.nc
        kb.nres += 1
        self.t = es.enter_context((nc.psum_tensor if psum else nc.sbuf_tensor)(f"{name}_{kb.nres}", shape, dtype))
        self.r = kb.res(name)

    def __getitem__(self, idx):
        return self.t[idx]


class Prog:
    def __init__(self, phases, fused=False, nlayers=1):
        self.phases = phases
        self.fused = fused
        self.nl = nlayers
        self.nc = bass.Bass("TRN2", target_bir_lowering=False)
        self.in_names = []
        self.out_names = []

    def din(self, name, shape, dt=F32):
        self.in_names.append(name)
        return self.nc.dram_tensor(name, list(shape), dt, kind="ExternalInput").ap()

    def dout(self, name, shape, dt=F32):
        self.out_names.append(name)
        return self.nc.dram_tensor(name, list(shape), dt, kind="ExternalOutput").ap()

    def dint(self, name, shape, dt=F32):
        return self.nc.dram_tensor(name, list(shape), dt, kind="Internal").ap()

    def build(self):
        nc = self.nc
        L = self.nl
        P = self.phases
        with ExitStack() as es:
            kb = self.kb = KB(nc, es)
            d = self.d = {}
            d["cvec"] = self.din("cvec", [128, 8, 2])
            d["w_ada"] = self.din("w_ada", [L, D, 9 * D])
            d["b_ada"] = self.din("b_ada", [L, 128, 72])
            d["norm_g"] = self.din("norm_g", [L, 128, 3, 8])
            d["ident"] = self.din("ident", [128, 128])
            d["permm"] = self.din("permm", [128, 128])
            d["ropeC"] = self.din("ropeC", [128, TOK])
            d["ropeS"] = self.din("ropeS", [128, TOK])
            d["qk_g"] = self.din("qk_g", [L, 128, 4])
            d["w_in"] = self.din("w_in", [L, D, 3072])
            self.ffn_which = [w for w, p in ((0, "A"), (1, "C")) if p in P]
            if self.ffn_which:
                nw = len(self.ffn_which)
                d["wg"] = self.din("ffn_w_gate", [L, nw, D, DFF])
                d["wu"] = self.din("ffn_w_up", [L, nw, D, DFF])
                d["wd"] = self.din("ffn_w_down", [L, nw, DFF, D])
            if "B1" in P or "B2" in P:
                d["sink"] = self.din("sink_a", [L, 1, 8])
                d["conv_w"] = self.din("conv_w", [L, 128, 4, 3])
                d["w_pa"] = self.din("w_pa", [L, 512, D])
                d["w_pb"] = self.din("w_pb", [L, 512, D])
                d["w_pc"] = self.din("w_pc", [L, 512, D])
                d["w_gate"] = self.din("w_gate", [L, D, 3 * D])
                d["b_gate"] = self.din("b_gate", [L, 128, 24])
                d["w_o"] = self.din("w_o", [L, D, D])
                d["wmask"] = self.din("wmask", [128, 4, 512])
                d["emask"] = self.din("emask", [128, 2])
            d["xin"] = self.din("xin", [D, TCOLS])
            d["xw"] = self.dout("xw", [D, TCOLS])
            if self.fused:
                d["sel"] = self.din("sel", [128, 8])
                d["xedge_dummy"] = None
                for par in range(2):
                    d[f"kaT{par}"] = self.dint(f"kaT{par}", [128, TCOLS], BF16)
                    d[f"va{par}"] = self.dint(f"va{par}", [TCOLS, 128], BF16)
                    d[f"kae{par}"] = self.dint(f"kae{par}", [128, 256], BF16)
                    d[f"vae{par}"] = self.dint(f"vae{par}", [256, 128], BF16)
                    d[f"kcl{par}"] = self.dint(f"kcl{par}", [128, TOK], BF16)
                    d[f"vcl{par}"] = self.dint(f"vcl{par}", [TOK, 128], BF16)
                    d[f"kcc{par}"] = self.dint(f"kcc{par}", [128, CTX], BF16)
                    d[f"vcc{par}"] = self.dint(f"vcc{par}", [CTX, 128], BF16)
                    d[f"xe{par}"] = self.dint(f"xe{par}", [128, 16])
                    d[f"gkae{par}"] = self.dint(f"gkae{par}", [4 * 128, 256], BF16)
                    d[f"gvae{par}"] = self.dint(f"gvae{par}", [4 * 256, 128], BF16)
                    d[f"gkc{par}"] = self.dint(f"gkc{par}", [4 * 128, TOK], BF16)
                    d[f"gvc{par}"] = self.dint(f"gvc{par}", [4 * TOK, 128], BF16)
                    d[f"gxe{par}"] = self.dint(f"gxe{par}", [4 * 128, 16])
                    d[f"syn{par}"] = self.dint(f"syn{par}", [128, 16])
                    d[f"gsyn{par}"] = self.dint(f"gsyn{par}", [4 * 128, 16])
                for nm, shp in (("wg", [L, 2, D, DFF]), ("wu", [L, 2, D, DFF]), ("wd", [L, 2, DFF, D]), ("w_in", [L, D, 3072]),
                                ("w_gate", [L, D, 3 * D]), ("w_pa", [L, 512, D]), ("w_pb", [L, 512, D]), ("w_pc", [L, 512, D]),
                                ("w_o", [L, D, D])):
                    d["S_" + nm] = self.dint("S_" + nm, shp, BF16)
                d["QC"] = self.dint("QC", [512, TCOLS], BF16)
                d["ACC"] = self.dint("ACC", [D, TCOLS])
                d["GC"] = self.dint("GC", [D, TCOLS])

            c = self.c = {}
            c["ones"] = T(kb, es, "ones", [128, 128], BF16)
            c["bones"] = T(kb, es, "bones", [128, 128], BF16)
            c["onesf"] = T(kb, es, "onesf", [128, 128], F32)
            c["ident"] = T(kb, es, "identb", [128, 128], BF16)
            c["permm"] = T(kb, es, "permb", [128, 128], BF16)
            c["qkg"] = T(kb, es, "qkg", [128, L, 4], F32)
            c["epsc"] = T(kb, es, "epsc", [128, 1], F32)
            c["modA"] = T(kb, es, "modA", [128, L * 3 * 8 * 2], F32)
            c["modB"] = T(kb, es, "modB", [128, L * 3 * 8 * 2], F32)
            c["modG"] = T(kb, es, "modG", [128, L * 3 * 8 * 2], F32)
            kb.op("dve", lambda e: e.memset(c["ones"][:], 1.0), writes=[c["ones"].r])
            kb.op("dve", lambda e: e.memset(c["bones"][:], 0.0), writes=[c["bones"].r])
            kb.op("dve", lambda e: e.memset(c["bones"][0:64, 0:64], 1.0), writes=[c["bones"].r])
            kb.op("dve", lambda e: e.memset(c["bones"][64:128, 64:128], 1.0), writes=[c["bones"].r])
            kb.op("dve", lambda e: e.memset(c["onesf"][:], 1.0), writes=[c["onesf"].r])
            kb.op("dve", lambda e: e.memset(c["epsc"][:], EPS), writes=[c["epsc"].r])
            kb.dma("pool", c["ident"][:], d["ident"][:, :], writes=[c["ident"].r], own=c["ident"].r)
            kb.dma("pool", c["permm"][:], d["permm"][:, :], writes=[c["permm"].r], own=c["permm"].r)
            kb.dma("sp", c["qkg"][:], d["qk_g"].rearrange("l p f -> p l f"), writes=[c["qkg"].r], own=c["qkg"].r)

            kb.barrier()
            kb.phase_dsems = []
            self.precast((0, "A"))
            for l in range(L):
                self.phase_M(l)
            xsrc = d["xin"]
            for l in range(L):
                if "A" in P:
                    self.phase_ffn(l, 0, xsrc, d["xw"], with_kv=True)
                    xsrc = d["xw"]
                if "X" in P:
                    self.phase_X(l)
                if "B1" in P:
                    self.phase_B1(l, xsrc)
                if "B2" in P:
                    self.phase_B2(l, xsrc, d["xw"])
                    xsrc = d["xw"]
                if "C" in P:
                    self.phase_ffn(l, 1, xsrc, d["xw"], with_kv=False)
                    xsrc = d["xw"]
            kb.phase_end()
        return self

    def precast(self, key, after=()):
        if key is None:
            return
        after = [t.r for t in after]
        l, p = key
        kb, d = self.kb, self.d
        if p == "A":
            items = [("wg", (l, 0)), ("wu", (l, 0)), ("wd", (l, 0)), ("w_in", (l,))]
        elif p == "B1":
            items = [("w_gate", (l,)), ("w_pa", (l,)), ("w_pb", (l,))]
        elif p == "B2":
            items = [("w_pc", (l,)), ("w_o", (l,))]
        else:
            items = [("wg", (l, 1)), ("wu", (l, 1)), ("wd", (l, 1))]
        for nm, idx in items:
            src, dst = d[nm], d["S_" + nm]
            for i in idx:
                src, dst = src[i], dst[i]
            r = kb.res("cast")
            rows = src.shape[0]
            h = rows // 2
            for a, b in ((0, h), (h, rows)):
                kb.dma("pool", dst[a:b, :], src[a:b, :], reads=after, own=r)

    def next_key(self, l, p):
        seq = [(ll, pp) for ll in range(self.nl) for pp in ("A", "B1", "B2", "C")]
        i = seq.index((l, p))
        return seq[i + 1] if i + 1 < len(seq) else None

    def mod(self, which, l, n, ch, col):
        i = ((l * 3 + n) * 8 + ch) * 2 + col
        return self.c[which][:, i:i + 1]

    def phase_M(self, l):
        nc, kb, d, c = self.nc, self.kb, self.d, self.c
        with ExitStack() as ph:
            cv = T(kb, ph, "cv", [128, 8, 2], F32)
            sc = T(kb, ph, "sc", [128, 8, 2], F32)
            sg = T(kb, ph, "sgm", [128, 8, 2], F32)
            ba = T(kb, ph, "ba", [128, 72], F32)
            ng = T(kb, ph, "ng", [128, 3, 8], F32)
            mt = T(kb, ph, "mt", [128, 72, 2], F32)
            st = [T(kb, ph, f"wst{i}", [128, 8, 1024], F32) for i in range(2)]
            pm = T(kb, ph, "pm", [128, 72, 2], F32, psum=True)
            kb.dma("sp", cv[:], d["cvec"][:, :, :], writes=[cv.r], own=cv.r)
            kb.dma("sp", ba[:], d["b_ada"][l], writes=[ba.r], own=ba.r)
            kb.dma("sp", ng[:], d["norm_g"][l], writes=[ng.r], own=ng.r)
            kb.op("act", lambda e: e.activation(out=sg[:], in_=cv[:], func=AF.Sigmoid), reads=[cv.r], writes=[sg.r])
            kb.op("dve", lambda e: e.tensor_tensor(sc[:], cv[:], sg[:], op=ALU.mult), reads=[cv.r, sg.r], writes=[sc.r])
            wsrc = d["w_ada"][l].rearrange("(kc p) n -> p kc n", p=128)
            for cb in range(9):
                s = st[cb % 2]
                q = "sp" if cb % 2 == 0 else "act"
                kb.dma(q, s[:], wsrc[:, :, cb * 1024:(cb + 1) * 1024], writes=[s.r], own=s.r)
                for o in range(8):
                    oc = cb * 8 + o
                    for kc in range(8):
                        kb.op("pe", lambda e, kc=kc, o=o, oc=oc: e.matmul(
                            pm[:, oc, :], s[:, kc, o * 128:(o + 1) * 128], sc[:, kc, :],
                            start=(kc == 0), stop=(kc == 7)), reads=[s.r, sc.r], writes=[pm.r])
            for col in range(2):
                kb.op("dve", lambda e, col=col: e.tensor_tensor(mt[:, :, col], pm[:, :, col], ba[:], op=ALU.add),
                      reads=[pm.r, ba.r], writes=[mt.r])
            for n in range(3):
                for col in range(2):
                    base = (l * 3 + n) * 16
                    A = c["modA"][:, base + col: base + 16: 2]
                    B = c["modB"][:, base + col: base + 16: 2]
                    G = c["modG"][:, base + col: base + 16: 2]
                    sh = mt[:, (3 * n) * 8:(3 * n + 1) * 8, col]
                    scl = mt[:, (3 * n + 1) * 8:(3 * n + 2) * 8, col]
                    gt = mt[:, (3 * n + 2) * 8:(3 * n + 3) * 8, col]
                    kb.op("dve", lambda e, A=A, scl=scl, n=n: e.scalar_tensor_tensor(
                        out=A, in0=scl, scalar=1.0, in1=ng[:, n, :], op0=ALU.add, op1=ALU.mult),
                        reads=[mt.r, ng.r], writes=[c["modA"].r])
                    kb.op("dve", lambda e, B=B, sh=sh: e.tensor_copy(B, sh), reads=[mt.r], writes=[c["modB"].r])
                    kb.op("dve", lambda e, G=G, gt=gt, n=n: e.tensor_scalar(
                        G, gt, 1.0 if n == 1 else 0.5, None, op0=ALU.mult), reads=[mt.r], writes=[c["modG"].r])
            kb.phase_end()

    def emit_norm(self, ph_t, xt, xr, width, l, n, col, hout, hr, xoff=0):
        kb, c = self.kb, self.c
        sq, pss, rs, tmp = ph_t["sq"], ph_t["pss"], ph_t["rs"], ph_t["tmp"]
        W = width
        kb.op("act", lambda e: e.activation(out=sq[:, :, 0:W], in_=xt[:, :, xoff:xoff + W], func=AF.Square),
              reads=[xr], writes=[sq.r])
        for ch in range(8):
            kb.op("pe", lambda e, ch=ch: e.matmul(pss[:, 0:W], c["ones"][:], sq[:, ch, 0:W],
                                                 start=(ch == 0), stop=(ch == 7)),
                  reads=[sq.r, c["ones"].r], writes=[pss.r])
        kb.op("act", lambda e: e.activation(out=rs[:, 0:W], in_=pss[:, 0:W], func=AF.Ln,
                                            bias=c["epsc"][:, 0:1], scale=1.0 / D),
              reads=[pss.r, c["epsc"].r], writes=[rs.r])
        kb.op("act", lambda e: e.activation(out=rs[:, 0:W], in_=rs[:, 0:W], func=AF.Exp, scale=-0.5),
              reads=[rs.r], writes=[rs.r])
        for ch in range(8):
            tb = tmp[ch % 2]
            kb.op("dve", lambda e, ch=ch, tb=tb: e.scalar_tensor_tensor(
                out=tb[:, 0:W], in0=xt[:, ch, xoff:xoff + W], scalar=self.mod("modA", l, n, ch, col),
                in1=rs[:, 0:W], op0=ALU.mult, op1=ALU.mult),
                reads=[xr, rs.r, c["modA"].r], writes=[tb.r])
            kb.op("act", lambda e, ch=ch, tb=tb: e.activation(
                out=hout[:, ch, 0:W], in_=tb[:, 0:W], func=AF.Identity,
                bias=self.mod("modB", l, n, ch, col), scale=1.0),
                reads=[tb.r, c["modB"].r], writes=[hr])

    def norm_tiles(self, ph, wmax):
        kb = self.kb
        return {
            "sq": T(kb, ph, "sq", [128, 8, wmax], BF16),
            "pss": T(kb, ph, "pss", [128, 512], F32, psum=True),
            "rs": T(kb, ph, "rs", [128, wmax], F32),
            "tmp": [T(kb, ph, f"ntmp{i}", [128, wmax], F32) for i in range(2)],
        }

    def qk_tiles(self, ph):
        kb = self.kb
        return {
            "sqk": T(kb, ph, "sqk", [128, TT], BF16),
            "psk": T(kb, ph, "psk", [128, 512], F32, psum=True),
            "rk": T(kb, ph, "rk", [128, TT], F32),
            "yf": T(kb, ph, "yf", [128, TT], F32),
            "yb": T(kb, ph, "yb", [128, TT], BF16),
            "t1": T(kb, ph, "t1", [128, TT], F32),
            "t2": T(kb, ph, "t2", [128, TT], F32),
            "ropeC": T(kb, ph, "rpC", [128, TT], F32),
            "ropeS": T(kb, ph, "rpS", [128, TT], F32),
        }

    def load_rope(self, qt, t):
        kb, d = self.kb, self.d
        kb.dma("sp", qt["ropeC"][:], d["ropeC"][:, t * TT:(t + 1) * TT], writes=[qt["ropeC"].r], own=qt["ropeC"].r)
        kb.dma("sp", qt["ropeS"][:], d["ropeS"][:, t * TT:(t + 1) * TT], writes=[qt["ropeS"].r], own=qt["ropeS"].r)

    def emit_qknorm(self, qt, ps, psr, l, gi, rope, out_ap, out_r, split=False):
        kb, c = self.kb, self.c
        g = c["qkg"][:, l, gi:gi + 1]
        kb.op("act", lambda e: e.activation(out=qt["sqk"][:], in_=ps, func=AF.Square), reads=[psr], writes=[qt["sqk"].r])
        kb.op("pe", lambda e: e.matmul(qt["psk"][:, 0:TT], c["bones"][:], qt["sqk"][:], start=True, stop=True),
              reads=[qt["sqk"].r, c["bones"].r], writes=[qt["psk"].r])
        kb.op("act", lambda e: e.activation(out=qt["rk"][:], in_=qt["psk"][:, 0:TT], func=AF.Ln,
                                            bias=c["epsc"][:, 0:1], scale=1.0 / 64),
              reads=[qt["psk"].r, c["epsc"].r], writes=[qt["rk"].r])
        kb.op("act", lambda e: e.activation(out=qt["rk"][:], in_=qt["rk"][:], func=AF.Exp, scale=-0.5),
              reads=[qt["rk"].r], writes=[qt["rk"].r])
        if not rope:
            kb.op("dve", lambda e: e.scalar_tensor_tensor(out=out_ap, in0=ps, scalar=g, in1=qt["rk"][:],
                                                          op0=ALU.mult, op1=ALU.mult),
                  reads=[psr, qt["rk"].r, c["qkg"].r], writes=[out_r])
            return None
        kb.op("dve", lambda e: e.scalar_tensor_tensor(out=qt["yf"][:], in0=ps, scalar=g, in1=qt["rk"][:],
                                                      op0=ALU.mult, op1=ALU.mult),
              reads=[psr, qt["rk"].r, c["qkg"].r], writes=[qt["yf"].r])
        kb.op("act", lambda e: e.copy(qt["yb"][:], qt["yf"][:]), reads=[qt["yf"].r], writes=[qt["yb"].r])
        def fin():
            kb.op("pe", lambda e: e.matmul(qt["psk"][:, TT:2 * TT], c["permm"][:], qt["yb"][:], start=True, stop=True),
                  reads=[qt["yb"].r, c["permm"].r], writes=[qt["psk"].r])
            kb.op("pool", lambda e: e.tensor_tensor(qt["t1"][:], qt["yf"][:], qt["ropeC"][:], op=ALU.mult),
                  reads=[qt["yf"].r, qt["ropeC"].r], writes=[qt["t1"].r])
            kb.op("dve", lambda e: e.tensor_tensor(qt["t2"][:], qt["psk"][:, TT:2 * TT], qt["ropeS"][:], op=ALU.mult),
                  reads=[qt["psk"].r, qt["ropeS"].r], writes=[qt["t2"].r])
            kb.op("dve", lambda e: e.tensor_tensor(out_ap, qt["t1"][:], qt["t2"][:], op=ALU.add),
                  reads=[qt["t1"].r, qt["t2"].r], writes=[out_r])

        if split:
            return fin
        fin()
        return None

    def load_v(self, q, vt, ch0, nch, src_rows):
        for kv in range(2):
            self.kb.dma(q, vt[:, ch0:ch0 + nch, kv, 0:64],
                        src_rows[:, kv * 64:(kv + 1) * 64].rearrange("(ch p) dd -> p ch dd", p=128),
                        writes=[vt.r], own=vt.r)

    def load_w(self, dst, src_ap, nchunk, per=1):
        kb = self.kb
        for n_, k0 in enumerate(range(0, nchunk, per)):
            k1 = min(nchunk, k0 + per)
            kb.dma("sp" if n_ % 2 == 0 else "act", dst[:, k0:k1, :], src_ap[:, k0:k1, :], writes=[dst.r], own=dst.r)

    def phase_ffn(self, l, which, xsrc, xdst, with_kv):
        nc, kb, d, c = self.nc, self.kb, self.d, self.c
        n = 0 if which == 0 else 2
        xs = xsrc.rearrange("(c p) t -> p c t", p=128)
        xd = xdst.rearrange("(c p) t -> p c t", p=128)
        with ExitStack() as ph:
            wg = T(kb, ph, "wg", [128, 8, DFF], BF16)
            wu = T(kb, ph, "wu", [128, 8, DFF], BF16)
            wd = T(kb, ph, "wd", [128, NJ, D], BF16)
            wi_ = self.ffn_which.index(which)
            self.load_w(wg, d["S_wg"][l, which].rearrange("(c p) n -> p c n", p=128), 8, 2)
            self.load_w(wu, d["S_wu"][l, which].rearrange("(c p) n -> p c n", p=128), 8, 2)
            self.load_w(wd, d["S_wd"][l, which].rearrange("(c p) n -> p c n", p=128), NJ, 6)
            self.precast(self.next_key(l, "A" if which == 0 else "C"), after=[wg, wu, wd])
            xb = [T(kb, ph, f"xb{i}", [128, 8, TT], F32) for i in range(2)]
            hb = [T(kb, ph, f"hb{i}", [128, 8, TT], BF16) for i in range(3 if with_kv else 2)]
            act = T(kb, ph, "actb", [128, NJ, TT], BF16)
            sgb = [T(kb, ph, f"sgb{i}", [128, TT], F32) for i in range(2)]
            nt = self.norm_tiles(ph, TT)
            pgu = [T(kb, ph, f"pgu{i}", [128, 2, TT], F32, psum=True) for i in range(2)]
            py = [T(kb, ph, f"py{i}", [128, 512], F32, psum=True) for i in range(2)]
            if with_kv:
                wkv = T(kb, ph, "wkv", [128, 8, 512], BF16)
                self.load_w(wkv, d["S_w_in"][l].rearrange("(c p) n -> p c n", p=128)[:, :, 0:512], 8, 8)
                qt = self.qk_tiles(ph)
                kob = [T(kb, ph, f"kob{i}", [128, TT], BF16) for i in range(2)]
                vob = T(kb, ph, "vob", [128, 2, 2, 128], BF16)
                pkv = T(kb, ph, "pkv", [128, 512], F32, psum=True)
                pv = T(kb, ph, "pvv", [128, 512], F32, psum=True)
                xeb = T(kb, ph, "xeb", [128, 8, 2], F32)

            def load_x(t):
                b = xb[t % 2]
                kb.dma("sp", b[:], xs[:, :, t * TT:(t + 1) * TT], writes=[b.r], own=b.r)

            load_x(0)
            self.emit_norm(nt, xb[0], xb[0].r, TT, l, n, 0, hb[0], hb[0].r)
            for t in range(NTILE):
                col = 1 if t == NLT else 0
                if t + 1 < NTILE:
                    load_x(t + 1)
                x = xb[t % 2]
                h = hb[t % 2]
                for j in range(NJ):
                    p = pgu[j % 2]
                    for (wi, W_) in ((0, wg), (1, wu)):
                        for ch in range(8):
                            kb.op("pe", lambda e, ch=ch, W_=W_, wi=wi, p=p, j=j: e.matmul(
                                p[:, wi, :], W_[:, ch, j * 128:(j + 1) * 128], h[:, ch, :],
                                start=(ch == 0), stop=(ch == 7)), reads=[W_.r, h.r], writes=[p.r])
                    sg_ = sgb[j % 2]
                    kb.op("act", lambda e, p=p, sg_=sg_: e.activation(out=sg_[:], in_=p[:, 0, :], func=AF.Silu),
                          reads=[p.r], writes=[sg_.r])
                    kb.op("dve", lambda e, p=p, sg_=sg_, j=j: e.tensor_tensor(act[:, j, :], sg_[:], p[:, 1, :], op=ALU.mult),
                          reads=[p.r, sg_.r], writes=[act.r])
                for oc in range(8):
                    if oc == 2 and t + 1 < NTILE:
                        xn, hn = xb[(t + 1) % 2], hb[(t + 1) % 2]
                        self.emit_norm(nt, xn, xn.r, TT, l, n, 1 if t + 1 == NLT else 0, hn, hn.r)
                    p = py[oc % 2]
                    for j in range(NJ):
                        kb.op("pe", lambda e, p=p, j=j, oc=oc: e.matmul(
                            p[:, 0:TT], wd[:, j, oc * 128:(oc + 1) * 128], act[:, j, :],
                            start=(j == 0), stop=(j == NJ - 1)), reads=[wd.r, act.r], writes=[p.r])
                    kb.op("dve", lambda e, p=p, oc=oc: e.scalar_tensor_tensor(
                        out=x[:, oc, :], in0=p[:, 0:TT], scalar=self.mod("modG", l, n, oc, col), in1=x[:, oc, :],
                        op0=ALU.mult, op1=ALU.add), reads=[p.r, x.r, c["modG"].r], writes=[x.r])
                kb.dma("sp", xd[:, :, t * TT:(t + 1) * TT], x[:], reads=[x.r], own=x.r)
                if not with_kv:
                    continue
                if t == 0:
                    kb.op("pool", lambda e: e.tensor_copy(xeb[:, :, 0], x[:, :, 0]), reads=[x.r], writes=[xeb.r])
                if t == NLT - 1:
                    kb.op("pool", lambda e: e.tensor_copy(xeb[:, :, 1], x[:, :, TT - 1]), reads=[x.r], writes=[xeb.r])
                    kb.dma("sp", d[f"xe{l % 2}"][:, :], xeb[:].rearrange("p c j -> p (c j)"), reads=[xeb.r], own=xeb.r)
                h2 = hb[2]
                self.emit_norm(nt, x, x.r, TT, l, 1, col, h2, h2.r)
                if col == 0:
                    self.load_rope(qt, t)
                par = l % 2
                for ki, (c0, gi) in enumerate(((0, 1), (256, 3))):
                    for ch in range(8):
                        kb.op("pe", lambda e, ch=ch, c0=c0: e.matmul(
                            pkv[:, 0:TT], wkv[:, ch, c0:c0 + 128], h2[:, ch, :], start=(ch == 0), stop=(ch == 7)),
                            reads=[wkv.r, h2.r], writes=[pkv.r])
                    ko = kob[ki]
                    self.emit_qknorm(qt, pkv[:, 0:TT], pkv.r, l, gi, col == 0, ko[:], ko.r)
                    if ki == 0:
                        kb.dma("sp", d[f"kaT{par}"][:, t * TT:(t + 1) * TT], ko[:], reads=[ko.r], own=ko.r)
                        if t == 0:
                            kb.dma("sp", d[f"kae{par}"][:, 0:128], ko[:, 0:128], reads=[ko.r], own=ko.r)
                        if t == NLT - 1:
                            kb.dma("sp", d[f"kae{par}"][:, 128:256], ko[:, 128:256], reads=[ko.r], own=ko.r)
                    elif col == 0:
                        kb.dma("sp", d[f"kcl{par}"][:, t * TT:(t + 1) * TT], ko[:], reads=[ko.r], own=ko.r)
                    else:
                        kb.dma("sp", d[f"kcc{par}"][:, :], ko[:], reads=[ko.r], own=ko.r)
                for tb in range(2):
                    for vi, c0 in enumerate((128, 384)):
                        for ch in range(8):
                            kb.op("pe", lambda e, ch=ch, c0=c0, tb=tb, vi=vi: e.matmul(
                                pv[:, (tb * 2 + vi) * 128:(tb * 2 + vi + 1) * 128], h2[:, ch, tb * 128:(tb + 1) * 128],
                                wkv[:, ch, c0:c0 + 128], start=(ch == 0), stop=(ch == 7)),
                                reads=[wkv.r, h2.r], writes=[pv.r])
                kb.op("act", lambda e: e.copy(vob[:].rearrange("p a b c -> p (a b c)"), pv[:, :]), reads=[pv.r], writes=[vob.r])
                vdst = [d[f"va{par}"][t * TT:(t + 1) * TT, :],
                        d[f"vcl{par}"][t * TT:(t + 1) * TT, :] if col == 0 else d[f"vcc{par}"][:, :]]
                for vi, dst in enumerate(vdst):
                    kb.dma("sp", dst.rearrange("(tb p) f -> p tb f", p=128), vob[:, :, vi, :], reads=[vob.r], own=vob.r)
                if t == 0:
                    kb.dma("sp", d[f"vae{par}"][0:128, :], vob[:, 0, 0, :], reads=[vob.r], own=vob.r)
                if t == NLT - 1:
                    kb.dma("sp", d[f"vae{par}"][128:256, :], vob[:, 1, 0, :], reads=[vob.r], own=vob.r)
            kb.phase_end()

    def phase_X(self, l):
        nc, kb, d = self.nc, self.kb, self.d
        par = l % 2
        if not hasattr(self, "_xsems"):
            self._xsems = []
            for nm in ("xbig", "xsmall", "xsync"):
                ds = _DSem(nm, kb.es.enter_context(nc.semaphore(nm)))
                kb.dsems.append(ds)
                self._xsems.append(ds)
        xbig, xsmall, xsync = self._xsems
        kb.barrier()
        rg = [[0, 1, 2, 3], [4, 5, 6, 7]]

        def gather(a, b, ds):
            ins = nc.gpsimd.collective_compute("AllGather", ALU.bypass, replica_groups=rg,
                                               ins=[d[f"{a}{par}"][:, :]], outs=[d[f"{b}{par}"][:, :]])
            ds.cnt += 1
            ins.then_inc(ds.sem, 1)

        gather("kcl", "gkc", xbig)
        gather("vcl", "gvc", xbig)
        kb.barrier()
        for a, b in (("kae", "gkae"), ("vae", "gvae"), ("xe", "gxe")):
            gather(a, b, xsmall)
        kb.barrier()
        gather("syn", "gsyn", xsync)
        kb.barrier()

    def attn_tiles(self, ph, npt):
        kb = self.kb
        return {
            "pT": [T(kb, ph, f"pT{i}", [128, 512], BF16) for i in range(npt)],
            "dsb": T(kb, ph, "dsb", [128, 512], F32),
            "rbs": T(kb, ph, "rbs", [64, 512], F32),
            "pbb": T(kb, ph, "pbb", [128, 512], F32, psum=True),
            "cnt": 0,
        }

    def attn_group(self, at, pss, po, chunks, q_ap, q_r, out_ap, out_r, sinkrow=None, prev_epi=None):
        kb, c = self.kb, self.c
        n = len(chunks)
        LA = min(2, len(pss) - 1)
        bufs = []
        for i in range(n):
            bufs.append((pss[at["cnt"] % len(pss)], at["pT"][at["cnt"] % len(at["pT"])]))
            at["cnt"] += 1

        def emit_S(i):
            k_ap, k_r, v_ap, v_r, m_ap = chunks[i]
            ps = bufs[i][0]
            if m_ap is not None:
                kb.op("pe", lambda e: e.matmul(ps[:, :], c["ident"][:], m_ap, start=True, stop=False),
                      reads=[c["ident"].r, self._wm.r], writes=[ps.r])
            kb.op("pe", lambda e: e.matmul(ps[:, :].rearrange("p (j q) -> p j q", j=4), k_ap, q_ap,
                                           start=(m_ap is None), stop=True), reads=[k_r, q_r], writes=[ps.r])

        for i in range(min(LA, n)):
            emit_S(i)
        if prev_epi is not None:
            prev_epi()
        for i in range(n):
            ps, pT = bufs[i]
            kb.op("act", lambda e, ps=ps, pT=pT: e.activation(out=pT[:], in_=ps[:, :], func=AF.Exp, scale=SCALE),
                  reads=[ps.r], writes=[pT.r])
            if i + LA < n:
                emit_S(i + LA)
            v_ap, v_r = chunks[i][2], chunks[i][3]
            kb.op("pe", lambda e, pT=pT, v_ap=v_ap, i=i: e.matmul(po[0:65, :], v_ap, pT[:], start=(i == 0), stop=(i == n - 1)),
                  reads=[v_r, pT.r], writes=[po.r])

        def epilogue():
            dsb, rbs, pbb = at["dsb"], at["rbs"], at["pbb"]
            if sinkrow is not None:
                kb.op("dve", lambda e: e.tensor_tensor(dsb[64:65, :], po[64:65, :], sinkrow, op=ALU.add),
                      reads=[po.r, self._sinkrow.r], writes=[dsb.r])
            else:
                kb.op("dve", lambda e: e.tensor_copy(dsb[64:65, :], po[64:65, :]), reads=[po.r], writes=[dsb.r])
            kb.op("act", lambda e: e.activation(out=dsb[64:65, :], in_=dsb[64:65, :], func=AF.Ln), reads=[dsb.r], writes=[dsb.r])
            kb.op("act", lambda e: e.activation(out=dsb[64:65, :], in_=dsb[64:65, :], func=AF.Exp, scale=-1.0), reads=[dsb.r], writes=[dsb.r])
            kb.op("pe", lambda e: e.matmul(pbb[0:64, :], c["onesf"][64:65, 0:64], dsb[64:65, :], start=True, stop=True),
                  reads=[dsb.r, c["onesf"].r], writes=[pbb.r])
            kb.op("act", lambda e: e.copy(rbs[:, :], pbb[0:64, :]), reads=[pbb.r], writes=[rbs.r])
            kb.op("dve", lambda e: e.tensor_tensor(out_ap, po[0:64, :].rearrange("p (j q) -> p j q", j=4),
                                                   rbs[:, :].rearrange("p (j q) -> p j q", j=4), op=ALU.mult),
                  reads=[po.r, rbs.r], writes=[out_r])

        return epilogue

    def attn_pair(self, st, psp, pTp, po, kcall, vflat, nkc, qc, qb, ocT, prev_epi=None):
        kb, c = self.kb, self.c
        n = nkc
        bufs = []
        for i in range(n):
            bufs.append((psp[st["cnt"] % len(psp)], pTp[st["cnt"] % len(pTp)]))
            st["cnt"] += 1

        def emit_S(i):
            ps = bufs[i][0]
            for kv in range(2):
                ks = slice(kv * 64, (kv + 1) * 64)
                kb.op("pe", lambda e, kv=kv, ks=ks: e.matmul(
                    ps[:, kv, :].rearrange("p (j q) -> p j q", j=4), kcall[ks, i * 128:(i + 1) * 128],
                    qc[ks, :, qb * 128:(qb + 1) * 128], start=True, stop=True), reads=[kcall.r, qc.r], writes=[ps.r])

        emit_S(0)
        if prev_epi is not None:
            prev_epi()
        for i in range(n):
            ps, pT = bufs[i]
            kb.op("act", lambda e, ps=ps, pT=pT: e.activation(
                out=pT[:].rearrange("p a b -> p (a b)"), in_=ps[:].rearrange("p a b -> p (a b)"), func=AF.Exp, scale=SCALE),
                reads=[ps.r], writes=[pT.r])
            if i + 1 < n:
                emit_S(i + 1)
            for kv in range(2):
                o0 = i * 130 + kv * 65
                kb.op("pe", lambda e, pT=pT, kv=kv, o0=o0, i=i: e.matmul(
                    po[:, kv, :], vflat[:, o0:o0 + 128], pT[:, kv, :], start=(i == 0), stop=(i == n - 1)),
                    reads=[st["vr"], pT.r], writes=[po.r])

        def epilogue():
            dsb, rbs, pbb = st["dsb"], st["rbs"], st["pbb"]
            kb.op("dve", lambda e: e.tensor_copy(dsb[64:65, :, :], po[64:65, :, :]), reads=[po.r], writes=[dsb.r])
            dflat = dsb[64:65, :, :].rearrange("p a b -> p (a b)")
            kb.op("act", lambda e: e.activation(out=dflat, in_=dflat, func=AF.Ln), reads=[dsb.r], writes=[dsb.r])
            kb.op("act", lambda e: e.activation(out=dflat, in_=dflat, func=AF.Exp, scale=-1.0), reads=[dsb.r], writes=[dsb.r])
            for kv in range(2):
                kb.op("pe", lambda e, kv=kv: e.matmul(pbb[0:64, :], c["onesf"][64:65, 0:64], dsb[64:65, kv, :], start=True, stop=True),
                      reads=[dsb.r, c["onesf"].r], writes=[pbb.r])
                kb.op("act", lambda e: e.copy(rbs[:, :], pbb[0:64, :]), reads=[pbb.r], writes=[rbs.r])
                kb.op("dve", lambda e, kv=kv: e.tensor_tensor(
                    ocT[:, kv * 4:(kv + 1) * 4, qb * 128:(qb + 1) * 128], po[0:64, kv, :].rearrange("p (j q) -> p j q", j=4),
                    rbs[:, :].rearrange("p (j q) -> p j q", j=4), op=ALU.mult), reads=[po.r, rbs.r], writes=[ocT.r])

        return epilogue

    def phase_B1(self, l, xsrc):
        nc, kb, d, c = self.nc, self.kb, self.d, self.c
        xs = xsrc.rearrange("(c p) t -> p c t", p=128)
        W2 = TT + 2
        with ExitStack() as ph:
            win = T(kb, ph, "win", [128, 8, 2560], BF16)
            self.load_w(win, d["S_w_in"][l].rearrange("(c p) n -> p c n", p=128)[:, :, 512:3072], 8, 2)
            wgt = T(kb, ph, "wgt", [128, 8, 3072], BF16)
            self.load_w(wgt, d["S_w_gate"][l].rearrange("(c p) n -> p c n", p=128), 8, 2)
            wpa = T(kb, ph, "wpa", [64, 8, D], BF16)
            self.load_w(wpa, d["S_w_pa"][l].rearrange("(h dd) n -> dd h n", dd=64), 8, 4)
            wpb = T(kb, ph, "wpb", [128, 4, D], BF16)
            self.load_w(wpb, d["S_w_pb"][l].rearrange("(c p) n -> p c n", p=128), 4, 4)
            self.precast(self.next_key(l, "B1"), after=[win, wgt, wpa, wpb])
            bgt = T(kb, ph, "bgt", [128, 24], F32)
            cw = T(kb, ph, "cw", [128, 4, 3], F32)
            wm = self._wm = T(kb, ph, "wm", [128, 4, 512], BF16)
            em = T(kb, ph, "em", [128, 2], F32)
            xedge = T(kb, ph, "xedge", [128, 8, 2], F32)
            sk = T(kb, ph, "sk", [128, 8], F32)
            ske = T(kb, ph, "ske", [128, 8], F32)
            sinkrow = self._sinkrow = T(kb, ph, "sinkrow", [128, 2, 512], F32)
            kactx = T(kb, ph, "kactx", [128, CTX], BF16)
            vactx = T(kb, ph, "vactx", [128, 2, 2, 65], BF16)
            kb.dma("sp", bgt[:], d["b_gate"][l], writes=[bgt.r], own=bgt.r)
            kb.dma("sp", cw[:], d["conv_w"][l], writes=[cw.r], own=cw.r)
            kb.dma("pool", wm[:], d["wmask"][:, :, :], writes=[wm.r], own=wm.r)
            kb.dma("sp", em[:], d["emask"][:, :], writes=[em.r], own=em.r)
            par = l % 2
            kaT, va_d = d[f"kaT{par}"], d[f"va{par}"]
            gkae, gvae, gxe = d[f"gkae{par}"], d[f"gvae{par}"], d[f"gxe{par}"]
            sel = T(kb, ph, "sel", [128, 8], F32)
            cand = T(kb, ph, "cand", [128, 4, 4, 128], BF16)
            cx = T(kb, ph, "cx", [128, 4, 16], F32)
            halo = T(kb, ph, "halo", [128, 4, 128], BF16)
            hacc = T(kb, ph, "hacc", [128, 128], F32)
            gk3 = gkae.rearrange("(i p) c -> p i c", p=128)
            gv3 = gvae.rearrange("(i t) f -> t i f", i=4)

            def prep_halos():
                kb.dma("sp", sel[:], d["sel"][:, :], writes=[sel.r], own=sel.r)
                kb.dma("sp", cand[:, 0, :, :], gk3[:, :, 128:256], writes=[cand.r], own=cand.r)
                kb.dma("sp", cand[:, 1, :, :], gk3[:, :, 0:128], writes=[cand.r], own=cand.r)
                kb.dma("sp", cand[:, 2, :, :], gv3[128:256, :, :], writes=[cand.r], own=cand.r)
                kb.dma("sp", cand[:, 3, :, :], gv3[0:128, :, :], writes=[cand.r], own=cand.r)
                kb.dma("sp", cx[:], gxe.rearrange("(i p) f -> p i f", p=128), writes=[cx.r], own=cx.r)
                for w in range(4):
                    so = 0 if w % 2 == 0 else 4
                    kb.op("dve", lambda e, w=w, so=so: e.tensor_scalar(hacc[:], cand[:, w, 0, :], sel[:, so:so + 1], None, op0=ALU.mult),
                          reads=[cand.r, sel.r], writes=[hacc.r])
                    for i in range(1, 4):
                        o_ap = halo[:, w, :] if i == 3 else hacc[:]
                        kb.op("dve", lambda e, w=w, so=so, i=i, o_ap=o_ap: e.scalar_tensor_tensor(
                            out=o_ap, in0=cand[:, w, i, :], scalar=sel[:, so + i:so + i + 1], in1=hacc[:], op0=ALU.mult, op1=ALU.add),
                            reads=[cand.r, sel.r, hacc.r], writes=[halo.r if i == 3 else hacc.r])
                for j_out, so, j_in in ((0, 0, 1), (1, 4, 0)):
                    kb.op("dve", lambda e, j_out=j_out, so=so, j_in=j_in: e.tensor_scalar(
                        xedge[:, :, j_out], cx[:, 0, j_in:16:2], sel[:, so:so + 1], None, op0=ALU.mult),
                        reads=[cx.r, sel.r], writes=[xedge.r])
                    for i in range(1, 4):
                        kb.op("dve", lambda e, j_out=j_out, so=so, j_in=j_in, i=i: e.scalar_tensor_tensor(
                            out=xedge[:, :, j_out], in0=cx[:, i, j_in:16:2], scalar=sel[:, so + i:so + i + 1], in1=xedge[:, :, j_out],
                            op0=ALU.mult, op1=ALU.add), reads=[cx.r, sel.r, xedge.r], writes=[xedge.r])

            kb.dma("sp", sk[64:65, :], d["sink"][l], writes=[sk.r], own=sk.r)
            kb.op("act", lambda e: e.activation(out=ske[64:65, :], in_=sk[64:65, :], func=AF.Exp), reads=[sk.r], writes=[ske.r])
            for hh in range(8):
                kb.op("dve", lambda e, hh=hh: e.tensor_scalar(
                    sinkrow[64:65, hh // 4, (hh % 4) * 128:(hh % 4 + 1) * 128], c["onesf"][64:65, 0:128],
                    ske[64:65, hh:hh + 1], None, op0=ALU.mult), reads=[ske.r, c["onesf"].r], writes=[sinkrow.r])
            kb.dma("sp", kactx[:], kaT[:, TOK:TOK + CTX], writes=[kactx.r], own=kactx.r)
            kb.op("dve", lambda e: e.memset(vactx[:].rearrange("p a b c -> p (a b c)"), 1.0), writes=[vactx.r])
            self.load_v("sp", vactx, 0, 2, va_d[TOK:TOK + CTX, :])

            xh = [T(kb, ph, f"xh{i}", [128, 8, W2], F32) for i in range(2)]
            h = T(kb, ph, "hB", [128, 8, W2], BF16)
            nt = self.norm_tiles(ph, W2)
            qt = self.qk_tiles(ph)
            qaT = T(kb, ph, "qaT", [128, 4, TT], BF16)
            qcT = [T(kb, ph, f"qcT{i}", [128, 4, TT], BF16) for i in range(2)]
            kwin = [T(kb, ph, f"kwin{i}", [128, 512], BF16) for i in range(2)]
            vwin = [T(kb, ph, f"vwin{i}", [128, 4, 2, 65], BF16) for i in range(2)]
            for v in vwin:
                kb.op("dve", lambda e, v=v: e.memset(v[:].rearrange("p a b c -> p (a b c)"), 1.0), writes=[v.r])
            csb = T(kb, ph, "csb", [128, W2], F32)
            cu = T(kb, ph, "cu", [128, W2], F32)
            yv = T(kb, ph, "yv", [128, TT], F32)
            ob = T(kb, ph, "ob", [128, 4, TT], BF16)
            oaT = T(kb, ph, "oaT", [64, 8, TT], BF16)
            gA = [T(kb, ph, f"gA{i}", [128, TT], F32) for i in range(2)]
            gB = [T(kb, ph, f"gB{i}", [128, TT], F32) for i in range(2)]
            gC = [T(kb, ph, f"gC{i}", [128, TT], F32) for i in range(2)]
            a1 = [T(kb, ph, f"a1{i}", [128, TT], F32) for i in range(2)]
            a2 = [T(kb, ph, f"a2{i}", [128, TT], F32) for i in range(2)]
            accb = [T(kb, ph, f"accb{i}", [128, TT], F32) for i in range(2)]
            at = self.attn_tiles(ph, 3)
            pp = [T(kb, ph, f"ppj{i}", [128, 512], F32, psum=True) for i in range(2)]
            pss = [T(kb, ph, f"pss{i}", [128, 512], F32, psum=True) for i in range(2)]
            po = T(kb, ph, "po", [128, 512], F32, psum=True)
            accd = d["ACC"].rearrange("(c p) t -> p c t", p=128)
            gcd = d["GC"].rearrange("(c p) t -> p c t", p=128)
            qcd = d["QC"].rearrange("(c p) t -> p c t", p=128)

            order = list(range(1, NLT - 1)) + [0, NLT - 1, NLT]
            slot = {t: si % 2 for si, t in enumerate(order)}

            def load_x(t):
                b = xh[slot[t]]
                if t == NLT:
                    kb.dma("sp", b[:, :, 1:TT + 1], xs[:, :, TOK:TOK + CTX], writes=[b.r], own=b.r)
                elif t == 0:
                    kb.dma("sp", b[:, :, 1:W2], xs[:, :, 0:TT + 1], writes=[b.r], own=b.r)
                    kb.op("pool", lambda e: e.tensor_copy(b[:, :, 0], xedge[:, :, 0]), reads=[xedge.r], writes=[b.r])
                elif t == NLT - 1:
                    kb.dma("sp", b[:, :, 0:TT + 1], xs[:, :, t * TT - 1:(t + 1) * TT], writes=[b.r], own=b.r)
                    kb.op("pool", lambda e: e.tensor_copy(b[:, :, TT + 1], xedge[:, :, 1]), reads=[xedge.r], writes=[b.r])
                else:
                    kb.dma("sp", b[:], xs[:, :, t * TT - 1:(t + 1) * TT + 1], writes=[b.r], own=b.r)

            def load_kv(t):
                if t >= NLT:
                    return
                kw, vw = kwin[slot[t]], vwin[slot[t]]
                lo, hi = t * TT - 128, t * TT + 384
                if t == 0:
                    kb.dma("sp", kw[:, 128:512], kaT[:, 0:hi], writes=[kw.r], own=kw.r)
                    self.load_v("sp", vw, 1, 3, va_d[0:hi, :])
                    kb.op("pool", lambda e: e.tensor_copy(kw[:, 0:128], halo[:, 0, :]), reads=[halo.r], writes=[kw.r])
                    kb.op("pool", lambda e: e.tensor_copy(vw[:, 0, :, 0:64], halo[:, 2, :].rearrange("p (kv dd) -> p kv dd", kv=2)),
                          reads=[halo.r], writes=[vw.r])
                elif t == NLT - 1:
                    kb.dma("sp", kw[:, 0:384], kaT[:, lo:TOK], writes=[kw.r], own=kw.r)
                    self.load_v("sp", vw, 0, 3, va_d[lo:TOK, :])
                    kb.op("pool", lambda e: e.tensor_copy(kw[:, 384:512], halo[:, 1, :]), reads=[halo.r], writes=[kw.r])
                    kb.op("pool", lambda e: e.tensor_copy(vw[:, 3, :, 0:64], halo[:, 3, :].rearrange("p (kv dd) -> p kv dd", kv=2)),
                          reads=[halo.r], writes=[vw.r])
                else:
                    kb.dma("sp", kw[:], kaT[:, lo:hi], writes=[kw.r], own=kw.r)
                    self.load_v("sp", vw, 0, 4, va_d[lo:hi, :])

            def proj(pt, half, wt, c0, rhs_lo, rhs_hi, hh=h):
                Wd = rhs_hi - rhs_lo
                o0 = half * 256
                for ch in range(8):
                    kb.op("pe", lambda e, ch=ch: e.matmul(pt[:, o0:o0 + Wd], wt[:, ch, c0:c0 + 128], hh[:, ch, rhs_lo:rhs_hi],
                                                         start=(ch == 0), stop=(ch == 7)),
                          reads=[wt.r, hh.r], writes=[pt.r])

            load_x(order[0])
            load_kv(order[0])
            for si, t in enumerate(order):
                col = 1 if t == NLT else 0
                lat = col == 0
                if si + 1 < NTILE:
                    if order[si + 1] == 0:
                        prep_halos()
                    load_x(order[si + 1])
                    load_kv(order[si + 1])
                x = xh[slot[t]]
                self.emit_norm(nt, x, x.r, W2, l, 1, col, h, h.r)
                if lat:
                    self.load_rope(qt, t)
                qc = qcT[slot[t]]
                bank6 = [pp[0], pp[1], pss[0], pss[1], po, at["pbb"]]
                LAq = 4
                for qi in range(LAq):
                    proj(bank6[qi % 6], 0, win, qi * 128, 1, TT + 1)
                for qi in range(8):
                    pt = bank6[qi % 6]
                    if qi < 4:
                        fin = self.emit_qknorm(qt, pt[:, 0:TT], pt.r, l, 0, lat, qaT[:, qi, :], qaT.r, split=True)
                    else:
                        fin = self.emit_qknorm(qt, pt[:, 0:TT], pt.r, l, 2, lat, qc[:, qi - 4, :], qc.r, split=True)
                    if qi + LAq < 8:
                        proj(bank6[(qi + LAq) % 6], 0, win, (qi + LAq) * 128, 1, TT + 1)
                    if fin is not None:
                        fin()
                kb.dma("sp", qcd[:, :, t * TT:(t + 1) * TT], qc[:], reads=[qc.r], own=qc.r)
                for cc in range(4):
                    pcb, pub, pbb_c = bank6[(3 * cc) % 6], bank6[(3 * cc + 1) % 6], bank6[(3 * cc + 2) % 6]
                    proj(pcb, 0, win, 1536 + cc * 128, 0, W2)
                    proj(pub, 0, win, 2048 + cc * 128, 0, W2)
                    proj(pbb_c, 0, win, 1024 + cc * 128, 1, TT + 1)
                    kb.op("act", lambda e, pcb=pcb: e.copy(csb[:], pcb[:, 0:W2]), reads=[pcb.r], writes=[csb.r])
                    kb.op("dve", lambda e, pub=pub: e.tensor_tensor(cu[:], csb[:], pub[:, 0:W2], op=ALU.mult),
                          reads=[csb.r, pub.r], writes=[cu.r])
                    if not lat:
                        kb.op("dve", lambda e: e.memset(cu[:, 0:1], 0.0), writes=[cu.r])
                        kb.op("dve", lambda e: e.memset(cu[:, TT + 1:W2], 0.0), writes=[cu.r])
                    elif t == 0:
                        kb.op("dve", lambda e: e.tensor_scalar(cu[:, 0:1], cu[:, 0:1], em[:, 0:1], None, op0=ALU.mult),
                              reads=[em.r, cu.r], writes=[cu.r])
                    elif t == NLT - 1:
                        kb.op("dve", lambda e: e.tensor_scalar(cu[:, TT + 1:W2], cu[:, TT + 1:W2], em[:, 1:2], None, op0=ALU.mult),
                              reads=[em.r, cu.r], writes=[cu.r])
                    kb.op("dve", lambda e, cc=cc: e.tensor_scalar(yv[:], cu[:, 0:TT], cw[:, cc, 0:1], None, op0=ALU.mult),
                          reads=[cu.r, cw.r], writes=[yv.r])
                    for k in (1, 2):
                        kb.op("dve", lambda e, cc=cc, k=k: e.scalar_tensor_tensor(
                            out=yv[:], in0=cu[:, k:k + TT], scalar=cw[:, cc, k:k + 1], in1=yv[:], op0=ALU.mult, op1=ALU.add),
                            reads=[cu.r, cw.r, yv.r], writes=[yv.r])
                    kb.op("dve", lambda e, cc=cc, pbb_c=pbb_c: e.tensor_tensor(ob[:, cc, :], yv[:], pbb_c[:, 0:TT], op=ALU.mult),
                          reads=[yv.r, pbb_c.r], writes=[ob.r])
                kw, vw = kwin[slot[t]], vwin[slot[t]]
                for qb in range(2):
                    for kv in range(2):
                        ks = slice(kv * 64, (kv + 1) * 64)
                        chunks = [(kactx[ks, ci * 128:(ci + 1) * 128], kactx.r, vactx[:, ci, kv, :], vactx.r, None) for ci in range(2)]
                        if lat:
                            for wi in (qb, qb + 1, qb + 2):
                                m_ap = None
                                if wi == qb:
                                    m_ap = wm[:, 2 if (t == 0 and qb == 0) else 0, :]
                                elif wi == qb + 2:
                                    m_ap = wm[:, 3 if (t == NLT - 1 and qb == 1) else 1, :]
                                chunks.append((kw[ks, wi * 128:(wi + 1) * 128], kw.r, vw[:, wi, kv, :], vw.r, m_ap))
                        epi = self.attn_group(at, pss, po, chunks, qaT[ks, :, qb * 128:(qb + 1) * 128], qaT.r,
                                              oaT[:, kv * 4:(kv + 1) * 4, qb * 128:(qb + 1) * 128], oaT.r,
                                              sinkrow=sinkrow[64:65, kv, :])
                        epi()
                pool6 = [pp[0], pp[1], pss[0], pss[1], po, at["pbb"]]
                pc_ = [0]

                def nxt():
                    pc_[0] += 1
                    return pool6[pc_[0] % 6]

                for oc in range(8):
                    i2 = oc % 2
                    for gi_, gbuf in ((0, gA[i2]), (1, gB[i2]), (2, gC[i2])):
                        pt = nxt()
                        proj(pt, 0, wgt, (gi_ * 8 + oc) * 128, 1, TT + 1)
                        kb.op("act", lambda e, pt=pt, gbuf=gbuf, gi_=gi_, oc=oc: e.activation(
                            out=gbuf[:], in_=pt[:, 0:TT], func=AF.Sigmoid, bias=bgt[:, gi_ * 8 + oc: gi_ * 8 + oc + 1], scale=1.0),
                            reads=[pt.r, bgt.r], writes=[gbuf.r])
                    kb.dma("sp", gcd[:, oc, t * TT:(t + 1) * TT], gC[i2][:], reads=[gC[i2].r], own=gC[i2].r)
                    pa_ = nxt()
                    for hd in range(8):
                        kb.op("pe", lambda e, hd=hd, oc=oc, pa_=pa_: e.matmul(pa_[:, 0:TT], wpa[:, hd, oc * 128:(oc + 1) * 128], oaT[:, hd, :],
                                                                            start=(hd == 0), stop=(hd == 7)),
                              reads=[wpa.r, oaT.r], writes=[pa_.r])
                    pb_ = nxt()
                    for cc in range(4):
                        kb.op("pe", lambda e, cc=cc, oc=oc, pb_=pb_: e.matmul(pb_[:, 0:TT], wpb[:, cc, oc * 128:(oc + 1) * 128], ob[:, cc, :],
                                                                            start=(cc == 0), stop=(cc == 3)),
                              reads=[wpb.r, ob.r], writes=[pb_.r])
                    kb.op("dve", lambda e, i2=i2, pa_=pa_: e.tensor_tensor(a1[i2][:], pa_[:, 0:TT], gA[i2][:], op=ALU.mult),
                          reads=[pa_.r, gA[i2].r], writes=[a1[i2].r])
                    kb.op("dve", lambda e, i2=i2, pb_=pb_: e.tensor_tensor(a2[i2][:], pb_[:, 0:TT], gB[i2][:], op=ALU.mult),
                          reads=[pb_.r, gB[i2].r], writes=[a2[i2].r])
                    kb.op("pool", lambda e, i2=i2: e.tensor_tensor(accb[i2][:], a1[i2][:], a2[i2][:], op=ALU.add),
                          reads=[a1[i2].r, a2[i2].r], writes=[accb[i2].r])
                    kb.dma("sp", accd[:, oc, t * TT:(t + 1) * TT], accb[i2][:], reads=[accb[i2].r], own=accb[i2].r)
            kb.phase_end()

    def phase_B2(self, l, xsrc, xdst):
        nc, kb, d, c = self.nc, self.kb, self.d, self.c
        xs = xsrc.rearrange("(c p) t -> p c t", p=128)
        xd = xdst.rearrange("(c p) t -> p c t", p=128)
        accd = d["ACC"].rearrange("(c p) t -> p c t", p=128)
        gcd = d["GC"].rearrange("(c p) t -> p c t", p=128)
        qcd = d["QC"].rearrange("(c p) t -> p c t", p=128)
        with ExitStack() as ph:
            kcall = T(kb, ph, "kcall", [128, KALL], BF16)
            vcall = T(kb, ph, "vcall", [128, NKC + 1, 2, 65], BF16)
            kb.op("dve", lambda e: e.memset(vcall[:].rearrange("p a b c -> p (a b c)"), 1.0), writes=[vcall.r])
            par = l % 2
            gkc, gvc = d[f"gkc{par}"], d[f"gvc{par}"]
            kb.dma("sp", kcall[:, 0:CTX], d[f"kcc{par}"][:, :], writes=[kcall.r], own=kcall.r)
            self.load_v("act", vcall, 0, 2, d[f"vcc{par}"][:, :])
            for i in range(4):
                for hh in range(2):
                    c0 = hh * 2048
                    kb.dma("sp", kcall[:, CTX + i * TOK + c0: CTX + i * TOK + c0 + 2048], gkc[i * 128:(i + 1) * 128, c0:c0 + 2048],
                           writes=[kcall.r], own=kcall.r)
                for k0 in range(0, 32, 4):
                    self.load_v("act", vcall, 2 + i * 32 + k0, 4, gvc[i * TOK + k0 * 128: i * TOK + (k0 + 4) * 128, :])
            wpc = T(kb, ph, "wpc", [64, 8, D], BF16)
            self.load_w(wpc, d["S_w_pc"][l].rearrange("(h dd) n -> dd h n", dd=64), 8, 4)
            wo = T(kb, ph, "wo", [128, 8, D], BF16)
            self.load_w(wo, d["S_w_o"][l].rearrange("(c p) n -> p c n", p=128), 8, 4)
            self.precast(self.next_key(l, "B2"), after=[kcall, vcall, wpc, wo])
            xb = [T(kb, ph, f"xb2{i}", [128, 8, TT], F32) for i in range(2)]
            qcb = [T(kb, ph, f"qcb{i}", [128, 4, TT], BF16) for i in range(2)]
            acb = [T(kb, ph, f"acb{i}", [128, 8, TT], F32) for i in range(2)]
            gcb = [T(kb, ph, f"gcb{i}", [128, 8, TT], F32) for i in range(2)]
            ocT = T(kb, ph, "ocT", [64, 8, TT], BF16)
            m1 = [T(kb, ph, f"m1{i}", [128, TT], F32) for i in range(2)]
            mrg = T(kb, ph, "mrg", [128, 8, TT], BF16)
            st = {"cnt": 0, "vr": vcall.r,
                  "dsb": T(kb, ph, "dsb2", [128, 2, 512], F32), "rbs": T(kb, ph, "rbs2", [64, 512], F32),
                  "pbb": T(kb, ph, "pbb2", [128, 512], F32, psum=True)}
            vflat = vcall[:].rearrange("p a b c -> p (a b c)")
            pTp = [T(kb, ph, f"pTp{i}", [128, 2, 512], BF16) for i in range(3)]
            psp = [T(kb, ph, f"psp{i}", [128, 2, 512], F32, psum=True) for i in range(2)]
            po = T(kb, ph, "po2", [128, 2, 512], F32, psum=True)
            pp = [psp[0][:, 0, :], psp[1][:, 0, :]]
            ppr = [psp[0].r, psp[1].r]

            def load_t(t):
                i = t % 2
                kb.dma("sp", xb[i][:], xs[:, :, t * TT:(t + 1) * TT], writes=[xb[i].r], own=xb[i].r)
                kb.dma("sp", qcb[i][:], qcd[:, :, t * TT:(t + 1) * TT], writes=[qcb[i].r], own=qcb[i].r)
                kb.dma("sp", acb[i][:], accd[:, :, t * TT:(t + 1) * TT], writes=[acb[i].r], own=acb[i].r)
                kb.dma("sp", gcb[i][:], gcd[:, :, t * TT:(t + 1) * TT], writes=[gcb[i].r], own=gcb[i].r)

            load_t(0)
            g = 0
            pend = None
            for t in range(NTILE):
                col = 1 if t == NLT else 0
                if t + 1 < NTILE:
                    load_t(t + 1)
                i = t % 2
                x, qc, ac, gc = xb[i], qcb[i], acb[i], gcb[i]
                nkc = NKC if col == 0 else 2
                for qb in range(2):
                    pend = self.attn_pair(st, psp, pTp, po, kcall, vflat, nkc, qc, qb, ocT, prev_epi=pend)
                pend()
                pend = None
                for oc in range(8):
                    pt, ptr = pp[oc % 2], ppr[oc % 2]
                    for hd in range(8):
                        kb.op("pe", lambda e, hd=hd, oc=oc, pt=pt: e.matmul(pt[:, 0:TT], wpc[:, hd, oc * 128:(oc + 1) * 128], ocT[:, hd, :],
                                                                          start=(hd == 0), stop=(hd == 7)),
                              reads=[wpc.r, ocT.r], writes=[ptr])
                    mm = m1[oc % 2]
                    kb.op("dve", lambda e, pt=pt, mm=mm, oc=oc: e.tensor_tensor(mm[:], pt[:, 0:TT], gc[:, oc, :], op=ALU.mult),
                          reads=[ptr, gc.r], writes=[mm.r])
                    kb.op("pool", lambda e, mm=mm, oc=oc: e.tensor_tensor(mrg[:, oc, :], mm[:], ac[:, oc, :], op=ALU.add),
                          reads=[mm.r, ac.r], writes=[mrg.r])
                for oc in range(8):
                    pt, ptr = pp[oc % 2], ppr[oc % 2]
                    for ch in range(8):
                        kb.op("pe", lambda e, ch=ch, oc=oc, pt=pt: e.matmul(pt[:, 0:TT], wo[:, ch, oc * 128:(oc + 1) * 128], mrg[:, ch, :],
                                                                          start=(ch == 0), stop=(ch == 7)),
                              reads=[wo.r, mrg.r], writes=[ptr])
                    kb.op("dve", lambda e, pt=pt, oc=oc: e.scalar_tensor_tensor(
                        out=x[:, oc, :], in0=pt[:, 0:TT], scalar=self.mod("modG", l, 1, oc, col), in1=x[:, oc, :],
                        op0=ALU.mult, op1=ALU.add), reads=[ptr, x.r, c["modG"].r], writes=[x.r])
                kb.dma("sp", xd[:, :, t * TT:(t + 1) * TT], x[:], reads=[x.r], own=x.r)
            kb.phase_end()


def _rope_tables_np(pos0):
    pos = np.arange(pos0, pos0 + TOK)
    row = (pos // 64).astype(np.float32)
    colp = (pos % 64).astype(np.float32)
    half = 32
    inv = (np.float32(10000.0) ** (-np.arange(0, half, 2, dtype=np.float32) / np.float32(half))).astype(np.float32)
    C = np.zeros((128, TOK), np.float32)
    S = np.zeros((128, TOK), np.float32)
    for p in range(128):
        dd = p % 64
        blk, within = dd // 32, dd % 32
        i = within % 16
        first = within < 16
        ang = ((row if blk == 0 else colp) * inv[i]).astype(np.float32)
        C[p] = np.cos(ang)
        S[p] = -np.sin(ang) if first else np.sin(ang)
    return C, S


def _perm_matrix():
    Pm = np.zeros((128, 128), np.float32)
    for m in range(128):
        dd = m % 64
        within = dd % 32
        partner = m + 16 if within < 16 else m - 16
        Pm[partner, m] = 1.0
    return Pm


_QPERM = np.concatenate([np.concatenate([np.arange(j * 64, (j + 1) * 64), np.arange((4 + j) * 64, (5 + j) * 64)])
                         for j in range(4)])


def _prep_common(inp):
    f = lambda a: np.ascontiguousarray(np.asarray(a, dtype=np.float32))
    w_in = f(inp["w_in"]).copy()
    w_in[:, :, 512:1024] = w_in[:, :, 512 + _QPERM]
    w_in[:, :, 1024:1536] = w_in[:, :, 1024 + _QPERM]
    com = {
        "w_ada": f(inp["w_ada"]),
        "b_ada": f(np.asarray(inp["b_ada"]).reshape(DEPTH, 72, 128).transpose(0, 2, 1)),
        "norm_g": f(np.asarray(inp["norm_g"]).reshape(DEPTH, 3, 8, 128).transpose(0, 3, 1, 2)),
        "ident": np.eye(128, dtype=np.float32),
        "permm": _perm_matrix(),
        "qk_g": f(np.tile(np.asarray(inp["qk_g"]), (1, 1, 2)).transpose(0, 2, 1)),
        "w_in": w_in,
        "ffn_w_gate": f(inp["ffn_w_gate"]), "ffn_w_up": f(inp["ffn_w_up"]), "ffn_w_down": f(inp["ffn_w_down"]),
        "sink_a": f(np.asarray(inp["sink_a"]).reshape(DEPTH, 1, 8)),
        "conv_w": f(np.asarray(inp["conv_w"]).reshape(DEPTH, 3, 4, 128).transpose(0, 3, 2, 1)),
        "w_pa": f(inp["w_pa"]), "w_pb": f(inp["w_pb"]), "w_pc": f(inp["w_pc"]),
        "w_gate": f(inp["w_gate"]),
        "b_gate": f(np.asarray(inp["b_gate"]).reshape(DEPTH, 24, 128).transpose(0, 2, 1)),
        "w_o": f(inp["w_o"]),
    }
    return com


def _prep_core(inp, r):
    b, q = r // 4, r % 4
    x = np.asarray(inp["x"], dtype=np.float32)
    ctx = np.asarray(inp["ctx"], dtype=np.float32)
    xT = np.concatenate([x[b, q * TOK:(q + 1) * TOK].T, ctx[b].T], axis=1)
    cvec = np.stack([np.asarray(inp["c"], np.float32)[b], np.asarray(inp["c_ctx"], np.float32)], axis=1)
    cvec = cvec.reshape(8, 128, 2).transpose(1, 0, 2)
    C, S = _rope_tables_np(q * TOK)
    kk = np.arange(128)[:, None]
    qq = np.arange(128)[None, :]
    mprev = np.where(kk >= qq, 0.0, NEG).astype(np.float32)
    mnext = np.where(kk <= qq, 0.0, NEG).astype(np.float32)
    allneg = np.full((128, 128), NEG, np.float32)
    wm = np.stack([np.tile(m, (1, 4)) for m in (mprev, mnext, mprev if q > 0 else allneg, mnext if q < 3 else allneg)], axis=1)
    emask = np.zeros((128, 2), np.float32)
    emask[:, 0] = 1.0 if q > 0 else 0.0
    emask[:, 1] = 1.0 if q < 3 else 0.0
    sel = np.zeros((128, 8), np.float32)
    if q > 0:
        sel[:, q - 1] = 1.0
    if q < 3:
        sel[:, 4 + q + 1] = 1.0
    return {"xin": np.ascontiguousarray(xT), "cvec": np.ascontiguousarray(cvec), "ropeC": C, "ropeS": S,
            "wmask": np.ascontiguousarray(wm), "emask": emask, "sel": sel}


_PROGS = {}


def _get_prog(phases, fused=False, nlayers=1):
    key = (tuple(phases), fused, nlayers)
    if key not in _PROGS:
        _PROGS[key] = Prog(set(phases), fused, nlayers).build()
    return _PROGS[key]


def _run(prog, maps):
    maps = [{k: m[k] for k in prog.in_names} for m in maps]
    res = run_bass_kernel_spmd(prog.nc, maps, core_ids=list(range(NCORE)))
    return res.results


def _layer_slice(com, l, which=None):
    out = {}
    for k, v in com.items():
        if k in ("ident", "permm"):
            out[k] = v
        elif k.startswith("ffn_") and which is not None:
            out[k] = np.ascontiguousarray(v[l:l + 1, which:which + 1])
        else:
            out[k] = np.ascontiguousarray(v[l:l + 1])
    return out


def kernel(**inp):
    com = _prep_common(inp)
    prog = _get_prog(["A", "X", "B1", "B2", "C"], fused=True, nlayers=DEPTH)
    maps = []
    for r in range(NCORE):
        m = dict(com, **_prep_core(inp, r))
        maps.append(m)
    res = _run(prog, maps)
    out = np.zeros((2, SEQ, D), np.float32)
    for r in range(NCORE):
        b, q = r // 4, r % 4
        out[b, q * TOK:(q + 1) * TOK] = res[r]["xw"][:, :TOK].T
    return out
```

```python
import numpy as np
import ml_dtypes
from contextlib import ExitStack
import concourse.bass as bass
import concourse.mybir as mybir
from concourse.bass_utils import run_bass_kernel_spmd

F32 = mybir.dt.float32
BF16 = mybir.dt.bfloat16
AF = mybir.ActivationFunctionType
ALU = mybir.AluOpType

D = 1024
DFF = 2816
NJ = DFF // 128
DEPTH = 4
SEQ = 16384
NCORE = 8
TOK = 4096
CTX = 256
TT = 256
NLT = TOK // TT
NTILE = NLT + 1
TCOLS = TOK + CTX
EPS = 1e-6
SCALE = 0.125
NEG = -1e30
KH = CTX + 128 + TOK + 128
KALL = CTX + SEQ
NKC = KALL // 128


class Res:
    __slots__ = ("name", "w", "r", "dsem")

    def __init__(self, name):
        self.name = name
        self.w = {}
        self.r = {}
        self.dsem = None


class _Eng:
    def __init__(self, name, e, sem):
        self.name, self.e, self.sem, self.cnt, self.seen = name, e, sem, 0, {}


class _DSem:
    def __init__(self, key, sem):
        self.key, self.sem, self.cnt = key, sem, 0


class KB:
    def __init__(self, nc, es):
        self.nc, self.es = nc, es
        self.engs = {}
        for name, e in (("pe", nc.tensor), ("act", nc.scalar), ("dve", nc.vector),
                        ("pool", nc.gpsimd), ("sp", nc.sync)):
            self.engs[name] = _Eng(name, e, es.enter_context(nc.semaphore("s_" + name)))
        self.dsems = []
        self.free_dsems = []
        self.phase_dsems = []
        self.nres = 0

    def res(self, name="r"):
        self.nres += 1
        return Res(f"{name}{self.nres}")

    def _dsem(self, r):
        if r.dsem is None:
            if self.free_dsems:
                r.dsem = self.free_dsems.pop()
            else:
                r.dsem = _DSem("d_" + r.name, self.es.enter_context(self.nc.semaphore("d_" + r.name)))
                self.dsems.append(r.dsem)
            self.phase_dsems.append(r.dsem)
        return r.dsem

    def phase_end(self):
        self.barrier()
        self.free_dsems.extend(self.phase_dsems)
        self.phase_dsems = []

    def _deps(self, E, reads, writes):
        toks = {}
        for r in reads:
            for k, t in r.w.items():
                if k not in toks or toks[k][1] < t[1]:
                    toks[k] = t
        for w in writes:
            for dct in (w.w, w.r):
                for k, t in dct.items():
                    if k not in toks or toks[k][1] < t[1]:
                        toks[k] = t
        for k, (sem, val) in toks.items():
            if k == E.name or E.seen.get(k, 0) >= val:
                continue
            E.e.wait_ge(sem, val)
            E.seen[k] = val

    def op(self, eng, fn, reads=(), writes=()):
        E = self.engs[eng]
        self._deps(E, reads, writes)
        ins = fn(E.e)
        E.cnt += 1
        ins.then_inc(E.sem, 1)
        tok = (E.sem, E.cnt)
        for r in reads:
            r.r[E.name] = tok
        for w in writes:
            w.w[E.name] = tok
        return ins

    def dma(self, q, out, in_, reads=(), writes=(), own=None):
        E = self.engs[q]
        self._deps(E, reads, writes)
        ds = self._dsem(own)
        ins = E.e.dma_start(out=out, in_=in_)
        ds.cnt += 16
        ins.then_inc(ds.sem, 16)
        tok = (ds.sem, ds.cnt)
        for r in reads:
            r.r[ds.key] = tok
        for w in writes:
            w.w[ds.key] = tok

    def barrier(self):
        toks = {E.name: (E.sem, E.cnt) for E in self.engs.values() if E.cnt > 0}
        for ds in self.dsems:
            if ds.cnt > 0:
                toks[ds.key] = (ds.sem, ds.cnt)
        for E in self.engs.values():
            for k, (sem, val) in toks.items():
                if k == E.name or E.seen.get(k, 0) >= val:
                    continue
                E.e.wait_ge(sem, val)
                E.seen[k] = val


class T:
    def __init__(self, kb, es, name, shape, dtype, psum=False):
        nc = kb.nc
        kb.nres += 1
        self.t = es.enter_context((nc.psum_tensor if psum else nc.sbuf_tensor)(f"{name}_{kb.nres}", shape, dtype))
        self.r = kb.res(name)

    def __getitem__(self, idx):
        return self.t[idx]


class Prog:
    def __init__(self, phases, fused=False, nlayers=1):
        self.phases = phases
        self.fused = fused
        self.nl = nlayers
        self.nc = bass.Bass("TRN2", target_bir_lowering=False)
        self.in_names = []
        self.out_names = []

    def din(self, name, shape, dt=F32):
        self.in_names.append(name)
        return self.nc.dram_tensor(name, list(shape), dt, kind="ExternalInput").ap()

    def dout(self, name, shape, dt=F32):
        self.out_names.append(name)
        return self.nc.dram_tensor(name, list(shape), dt, kind="ExternalOutput").ap()

    def dint(self, name, shape, dt=F32):
        return self.nc.dram_tensor(name, list(shape), dt, kind="Internal").ap()

    def build(self):
        nc = self.nc
        L = self.nl
        P = self.phases
        with ExitStack() as es:
            kb = self.kb = KB(nc, es)
            d = self.d = {}
            d["cvec"] = self.din("cvec", [128, 8, 2])
            d["w_ada"] = self.din("w_ada", [L, D, 9 * D])
            d["b_ada"] = self.din("b_ada", [L, 128, 72])
            d["norm_g"] = self.din("norm_g", [L, 128, 3, 8])
            d["ident"] = self.din("ident", [128, 128])
            d["permm"] = self.din("permm", [128, 128])
            d["ropeC"] = self.din("ropeC", [128, TOK])
            d["ropeS"] = self.din("ropeS", [128, TOK])
            d["qk_g"] = self.din("qk_g", [L, 128, 4])
            d["w_in"] = self.din("w_in", [L, D, 3072])
            self.ffn_which = [w for w, p in ((0, "A"), (1, "C")) if p in P]
            if self.ffn_which:
                nw = len(self.ffn_which)
                d["wg"] = self.din("ffn_w_gate", [L, nw, D, DFF])
                d["wu"] = self.din("ffn_w_up", [L, nw, D, DFF])
                d["wd"] = self.din("ffn_w_down", [L, nw, DFF, D])
            if "B1" in P or "B2" in P:
                d["sink"] = self.din("sink_a", [L, 1, 8])
                d["conv_w"] = self.din("conv_w", [L, 128, 4, 3])
                d["w_pa"] = self.din("w_pa", [L, 512, D])
                d["w_pb"] = self.din("w_pb", [L, 512, D])
                d["w_pc"] = self.din("w_pc", [L, 512, D])
                d["w_gate"] = self.din("w_gate", [L, D, 3 * D])
                d["b_gate"] = self.din("b_gate", [L, 128, 24])
                d["w_o"] = self.din("w_o", [L, D, D])
                d["wmask"] = self.din("wmask", [128, 4, 512])
                d["emask"] = self.din("emask", [128, 2])
            d["xin"] = self.din("xin", [D, TCOLS])
            d["xw"] = self.dout("xw", [D, TCOLS])
            if self.fused:
                d["sel"] = self.din("sel", [128, 8])
                d["xedge_dummy"] = None
                for par in range(2):
                    d[f"kaT{par}"] = self.dint(f"kaT{par}", [128, TCOLS], BF16)
                    d[f"va{par}"] = self.dint(f"va{par}", [TCOLS, 128], BF16)
                    d[f"kae{par}"] = self.dint(f"kae{par}", [128, 256], BF16)
                    d[f"vae{par}"] = self.dint(f"vae{par}", [256, 128], BF16)
                    d[f"kcl{par}"] = self.dint(f"kcl{par}", [128, TOK], BF16)
                    d[f"vcl{par}"] = self.dint(f"vcl{par}", [TOK, 128], BF16)
                    d[f"kcc{par}"] = self.dint(f"kcc{par}", [128, CTX], BF16)
                    d[f"vcc{par}"] = self.dint(f"vcc{par}", [CTX, 128], BF16)
                    d[f"xe{par}"] = self.dint(f"xe{par}", [128, 16])
                    d[f"gkae{par}"] = self.dint(f"gkae{par}", [4 * 128, 256], BF16)
                    d[f"gvae{par}"] = self.dint(f"gvae{par}", [4 * 256, 128], BF16)
                    d[f"gkc{par}"] = self.dint(f"gkc{par}", [4 * 128, TOK], BF16)
                    d[f"gvc{par}"] = self.dint(f"gvc{par}", [4 * TOK, 128], BF16)
                    d[f"gxe{par}"] = self.dint(f"gxe{par}", [4 * 128, 16])
                    d[f"syn{par}"] = self.dint(f"syn{par}", [128, 16])
                    d[f"gsyn{par}"] = self.dint(f"gsyn{par}", [4 * 128, 16])
                for nm, shp in (("wg", [L, 2, D, DFF]), ("wu", [L, 2, D, DFF]), ("wd", [L, 2, DFF, D]), ("w_in", [L, D, 3072]),
                                ("w_gate", [L, D, 3 * D]), ("w_pa", [L, 512, D]), ("w_pb", [L, 512, D]), ("w_pc", [L, 512, D]),
                                ("w_o", [L, D, D])):
                    d["S_" + nm] = self.dint("S_" + nm, shp, BF16)
                d["QC"] = self.dint("QC", [512, TCOLS], BF16)
                d["ACC"] = self.dint("ACC", [D, TCOLS])
                d["GC"] = self.dint("GC", [D, TCOLS])

            c = self.c = {}
            c["ones"] = T(kb, es, "ones", [128, 128], BF16)
            c["bones"] = T(kb, es, "bones", [128, 128], BF16)
            c["onesf"] = T(kb, es, "onesf", [128, 128], F32)
            c["ident"] = T(kb, es, "identb", [128, 128], BF16)
            c["permm"] = T(kb, es, "permb", [128, 128], BF16)
            c["qkg"] = T(kb, es, "qkg", [128, L, 4], F32)
            c["epsc"] = T(kb, es, "epsc", [128, 1], F32)
            c["modA"] = T(kb, es, "modA", [128, L * 3 * 8 * 2], F32)
            c["modB"] = T(kb, es, "modB", [128, L * 3 * 8 * 2], F32)
            c["modG"] = T(kb, es, "modG", [128, L * 3 * 8 * 2], F32)
            kb.op("dve", lambda e: e.memset(c["ones"][:], 1.0), writes=[c["ones"].r])
            kb.op("dve", lambda e: e.memset(c["bones"][:], 0.0), writes=[c["bones"].r])
            kb.op("dve", lambda e: e.memset(c["bones"][0:64, 0:64], 1.0), writes=[c["bones"].r])
            kb.op("dve", lambda e: e.memset(c["bones"][64:128, 64:128], 1.0), writes=[c["bones"].r])
            kb.op("dve", lambda e: e.memset(c["onesf"][:], 1.0), writes=[c["onesf"].r])
            kb.op("dve", lambda e: e.memset(c["epsc"][:], EPS), writes=[c["epsc"].r])
            kb.dma("pool", c["ident"][:], d["ident"][:, :], writes=[c["ident"].r], own=c["ident"].r)
            kb.dma("pool", c["permm"][:], d["permm"][:, :], writes=[c["permm"].r], own=c["permm"].r)
            kb.dma("sp", c["qkg"][:], d["qk_g"].rearrange("l p f -> p l f"), writes=[c["qkg"].r], own=c["qkg"].r)

            kb.barrier()
            kb.phase_dsems = []
            self.precast((0, "A"))
            for l in range(L):
                self.phase_M(l)
            xsrc = d["xin"]
            for l in range(L):
                if "A" in P:
                    self.phase_ffn(l, 0, xsrc, d["xw"], with_kv=True)
                    xsrc = d["xw"]
                if "X" in P:
                    self.phase_X(l)
                if "B1" in P:
                    self.phase_B1(l, xsrc)
                if "B2" in P:
                    self.phase_B2(l, xsrc, d["xw"])
                    xsrc = d["xw"]
                if "C" in P:
                    self.phase_ffn(l, 1, xsrc, d["xw"], with_kv=False)
                    xsrc = d["xw"]
            kb.phase_end()
        return self

    def precast(self, key, after=()):
        if key is None:
            return
        after = [t.r for t in after]
        l, p = key
        kb, d = self.kb, self.d
        if p == "A":
            items = [("wg", (l, 0)), ("wu", (l, 0)), ("wd", (l, 0)), ("w_in", (l,))]
        elif p == "B1":
            items = [("w_gate", (l,)), ("w_pa", (l,)), ("w_pb", (l,))]
        elif p == "B2":
            items = [("w_pc", (l,)), ("w_o", (l,))]
        else:
            items = [("wg", (l, 1)), ("wu", (l, 1)), ("wd", (l, 1))]
        for nm, idx in items:
            src, dst = d[nm], d["S_" + nm]
            for i in idx:
                src, dst = src[i], dst[i]
            r = kb.res("cast")
            rows = src.shape[0]
            h = rows // 2
            for a, b in ((0, h), (h, rows)):
                kb.dma("pool", dst[a:b, :], src[a:b, :], reads=after, own=r)

    def next_key(self, l, p):
        seq = [(ll, pp) for ll in range(self.nl) for pp in ("A", "B1", "B2", "C")]
        i = seq.index((l, p))
        return seq[i + 1] if i + 1 < len(seq) else None

    def mod(self, which, l, n, ch, col):
        i = ((l * 3 + n) * 8 + ch) * 2 + col
        return self.c[which][:, i:i + 1]

    def phase_M(self, l):
        nc, kb, d, c = self.nc, self.kb, self.d, self.c
        with ExitStack() as ph:
            cv = T(kb, ph, "cv", [128, 8, 2], F32)
            sc = T(kb, ph, "sc", [128, 8, 2], F32)
            sg = T(kb, ph, "sgm", [128, 8, 2], F32)
            ba = T(kb, ph, "ba", [128, 72], F32)
            ng = T(kb, ph, "ng", [128, 3, 8], F32)
            mt = T(kb, ph, "mt", [128, 72, 2], F32)
            st = [T(kb, ph, f"wst{i}", [128, 8, 1024], F32) for i in range(2)]
            pm = T(kb, ph, "pm", [128, 72, 2], F32, psum=True)
            kb.dma("sp", cv[:], d["cvec"][:, :, :], writes=[cv.r], own=cv.r)
            kb.dma("sp", ba[:], d["b_ada"][l], writes=[ba.r], own=ba.r)
            kb.dma("sp", ng[:], d["norm_g"][l], writes=[ng.r], own=ng.r)
            kb.op("act", lambda e: e.activation(out=sg[:], in_=cv[:], func=AF.Sigmoid), reads=[cv.r], writes=[sg.r])
            kb.op("dve", lambda e: e.tensor_tensor(sc[:], cv[:], sg[:], op=ALU.mult), reads=[cv.r, sg.r], writes=[sc.r])
            wsrc = d["w_ada"][l].rearrange("(kc p) n -> p kc n", p=128)
            for cb in range(9):
                s = st[cb % 2]
                q = "sp" if cb % 2 == 0 else "act"
                kb.dma(q, s[:], wsrc[:, :, cb * 1024:(cb + 1) * 1024], writes=[s.r], own=s.r)
                for o in range(8):
                    oc = cb * 8 + o
                    for kc in range(8):
                        kb.op("pe", lambda e, kc=kc, o=o, oc=oc: e.matmul(
                            pm[:, oc, :], s[:, kc, o * 128:(o + 1) * 128], sc[:, kc, :],
                            start=(kc == 0), stop=(kc == 7)), reads=[s.r, sc.r], writes=[pm.r])
            for col in range(2):
                kb.op("dve", lambda e, col=col: e.tensor_tensor(mt[:, :, col], pm[:, :, col], ba[:], op=ALU.add),
                      reads=[pm.r, ba.r], writes=[mt.r])
            for n in range(3):
                for col in range(2):
                    base = (l * 3 + n) * 16
                    A = c["modA"][:, base + col: base + 16: 2]
                    B = c["modB"][:, base + col: base + 16: 2]
                    G = c["modG"][:, base + col: base + 16: 2]
                    sh = mt[:, (3 * n) * 8:(3 * n + 1) * 8, col]
                    scl = mt[:, (3 * n + 1) * 8:(3 * n + 2) * 8, col]
                    gt = mt[:, (3 * n + 2) * 8:(3 * n + 3) * 8, col]
                    kb.op("dve", lambda e, A=A, scl=scl, n=n: e.scalar_tensor_tensor(
                        out=A, in0=scl, scalar=1.0, in1=ng[:, n, :], op0=ALU.add, op1=ALU.mult),
                        reads=[mt.r, ng.r], writes=[c["modA"].r])
                    kb.op("dve", lambda e, B=B, sh=sh: e.tensor_copy(B, sh), reads=[mt.r], writes=[c["modB"].r])
                    kb.op("dve", lambda e, G=G, gt=gt, n=n: e.tensor_scalar(
                        G, gt, 1.0 if n == 1 else 0.5, None, op0=ALU.mult), reads=[mt.r], writes=[c["modG"].r])
            kb.phase_end()

    def emit_norm(self, ph_t, xt, xr, width, l, n, col, hout, hr, xoff=0):
        kb, c = self.kb, self.c
        sq, pss, rs, tmp = ph_t["sq"], ph_t["pss"], ph_t["rs"], ph_t["tmp"]
        W = width
        kb.op("act", lambda e: e.activation(out=sq[:, :, 0:W], in_=xt[:, :, xoff:xoff + W], func=AF.Square),
              reads=[xr], writes=[sq.r])
        for ch in range(8):
            kb.op("pe", lambda e, ch=ch: e.matmul(pss[:, 0:W], c["ones"][:], sq[:, ch, 0:W],
                                                 start=(ch == 0), stop=(ch == 7)),
                  reads=[sq.r, c["ones"].r], writes=[pss.r])
        kb.op("act", lambda e: e.activation(out=rs[:, 0:W], in_=pss[:, 0:W], func=AF.Ln,
                                            bias=c["epsc"][:, 0:1], scale=1.0 / D),
              reads=[pss.r, c["epsc"].r], writes=[rs.r])
        kb.op("act", lambda e: e.activation(out=rs[:, 0:W], in_=rs[:, 0:W], func=AF.Exp, scale=-0.5),
              reads=[rs.r], writes=[rs.r])
        for ch in range(8):
            tb = tmp[ch % 2]
            kb.op("dve", lambda e, ch=ch, tb=tb: e.scalar_tensor_tensor(
                out=tb[:, 0:W], in0=xt[:, ch, xoff:xoff + W], scalar=self.mod("modA", l, n, ch, col),
                in1=rs[:, 0:W], op0=ALU.mult, op1=ALU.mult),
                reads=[xr, rs.r, c["modA"].r], writes=[tb.r])
            kb.op("act", lambda e, ch=ch, tb=tb: e.activation(
                out=hout[:, ch, 0:W], in_=tb[:, 0:W], func=AF.Identity,
                bias=self.mod("modB", l, n, ch, col), scale=1.0),
                reads=[tb.r, c["modB"].r], writes=[hr])

    def norm_tiles(self, ph, wmax):
        kb = self.kb
        return {
            "sq": T(kb, ph, "sq", [128, 8, wmax], BF16),
            "pss": T(kb, ph, "pss", [128, 512], F32, psum=True),
            "rs": T(kb, ph, "rs", [128, wmax], F32),
            "tmp": [T(kb, ph, f"ntmp{i}", [128, wmax], F32) for i in range(2)],
        }

    def qk_tiles(self, ph):
        kb = self.kb
        return {
            "sqk": T(kb, ph, "sqk", [128, TT], BF16),
            "psk": T(kb, ph, "psk", [128, 512], F32, psum=True),
            "rk": T(kb, ph, "rk", [128, TT], F32),
            "yf": T(kb, ph, "yf", [128, TT], F32),
            "yb": T(kb, ph, "yb", [128, TT], BF16),
            "t1": T(kb, ph, "t1", [128, TT], F32),
            "t2": T(kb, ph, "t2", [128, TT], F32),
            "ropeC": T(kb, ph, "rpC", [128, TT], F32),
            "ropeS": T(kb, ph, "rpS", [128, TT], F32),
        }

    def load_rope(self, qt, t):
        kb, d = self.kb, self.d
        kb.dma("sp", qt["ropeC"][:], d["ropeC"][:, t * TT:(t + 1) * TT], writes=[qt["ropeC"].r], own=qt["ropeC"].r)
        kb.dma("sp", qt["ropeS"][:], d["ropeS"][:, t * TT:(t + 1) * TT], writes=[qt["ropeS"].r], own=qt["ropeS"].r)

    def emit_qknorm(self, qt, ps, psr, l, gi, rope, out_ap, out_r, split=False):
        kb, c = self.kb, self.c
        g = c["qkg"][:, l, gi:gi + 1]
        kb.op("act", lambda e: e.activation(out=qt["sqk"][:], in_=ps, func=AF.Square), reads=[psr], writes=[qt["sqk"].r])
        kb.op("pe", lambda e: e.matmul(qt["psk"][:, 0:TT], c["bones"][:], qt["sqk"][:], start=True, stop=True),
              reads=[qt["sqk"].r, c["bones"].r], writes=[qt["psk"].r])
        kb.op("act", lambda e: e.activation(out=qt["rk"][:], in_=qt["psk"][:, 0:TT], func=AF.Ln,
                                            bias=c["epsc"][:, 0:1], scale=1.0 / 64),
              reads=[qt["psk"].r, c["epsc"].r], writes=[qt["rk"].r])
        kb.op("act", lambda e: e.activation(out=qt["rk"][:], in_=qt["rk"][:], func=AF.Exp, scale=-0.5),
              reads=[qt["rk"].r], writes=[qt["rk"].r])
        if not rope:
            kb.op("dve", lambda e: e.scalar_tensor_tensor(out=out_ap, in0=ps, scalar=g, in1=qt["rk"][:],
                                                          op0=ALU.mult, op1=ALU.mult),
                  reads=[psr, qt["rk"].r, c["qkg"].r], writes=[out_r])
            return None
        kb.op("dve", lambda e: e.scalar_tensor_tensor(out=qt["yf"][:], in0=ps, scalar=g, in1=qt["rk"][:],
                                                      op0=ALU.mult, op1=ALU.mult),
              reads=[psr, qt["rk"].r, c["qkg"].r], writes=[qt["yf"].r])
        kb.op("act", lambda e: e.copy(qt["yb"][:], qt["yf"][:]), reads=[qt["yf"].r], writes=[qt["yb"].r])
        def fin():
            kb.op("pe", lambda e: e.matmul(qt["psk"][:, TT:2 * TT], c["permm"][:], qt["yb"][:], start=True, stop=True),
                  reads=[qt["yb"].r, c["permm"].r], writes=[qt["psk"].r])
            kb.op("pool", lambda e: e.tensor_tensor(qt["t1"][:], qt["yf"][:], qt["ropeC"][:], op=ALU.mult),
                  reads=[qt["yf"].r, qt["ropeC"].r], writes=[qt["t1"].r])
            kb.op("dve", lambda e: e.tensor_tensor(qt["t2"][:], qt["psk"][:, TT:2 * TT], qt["ropeS"][:], op=ALU.mult),
                  reads=[qt["psk"].r, qt["ropeS"].r], writes=[qt["t2"].r])
            kb.op("dve", lambda e: e.tensor_tensor(out_ap, qt["t1"][:], qt["t2"][:], op=ALU.add),
                  reads=[qt["t1"].r, qt["t2"].r], writes=[out_r])

        if split:
            return fin
        fin()
        return None

    def load_v(self, q, vt, ch0, nch, src_rows):
        for kv in range(2):
            self.kb.dma(q, vt[:, ch0:ch0 + nch, kv, 0:64],
                        src_rows[:, kv * 64:(kv + 1) * 64].rearrange("(ch p) dd -> p ch dd", p=128),
                        writes=[vt.r], own=vt.r)

    def load_w(self, dst, src_ap, nchunk, per=1):
        kb = self.kb
        for n_, k0 in enumerate(range(0, nchunk, per)):
            k1 = min(nchunk, k0 + per)
            kb.dma("sp" if n_ % 2 == 0 else "act", dst[:, k0:k1, :], src_ap[:, k0:k1, :], writes=[dst.r], own=dst.r)

    def phase_ffn(self, l, which, xsrc, xdst, with_kv):
        nc, kb, d, c = self.nc, self.kb, self.d, self.c
        n = 0 if which == 0 else 2
        xs = xsrc.rearrange("(c p) t -> p c t", p=128)
        xd = xdst.rearrange("(c p) t -> p c t", p=128)
        with ExitStack() as ph:
            wg = T(kb, ph, "wg", [128, 8, DFF], BF16)
            wu = T(kb, ph, "wu", [128, 8, DFF], BF16)
            wd = T(kb, ph, "wd", [128, NJ, D], BF16)
            wi_ = self.ffn_which.index(which)
            self.load_w(wg, d["S_wg"][l, which].rearrange("(c p) n -> p c n", p=128), 8, 2)
            self.load_w(wu, d["S_wu"][l, which].rearrange("(c p) n -> p c n", p=128), 8, 2)
            self.load_w(wd, d["S_wd"][l, which].rearrange("(c p) n -> p c n", p=128), NJ, 6)
            self.precast(self.next_key(l, "A" if which == 0 else "C"), after=[wg, wu, wd])
            xb = [T(kb, ph, f"xb{i}", [128, 8, TT], F32) for i in range(2)]
            hb = [T(kb, ph, f"hb{i}", [128, 8, TT], BF16) for i in range(3 if with_kv else 2)]
            act = T(kb, ph, "actb", [128, NJ, TT], BF16)
            sgb = [T(kb, ph, f"sgb{i}", [128, TT], F32) for i in range(2)]
            nt = self.norm_tiles(ph, TT)
            pgu = [T(kb, ph, f"pgu{i}", [128, 2, TT], F32, psum=True) for i in range(2 if with_kv else 4)]
            py = [T(kb, ph, f"py{i}", [128, 512], F32, psum=True) for i in range(2)]
            if with_kv:
                wkv = T(kb, ph, "wkv", [128, 8, 512], BF16)
                self.load_w(wkv, d["S_w_in"][l].rearrange("(c p) n -> p c n", p=128)[:, :, 0:512], 8, 8)
                qt = self.qk_tiles(ph)
                kob = [T(kb, ph, f"kob{i}", [128, TT], BF16) for i in range(2)]
                vob = T(kb, ph, "vob", [128, 2, 2, 128], BF16)
                pkv = T(kb, ph, "pkv", [128, 512], F32, psum=True)
                pv = T(kb, ph, "pvv", [128, 512], F32, psum=True)
                xeb = T(kb, ph, "xeb", [128, 8, 2], F32)

            def load_x(t):
                b = xb[t % 2]
                kb.dma("sp", b[:], xs[:, :, t * TT:(t + 1) * TT], writes=[b.r], own=b.r)

            load_x(0)
            self.emit_norm(nt, xb[0], xb[0].r, TT, l, n, 0, hb[0], hb[0].r)
            for t in range(NTILE):
                col = 1 if t == NLT else 0
                if t + 1 < NTILE:
                    load_x(t + 1)
                x = xb[t % 2]
                h = hb[t % 2]
                for j in range(NJ):
                    p = pgu[j % len(pgu)]
                    for (wi, W_) in ((0, wg), (1, wu)):
                        for ch in range(8):
                            kb.op("pe", lambda e, ch=ch, W_=W_, wi=wi, p=p, j=j: e.matmul(
                                p[:, wi, :], W_[:, ch, j * 128:(j + 1) * 128], h[:, ch, :],
                                start=(ch == 0), stop=(ch == 7)), reads=[W_.r, h.r], writes=[p.r])
                    sg_ = sgb[j % 2]
                    kb.op("act", lambda e, p=p, sg_=sg_: e.activation(out=sg_[:], in_=p[:, 0, :], func=AF.Silu),
                          reads=[p.r], writes=[sg_.r])
                    kb.op("dve", lambda e, p=p, sg_=sg_, j=j: e.tensor_tensor(act[:, j, :], sg_[:], p[:, 1, :], op=ALU.mult),
                          reads=[p.r, sg_.r], writes=[act.r])
                for oc in range(8):
                    if oc == 2 and t + 1 < NTILE:
                        xn, hn = xb[(t + 1) % 2], hb[(t + 1) % 2]
                        self.emit_norm(nt, xn, xn.r, TT, l, n, 1 if t + 1 == NLT else 0, hn, hn.r)
                    p = py[oc % 2]
                    for j in range(NJ):
                        kb.op("pe", lambda e, p=p, j=j, oc=oc: e.matmul(
                            p[:, 0:TT], wd[:, j, oc * 128:(oc + 1) * 128], act[:, j, :],
                            start=(j == 0), stop=(j == NJ - 1)), reads=[wd.r, act.r], writes=[p.r])
                    kb.op("dve", lambda e, p=p, oc=oc: e.scalar_tensor_tensor(
                        out=x[:, oc, :], in0=p[:, 0:TT], scalar=self.mod("modG", l, n, oc, col), in1=x[:, oc, :],
                        op0=ALU.mult, op1=ALU.add), reads=[p.r, x.r, c["modG"].r], writes=[x.r])
                kb.dma("sp", xd[:, :, t * TT:(t + 1) * TT], x[:], reads=[x.r], own=x.r)
                if not with_kv:
                    continue
                if t == 0:
                    kb.op("pool", lambda e: e.tensor_copy(xeb[:, :, 0], x[:, :, 0]), reads=[x.r], writes=[xeb.r])
                if t == NLT - 1:
                    kb.op("pool", lambda e: e.tensor_copy(xeb[:, :, 1], x[:, :, TT - 1]), reads=[x.r], writes=[xeb.r])
                    kb.dma("sp", d[f"xe{l % 2}"][:, :], xeb[:].rearrange("p c j -> p (c j)"), reads=[xeb.r], own=xeb.r)
                h2 = hb[2]
                self.emit_norm(nt, x, x.r, TT, l, 1, col, h2, h2.r)
                if col == 0:
                    self.load_rope(qt, t)
                par = l % 2
                for ki, (c0, gi) in enumerate(((0, 1), (256, 3))):
                    for ch in range(8):
                        kb.op("pe", lambda e, ch=ch, c0=c0: e.matmul(
                            pkv[:, 0:TT], wkv[:, ch, c0:c0 + 128], h2[:, ch, :], start=(ch == 0), stop=(ch == 7)),
                            reads=[wkv.r, h2.r], writes=[pkv.r])
                    ko = kob[ki]
                    self.emit_qknorm(qt, pkv[:, 0:TT], pkv.r, l, gi, col == 0, ko[:], ko.r)
                    if ki == 0:
                        kb.dma("sp", d[f"kaT{par}"][:, t * TT:(t + 1) * TT], ko[:], reads=[ko.r], own=ko.r)
                        if t == 0:
                            kb.dma("sp", d[f"kae{par}"][:, 0:128], ko[:, 0:128], reads=[ko.r], own=ko.r)
                        if t == NLT - 1:
                            kb.dma("sp", d[f"kae{par}"][:, 128:256], ko[:, 128:256], reads=[ko.r], own=ko.r)
                    elif col == 0:
                        kb.dma("sp", d[f"kcl{par}"][:, t * TT:(t + 1) * TT], ko[:], reads=[ko.r], own=ko.r)
                    else:
                        kb.dma("sp", d[f"kcc{par}"][:, :], ko[:], reads=[ko.r], own=ko.r)
                for tb in range(2):
                    for vi, c0 in enumerate((128, 384)):
                        for ch in range(8):
                            kb.op("pe", lambda e, ch=ch, c0=c0, tb=tb, vi=vi: e.matmul(
                                pv[:, (tb * 2 + vi) * 128:(tb * 2 + vi + 1) * 128], h2[:, ch, tb * 128:(tb + 1) * 128],
                                wkv[:, ch, c0:c0 + 128], start=(ch == 0), stop=(ch == 7)),
                                reads=[wkv.r, h2.r], writes=[pv.r])
                kb.op("act", lambda e: e.copy(vob[:].rearrange("p a b c -> p (a b c)"), pv[:, :]), reads=[pv.r], writes=[vob.r])
                vdst = [d[f"va{par}"][t * TT:(t + 1) * TT, :],
                        d[f"vcl{par}"][t * TT:(t + 1) * TT, :] if col == 0 else d[f"vcc{par}"][:, :]]
                for vi, dst in enumerate(vdst):
                    kb.dma("sp", dst.rearrange("(tb p) f -> p tb f", p=128), vob[:, :, vi, :], reads=[vob.r], own=vob.r)
                if t == 0:
                    kb.dma("sp", d[f"vae{par}"][0:128, :], vob[:, 0, 0, :], reads=[vob.r], own=vob.r)
                if t == NLT - 1:
                    kb.dma("sp", d[f"vae{par}"][128:256, :], vob[:, 1, 0, :], reads=[vob.r], own=vob.r)
            kb.phase_end()

    def phase_X(self, l):
        nc, kb, d = self.nc, self.kb, self.d
        par = l % 2
        if not hasattr(self, "_xsems"):
            self._xsems = []
            for nm in ("xbig", "xsmall", "xsync"):
                ds = _DSem(nm, kb.es.enter_context(nc.semaphore(nm)))
                kb.dsems.append(ds)
                self._xsems.append(ds)
        xbig, xsmall, xsync = self._xsems
        kb.barrier()
        rg = [[0, 1, 2, 3], [4, 5, 6, 7]]

        def gather(a, b, ds):
            ins = nc.gpsimd.collective_compute("AllGather", ALU.bypass, replica_groups=rg,
                                               ins=[d[f"{a}{par}"][:, :]], outs=[d[f"{b}{par}"][:, :]])
            ds.cnt += 1
            ins.then_inc(ds.sem, 1)

        gather("kcl", "gkc", xbig)
        gather("vcl", "gvc", xbig)
        kb.barrier()
        for a, b in (("kae", "gkae"), ("vae", "gvae"), ("xe", "gxe")):
            gather(a, b, xsmall)
        kb.barrier()
        gather("syn", "gsyn", xsync)
        kb.barrier()

    def attn_tiles(self, ph, npt):
        kb = self.kb
        return {
            "pT": [T(kb, ph, f"pT{i}", [128, 512], BF16) for i in range(npt)],
            "dsb": T(kb, ph, "dsb", [128, 512], F32),
            "rbs": T(kb, ph, "rbs", [64, 512], F32),
            "pbb": T(kb, ph, "pbb", [128, 512], F32, psum=True),
            "cnt": 0,
        }

    def attn_group(self, at, pss, po, chunks, q_ap, q_r, out_ap, out_r, sinkrow=None, prev_epi=None):
        kb, c = self.kb, self.c
        n = len(chunks)
        LA = min(2, len(pss) - 1)
        bufs = []
        for i in range(n):
            bufs.append((pss[at["cnt"] % len(pss)], at["pT"][at["cnt"] % len(at["pT"])]))
            at["cnt"] += 1

        def emit_S(i):
            k_ap, k_r, v_ap, v_r, m_ap = chunks[i]
            ps = bufs[i][0]
            if m_ap is not None:
                kb.op("pe", lambda e: e.matmul(ps[:, :], c["ident"][:], m_ap, start=True, stop=False),
                      reads=[c["ident"].r, self._wm.r], writes=[ps.r])
            kb.op("pe", lambda e: e.matmul(ps[:, :].rearrange("p (j q) -> p j q", j=4), k_ap, q_ap,
                                           start=(m_ap is None), stop=True), reads=[k_r, q_r], writes=[ps.r])

        for i in range(min(LA, n)):
            emit_S(i)
        if prev_epi is not None:
            prev_epi()
        for i in range(n):
            ps, pT = bufs[i]
            kb.op("act", lambda e, ps=ps, pT=pT: e.activation(out=pT[:], in_=ps[:, :], func=AF.Exp, scale=SCALE),
                  reads=[ps.r], writes=[pT.r])
            if i + LA < n:
                emit_S(i + LA)
            v_ap, v_r = chunks[i][2], chunks[i][3]
            kb.op("pe", lambda e, pT=pT, v_ap=v_ap, i=i: e.matmul(po[0:65, :], v_ap, pT[:], start=(i == 0), stop=(i == n - 1)),
                  reads=[v_r, pT.r], writes=[po.r])

        def epilogue():
            dsb, rbs, pbb = at["dsb"], at["rbs"], at["pbb"]
            if sinkrow is not None:
                kb.op("dve", lambda e: e.tensor_tensor(dsb[64:65, :], po[64:65, :], sinkrow, op=ALU.add),
                      reads=[po.r, self._sinkrow.r], writes=[dsb.r])
            else:
                kb.op("dve", lambda e: e.tensor_copy(dsb[64:65, :], po[64:65, :]), reads=[po.r], writes=[dsb.r])
            kb.op("act", lambda e: e.activation(out=dsb[64:65, :], in_=dsb[64:65, :], func=AF.Ln), reads=[dsb.r], writes=[dsb.r])
            kb.op("act", lambda e: e.activation(out=dsb[64:65, :], in_=dsb[64:65, :], func=AF.Exp, scale=-1.0), reads=[dsb.r], writes=[dsb.r])
            kb.op("pe", lambda e: e.matmul(pbb[0:64, :], c["onesf"][64:65, 0:64], dsb[64:65, :], start=True, stop=True),
                  reads=[dsb.r, c["onesf"].r], writes=[pbb.r])
            kb.op("act", lambda e: e.copy(rbs[:, :], pbb[0:64, :]), reads=[pbb.r], writes=[rbs.r])
            kb.op("dve", lambda e: e.tensor_tensor(out_ap, po[0:64, :].rearrange("p (j q) -> p j q", j=4),
                                                   rbs[:, :].rearrange("p (j q) -> p j q", j=4), op=ALU.mult),
                  reads=[po.r, rbs.r], writes=[out_r])

        return epilogue

    def attn_pair(self, st, psp, pTp, po, kcall, vflat, nkc, qc, qb, ocT, prev_epi=None):
        kb, c = self.kb, self.c
        n = nkc
        bufs = []
        for i in range(n):
            bufs.append((psp[st["cnt"] % len(psp)], pTp[st["cnt"] % len(pTp)]))
            st["cnt"] += 1

        def emit_S(i):
            ps = bufs[i][0]
            for kv in range(2):
                ks = slice(kv * 64, (kv + 1) * 64)
                kb.op("pe", lambda e, kv=kv, ks=ks: e.matmul(
                    ps[:, kv, :].rearrange("p (j q) -> p j q", j=4), kcall[ks, i * 128:(i + 1) * 128],
                    qc[ks, :, qb * 128:(qb + 1) * 128], start=True, stop=True), reads=[kcall.r, qc.r], writes=[ps.r])

        emit_S(0)
        if prev_epi is not None:
            prev_epi()
        for i in range(n):
            ps, pT = bufs[i]
            kb.op("act", lambda e, ps=ps, pT=pT: e.activation(
                out=pT[:].rearrange("p a b -> p (a b)"), in_=ps[:].rearrange("p a b -> p (a b)"), func=AF.Exp, scale=SCALE),
                reads=[ps.r], writes=[pT.r])
            if i + 1 < n:
                emit_S(i + 1)
            for kv in range(2):
                o0 = i * 130 + kv * 65
                kb.op("pe", lambda e, pT=pT, kv=kv, o0=o0, i=i: e.matmul(
                    po[:, kv, :], vflat[:, o0:o0 + 128], pT[:, kv, :], start=(i == 0), stop=(i == n - 1)),
                    reads=[st["vr"], pT.r], writes=[po.r])

        def epilogue():
            dsb, rbs, pbb = st["dsb"], st["rbs"], st["pbb"]
            kb.op("dve", lambda e: e.tensor_copy(dsb[64:65, :, :], po[64:65, :, :]), reads=[po.r], writes=[dsb.r])
            dflat = dsb[64:65, :, :].rearrange("p a b -> p (a b)")
            kb.op("act", lambda e: e.activation(out=dflat, in_=dflat, func=AF.Ln), reads=[dsb.r], writes=[dsb.r])
            kb.op("act", lambda e: e.activation(out=dflat, in_=dflat, func=AF.Exp, scale=-1.0), reads=[dsb.r], writes=[dsb.r])
            for kv in range(2):
                kb.op("pe", lambda e, kv=kv: e.matmul(pbb[0:64, :], c["onesf"][64:65, 0:64], dsb[64:65, kv, :], start=True, stop=True),
                      reads=[dsb.r, c["onesf"].r], writes=[pbb.r])
                kb.op("act", lambda e: e.copy(rbs[:, :], pbb[0:64, :]), reads=[pbb.r], writes=[rbs.r])
                kb.op("dve", lambda e, kv=kv: e.tensor_tensor(
                    ocT[:, kv * 4:(kv + 1) * 4, qb * 128:(qb + 1) * 128], po[0:64, kv, :].rearrange("p (j q) -> p j q", j=4),
                    rbs[:, :].rearrange("p (j q) -> p j q", j=4), op=ALU.mult), reads=[po.r, rbs.r], writes=[ocT.r])

        return epilogue

    def phase_B1(self, l, xsrc):
        nc, kb, d, c = self.nc, self.kb, self.d, self.c
        xs = xsrc.rearrange("(c p) t -> p c t", p=128)
        W2 = TT + 2
        with ExitStack() as ph:
            win = T(kb, ph, "win", [128, 8, 2560], BF16)
            self.load_w(win, d["S_w_in"][l].rearrange("(c p) n -> p c n", p=128)[:, :, 512:3072], 8, 2)
            wgt = T(kb, ph, "wgt", [128, 8, 3072], BF16)
            self.load_w(wgt, d["S_w_gate"][l].rearrange("(c p) n -> p c n", p=128), 8, 2)
            wpa = T(kb, ph, "wpa", [64, 8, D], BF16)
            self.load_w(wpa, d["S_w_pa"][l].rearrange("(h dd) n -> dd h n", dd=64), 8, 4)
            wpb = T(kb, ph, "wpb", [128, 4, D], BF16)
            self.load_w(wpb, d["S_w_pb"][l].rearrange("(c p) n -> p c n", p=128), 4, 4)
            self.precast(self.next_key(l, "B1"), after=[win, wgt, wpa, wpb])
            bgt = T(kb, ph, "bgt", [128, 24], F32)
            cw = T(kb, ph, "cw", [128, 4, 3], F32)
            wm = self._wm = T(kb, ph, "wm", [128, 4, 512], BF16)
            em = T(kb, ph, "em", [128, 2], F32)
            xedge = T(kb, ph, "xedge", [128, 8, 2], F32)
            sk = T(kb, ph, "sk", [128, 8], F32)
            ske = T(kb, ph, "ske", [128, 8], F32)
            sinkrow = self._sinkrow = T(kb, ph, "sinkrow", [128, 2, 512], F32)
            kactx = T(kb, ph, "kactx", [128, CTX], BF16)
            vactx = T(kb, ph, "vactx", [128, 2, 2, 65], BF16)
            kb.dma("sp", bgt[:], d["b_gate"][l], writes=[bgt.r], own=bgt.r)
            kb.dma("sp", cw[:], d["conv_w"][l], writes=[cw.r], own=cw.r)
            kb.dma("pool", wm[:], d["wmask"][:, :, :], writes=[wm.r], own=wm.r)
            kb.dma("sp", em[:], d["emask"][:, :], writes=[em.r], own=em.r)
            par = l % 2
            kaT, va_d = d[f"kaT{par}"], d[f"va{par}"]
            gkae, gvae, gxe = d[f"gkae{par}"], d[f"gvae{par}"], d[f"gxe{par}"]
            sel = T(kb, ph, "sel", [128, 8], F32)
            cand = T(kb, ph, "cand", [128, 4, 4, 128], BF16)
            cx = T(kb, ph, "cx", [128, 4, 16], F32)
            halo = T(kb, ph, "halo", [128, 4, 128], BF16)
            hacc = T(kb, ph, "hacc", [128, 128], F32)
            gk3 = gkae.rearrange("(i p) c -> p i c", p=128)
            gv3 = gvae.rearrange("(i t) f -> t i f", i=4)

            def prep_halos():
                kb.dma("sp", sel[:], d["sel"][:, :], writes=[sel.r], own=sel.r)
                kb.dma("sp", cand[:, 0, :, :], gk3[:, :, 128:256], writes=[cand.r], own=cand.r)
                kb.dma("sp", cand[:, 1, :, :], gk3[:, :, 0:128], writes=[cand.r], own=cand.r)
                kb.dma("sp", cand[:, 2, :, :], gv3[128:256, :, :], writes=[cand.r], own=cand.r)
                kb.dma("sp", cand[:, 3, :, :], gv3[0:128, :, :], writes=[cand.r], own=cand.r)
                kb.dma("sp", cx[:], gxe.rearrange("(i p) f -> p i f", p=128), writes=[cx.r], own=cx.r)
                for w in range(4):
                    so = 0 if w % 2 == 0 else 4
                    kb.op("dve", lambda e, w=w, so=so: e.tensor_scalar(hacc[:], cand[:, w, 0, :], sel[:, so:so + 1], None, op0=ALU.mult),
                          reads=[cand.r, sel.r], writes=[hacc.r])
                    for i in range(1, 4):
                        o_ap = halo[:, w, :] if i == 3 else hacc[:]
                        kb.op("dve", lambda e, w=w, so=so, i=i, o_ap=o_ap: e.scalar_tensor_tensor(
                            out=o_ap, in0=cand[:, w, i, :], scalar=sel[:, so + i:so + i + 1], in1=hacc[:], op0=ALU.mult, op1=ALU.add),
                            reads=[cand.r, sel.r, hacc.r], writes=[halo.r if i == 3 else hacc.r])
                for j_out, so, j_in in ((0, 0, 1), (1, 4, 0)):
                    kb.op("dve", lambda e, j_out=j_out, so=so, j_in=j_in: e.tensor_scalar(
                        xedge[:, :, j_out], cx[:, 0, j_in:16:2], sel[:, so:so + 1], None, op0=ALU.mult),
                        reads=[cx.r, sel.r], writes=[xedge.r])
                    for i in range(1, 4):
                        kb.op("dve", lambda e, j_out=j_out, so=so, j_in=j_in, i=i: e.scalar_tensor_tensor(
                            out=xedge[:, :, j_out], in0=cx[:, i, j_in:16:2], scalar=sel[:, so + i:so + i + 1], in1=xedge[:, :, j_out],
                            op0=ALU.mult, op1=ALU.add), reads=[cx.r, sel.r, xedge.r], writes=[xedge.r])

            kb.dma("sp", sk[64:65, :], d["sink"][l], writes=[sk.r], own=sk.r)
            kb.op("act", lambda e: e.activation(out=ske[64:65, :], in_=sk[64:65, :], func=AF.Exp), reads=[sk.r], writes=[ske.r])
            for hh in range(8):
                kb.op("dve", lambda e, hh=hh: e.tensor_scalar(
                    sinkrow[64:65, hh // 4, (hh % 4) * 128:(hh % 4 + 1) * 128], c["onesf"][64:65, 0:128],
                    ske[64:65, hh:hh + 1], None, op0=ALU.mult), reads=[ske.r, c["onesf"].r], writes=[sinkrow.r])
            kb.dma("sp", kactx[:], kaT[:, TOK:TOK + CTX], writes=[kactx.r], own=kactx.r)
            kb.op("dve", lambda e: e.memset(vactx[:].rearrange("p a b c -> p (a b c)"), 1.0), writes=[vactx.r])
            self.load_v("sp", vactx, 0, 2, va_d[TOK:TOK + CTX, :])

            xh = [T(kb, ph, f"xh{i}", [128, 8, W2], F32) for i in range(2)]
            h = T(kb, ph, "hB", [128, 8, W2], BF16)
            nt = self.norm_tiles(ph, W2)
            qt = self.qk_tiles(ph)
            qaT = T(kb, ph, "qaT", [128, 4, TT], BF16)
            qcT = [T(kb, ph, f"qcT{i}", [128, 4, TT], BF16) for i in range(2)]
            kwin = [T(kb, ph, f"kwin{i}", [128, 512], BF16) for i in range(2)]
            vwin = [T(kb, ph, f"vwin{i}", [128, 4, 2, 65], BF16) for i in range(2)]
            for v in vwin:
                kb.op("dve", lambda e, v=v: e.memset(v[:].rearrange("p a b c -> p (a b c)"), 1.0), writes=[v.r])
            csb = T(kb, ph, "csb", [128, W2], F32)
            cu = T(kb, ph, "cu", [128, W2], F32)
            yv = T(kb, ph, "yv", [128, TT], F32)
            ob = T(kb, ph, "ob", [128, 4, TT], BF16)
            oaT = T(kb, ph, "oaT", [64, 8, TT], BF16)
            gA = [T(kb, ph, f"gA{i}", [128, TT], F32) for i in range(2)]
            gB = [T(kb, ph, f"gB{i}", [128, TT], F32) for i in range(2)]
            gC = [T(kb, ph, f"gC{i}", [128, TT], F32) for i in range(2)]
            a1 = [T(kb, ph, f"a1{i}", [128, TT], F32) for i in range(2)]
            a2 = [T(kb, ph, f"a2{i}", [128, TT], F32) for i in range(2)]
            accb = [T(kb, ph, f"accb{i}", [128, TT], F32) for i in range(2)]
            at = self.attn_tiles(ph, 3)
            pp = [T(kb, ph, f"ppj{i}", [128, 512], F32, psum=True) for i in range(2)]
            pss = [T(kb, ph, f"pss{i}", [128, 512], F32, psum=True) for i in range(2)]
            po = T(kb, ph, "po", [128, 512], F32, psum=True)
            accd = d["ACC"].rearrange("(c p) t -> p c t", p=128)
            gcd = d["GC"].rearrange("(c p) t -> p c t", p=128)
            qcd = d["QC"].rearrange("(c p) t -> p c t", p=128)

            order = list(range(1, NLT - 1)) + [0, NLT - 1, NLT]
            slot = {t: si % 2 for si, t in enumerate(order)}

            def load_x(t):
                b = xh[slot[t]]
                if t == NLT:
                    kb.dma("sp", b[:, :, 1:TT + 1], xs[:, :, TOK:TOK + CTX], writes=[b.r], own=b.r)
                elif t == 0:
                    kb.dma("sp", b[:, :, 1:W2], xs[:, :, 0:TT + 1], writes=[b.r], own=b.r)
                    kb.op("pool", lambda e: e.tensor_copy(b[:, :, 0], xedge[:, :, 0]), reads=[xedge.r], writes=[b.r])
                elif t == NLT - 1:
                    kb.dma("sp", b[:, :, 0:TT + 1], xs[:, :, t * TT - 1:(t + 1) * TT], writes=[b.r], own=b.r)
                    kb.op("pool", lambda e: e.tensor_copy(b[:, :, TT + 1], xedge[:, :, 1]), reads=[xedge.r], writes=[b.r])
                else:
                    kb.dma("sp", b[:], xs[:, :, t * TT - 1:(t + 1) * TT + 1], writes=[b.r], own=b.r)

            def load_kv(t):
                if t >= NLT:
                    return
                kw, vw = kwin[slot[t]], vwin[slot[t]]
                lo, hi = t * TT - 128, t * TT + 384
                if t == 0:
                    kb.dma("sp", kw[:, 128:512], kaT[:, 0:hi], writes=[kw.r], own=kw.r)
                    self.load_v("sp", vw, 1, 3, va_d[0:hi, :])
                    kb.op("pool", lambda e: e.tensor_copy(kw[:, 0:128], halo[:, 0, :]), reads=[halo.r], writes=[kw.r])
                    kb.op("pool", lambda e: e.tensor_copy(vw[:, 0, :, 0:64], halo[:, 2, :].rearrange("p (kv dd) -> p kv dd", kv=2)),
                          reads=[halo.r], writes=[vw.r])
                elif t == NLT - 1:
                    kb.dma("sp", kw[:, 0:384], kaT[:, lo:TOK], writes=[kw.r], own=kw.r)
                    self.load_v("sp", vw, 0, 3, va_d[lo:TOK, :])
                    kb.op("pool", lambda e: e.tensor_copy(kw[:, 384:512], halo[:, 1, :]), reads=[halo.r], writes=[kw.r])
                    kb.op("pool", lambda e: e.tensor_copy(vw[:, 3, :, 0:64], halo[:, 3, :].rearrange("p (kv dd) -> p kv dd", kv=2)),
                          reads=[halo.r], writes=[vw.r])
                else:
                    kb.dma("sp", kw[:], kaT[:, lo:hi], writes=[kw.r], own=kw.r)
                    self.load_v("sp", vw, 0, 4, va_d[lo:hi, :])

            def proj(pt, half, wt, c0, rhs_lo, rhs_hi, hh=h):
                Wd = rhs_hi - rhs_lo
                o0 = half * 256
                for ch in range(8):
                    kb.op("pe", lambda e, ch=ch: e.matmul(pt[:, o0:o0 + Wd], wt[:, ch, c0:c0 + 128], hh[:, ch, rhs_lo:rhs_hi],
                                                         start=(ch == 0), stop=(ch == 7)),
                          reads=[wt.r, hh.r], writes=[pt.r])

            load_x(order[0])
            load_kv(order[0])
            for si, t in enumerate(order):
                col = 1 if t == NLT else 0
                lat = col == 0
                if si + 1 < NTILE:
                    if order[si + 1] == 0:
                        prep_halos()
                    load_x(order[si + 1])
                    load_kv(order[si + 1])
                x = xh[slot[t]]
                self.emit_norm(nt, x, x.r, W2, l, 1, col, h, h.r)
                if lat:
                    self.load_rope(qt, t)
                qc = qcT[slot[t]]
                bank6 = [pp[0], pp[1], pss[0], pss[1], po, at["pbb"]]
                LAq = 4
                for qi in range(LAq):
                    proj(bank6[qi % 6], 0, win, qi * 128, 1, TT + 1)
                for qi in range(8):
                    pt = bank6[qi % 6]
                    if qi < 4:
                        fin = self.emit_qknorm(qt, pt[:, 0:TT], pt.r, l, 0, lat, qaT[:, qi, :], qaT.r, split=True)
                    else:
                        fin = self.emit_qknorm(qt, pt[:, 0:TT], pt.r, l, 2, lat, qc[:, qi - 4, :], qc.r, split=True)
                    if qi + LAq < 8:
                        proj(bank6[(qi + LAq) % 6], 0, win, (qi + LAq) * 128, 1, TT + 1)
                    if fin is not None:
                        fin()
                kb.dma("sp", qcd[:, :, t * TT:(t + 1) * TT], qc[:], reads=[qc.r], own=qc.r)
                for cc in range(4):
                    pcb, pub, pbb_c = bank6[(3 * cc) % 6], bank6[(3 * cc + 1) % 6], bank6[(3 * cc + 2) % 6]
                    proj(pcb, 0, win, 1536 + cc * 128, 0, W2)
                    proj(pub, 0, win, 2048 + cc * 128, 0, W2)
                    proj(pbb_c, 0, win, 1024 + cc * 128, 1, TT + 1)
                    kb.op("act", lambda e, pcb=pcb: e.copy(csb[:], pcb[:, 0:W2]), reads=[pcb.r], writes=[csb.r])
                    kb.op("dve", lambda e, pub=pub: e.tensor_tensor(cu[:], csb[:], pub[:, 0:W2], op=ALU.mult),
                          reads=[csb.r, pub.r], writes=[cu.r])
                    if not lat:
                        kb.op("dve", lambda e: e.memset(cu[:, 0:1], 0.0), writes=[cu.r])
                        kb.op("dve", lambda e: e.memset(cu[:, TT + 1:W2], 0.0), writes=[cu.r])
                    elif t == 0:
                        kb.op("dve", lambda e: e.tensor_scalar(cu[:, 0:1], cu[:, 0:1], em[:, 0:1], None, op0=ALU.mult),
                              reads=[em.r, cu.r], writes=[cu.r])
                    elif t == NLT - 1:
                        kb.op("dve", lambda e: e.tensor_scalar(cu[:, TT + 1:W2], cu[:, TT + 1:W2], em[:, 1:2], None, op0=ALU.mult),
                              reads=[em.r, cu.r], writes=[cu.r])
                    kb.op("dve", lambda e, cc=cc: e.tensor_scalar(yv[:], cu[:, 0:TT], cw[:, cc, 0:1], None, op0=ALU.mult),
                          reads=[cu.r, cw.r], writes=[yv.r])
                    for k in (1, 2):
                        kb.op("dve", lambda e, cc=cc, k=k: e.scalar_tensor_tensor(
                            out=yv[:], in0=cu[:, k:k + TT], scalar=cw[:, cc, k:k + 1], in1=yv[:], op0=ALU.mult, op1=ALU.add),
                            reads=[cu.r, cw.r, yv.r], writes=[yv.r])
                    kb.op("dve", lambda e, cc=cc, pbb_c=pbb_c: e.tensor_tensor(ob[:, cc, :], yv[:], pbb_c[:, 0:TT], op=ALU.mult),
                          reads=[yv.r, pbb_c.r], writes=[ob.r])
                kw, vw = kwin[slot[t]], vwin[slot[t]]
                for qb in range(2):
                    for kv in range(2):
                        ks = slice(kv * 64, (kv + 1) * 64)
                        chunks = [(kactx[ks, ci * 128:(ci + 1) * 128], kactx.r, vactx[:, ci, kv, :], vactx.r, None) for ci in range(2)]
                        if lat:
                            for wi in (qb, qb + 1, qb + 2):
                                m_ap = None
                                if wi == qb:
                                    m_ap = wm[:, 2 if (t == 0 and qb == 0) else 0, :]
                                elif wi == qb + 2:
                                    m_ap = wm[:, 3 if (t == NLT - 1 and qb == 1) else 1, :]
                                chunks.append((kw[ks, wi * 128:(wi + 1) * 128], kw.r, vw[:, wi, kv, :], vw.r, m_ap))
                        epi = self.attn_group(at, pss, po, chunks, qaT[ks, :, qb * 128:(qb + 1) * 128], qaT.r,
                                              oaT[:, kv * 4:(kv + 1) * 4, qb * 128:(qb + 1) * 128], oaT.r,
                                              sinkrow=sinkrow[64:65, kv, :])
                        epi()
                pool6 = [pp[0], pp[1], pss[0], pss[1], po, at["pbb"]]
                pc_ = [0]

                def nxt():
                    pc_[0] += 1
                    return pool6[pc_[0] % 6]

                for oc in range(8):
                    i2 = oc % 2
                    for gi_, gbuf in ((0, gA[i2]), (1, gB[i2]), (2, gC[i2])):
                        pt = nxt()
                        proj(pt, 0, wgt, (gi_ * 8 + oc) * 128, 1, TT + 1)
                        kb.op("act", lambda e, pt=pt, gbuf=gbuf, gi_=gi_, oc=oc: e.activation(
                            out=gbuf[:], in_=pt[:, 0:TT], func=AF.Sigmoid, bias=bgt[:, gi_ * 8 + oc: gi_ * 8 + oc + 1], scale=1.0),
                            reads=[pt.r, bgt.r], writes=[gbuf.r])
                    kb.dma("sp", gcd[:, oc, t * TT:(t + 1) * TT], gC[i2][:], reads=[gC[i2].r], own=gC[i2].r)
                    pa_ = nxt()
                    for hd in range(8):
                        kb.op("pe", lambda e, hd=hd, oc=oc, pa_=pa_: e.matmul(pa_[:, 0:TT], wpa[:, hd, oc * 128:(oc + 1) * 128], oaT[:, hd, :],
                                                                            start=(hd == 0), stop=(hd == 7)),
                              reads=[wpa.r, oaT.r], writes=[pa_.r])
                    pb_ = nxt()
                    for cc in range(4):
                        kb.op("pe", lambda e, cc=cc, oc=oc, pb_=pb_: e.matmul(pb_[:, 0:TT], wpb[:, cc, oc * 128:(oc + 1) * 128], ob[:, cc, :],
                                                                            start=(cc == 0), stop=(cc == 3)),
                              reads=[wpb.r, ob.r], writes=[pb_.r])
                    kb.op("dve", lambda e, i2=i2, pa_=pa_: e.tensor_tensor(a1[i2][:], pa_[:, 0:TT], gA[i2][:], op=ALU.mult),
                          reads=[pa_.r, gA[i2].r], writes=[a1[i2].r])
                    kb.op("dve", lambda e, i2=i2, pb_=pb_: e.tensor_tensor(a2[i2][:], pb_[:, 0:TT], gB[i2][:], op=ALU.mult),
                          reads=[pb_.r, gB[i2].r], writes=[a2[i2].r])
                    kb.op("pool", lambda e, i2=i2: e.tensor_tensor(accb[i2][:], a1[i2][:], a2[i2][:], op=ALU.add),
                          reads=[a1[i2].r, a2[i2].r], writes=[accb[i2].r])
                    kb.dma("sp", accd[:, oc, t * TT:(t + 1) * TT], accb[i2][:], reads=[accb[i2].r], own=accb[i2].r)
            kb.phase_end()

    def phase_B2(self, l, xsrc, xdst):
        nc, kb, d, c = self.nc, self.kb, self.d, self.c
        xs = xsrc.rearrange("(c p) t -> p c t", p=128)
        xd = xdst.rearrange("(c p) t -> p c t", p=128)
        accd = d["ACC"].rearrange("(c p) t -> p c t", p=128)
        gcd = d["GC"].rearrange("(c p) t -> p c t", p=128)
        qcd = d["QC"].rearrange("(c p) t -> p c t", p=128)
        with ExitStack() as ph:
            kcall = T(kb, ph, "kcall", [128, KALL], BF16)
            vcall = T(kb, ph, "vcall", [128, NKC + 1, 2, 65], BF16)
            kb.op("dve", lambda e: e.memset(vcall[:].rearrange("p a b c -> p (a b c)"), 1.0), writes=[vcall.r])
            par = l % 2
            gkc, gvc = d[f"gkc{par}"], d[f"gvc{par}"]
            kb.dma("sp", kcall[:, 0:CTX], d[f"kcc{par}"][:, :], writes=[kcall.r], own=kcall.r)
            self.load_v("act", vcall, 0, 2, d[f"vcc{par}"][:, :])
            for i in range(4):
                for hh in range(2):
                    c0 = hh * 2048
                    kb.dma("sp", kcall[:, CTX + i * TOK + c0: CTX + i * TOK + c0 + 2048], gkc[i * 128:(i + 1) * 128, c0:c0 + 2048],
                           writes=[kcall.r], own=kcall.r)
                for k0 in range(0, 32, 4):
                    self.load_v("act", vcall, 2 + i * 32 + k0, 4, gvc[i * TOK + k0 * 128: i * TOK + (k0 + 4) * 128, :])
            wpc = T(kb, ph, "wpc", [64, 8, D], BF16)
            self.load_w(wpc, d["S_w_pc"][l].rearrange("(h dd) n -> dd h n", dd=64), 8, 4)
            wo = T(kb, ph, "wo", [128, 8, D], BF16)
            self.load_w(wo, d["S_w_o"][l].rearrange("(c p) n -> p c n", p=128), 8, 4)
            self.precast(self.next_key(l, "B2"), after=[kcall, vcall, wpc, wo])
            xb = [T(kb, ph, f"xb2{i}", [128, 8, TT], F32) for i in range(2)]
            qcb = [T(kb, ph, f"qcb{i}", [128, 4, TT], BF16) for i in range(2)]
            acb = [T(kb, ph, f"acb{i}", [128, 8, TT], F32) for i in range(2)]
            gcb = [T(kb, ph, f"gcb{i}", [128, 8, TT], F32) for i in range(2)]
            ocT = T(kb, ph, "ocT", [64, 8, TT], BF16)
            m1 = [T(kb, ph, f"m1{i}", [128, TT], F32) for i in range(2)]
            mrg = T(kb, ph, "mrg", [128, 8, TT], BF16)
            st = {"cnt": 0, "vr": vcall.r,
                  "dsb": T(kb, ph, "dsb2", [128, 2, 512], F32), "rbs": T(kb, ph, "rbs2", [64, 512], F32),
                  "pbb": T(kb, ph, "pbb2", [128, 512], F32, psum=True)}
            vflat = vcall[:].rearrange("p a b c -> p (a b c)")
            pTp = [T(kb, ph, f"pTp{i}", [128, 2, 512], BF16) for i in range(3)]
            psp = [T(kb, ph, f"psp{i}", [128, 2, 512], F32, psum=True) for i in range(2)]
            po = T(kb, ph, "po2", [128, 2, 512], F32, psum=True)
            pp = [psp[0][:, 0, :], psp[1][:, 0, :]]
            ppr = [psp[0].r, psp[1].r]

            def load_t(t):
                i = t % 2
                kb.dma("sp", xb[i][:], xs[:, :, t * TT:(t + 1) * TT], writes=[xb[i].r], own=xb[i].r)
                kb.dma("sp", qcb[i][:], qcd[:, :, t * TT:(t + 1) * TT], writes=[qcb[i].r], own=qcb[i].r)
                kb.dma("sp", acb[i][:], accd[:, :, t * TT:(t + 1) * TT], writes=[acb[i].r], own=acb[i].r)
                kb.dma("sp", gcb[i][:], gcd[:, :, t * TT:(t + 1) * TT], writes=[gcb[i].r], own=gcb[i].r)

            load_t(0)
            g = 0
            pend = None
            for t in range(NTILE):
                col = 1 if t == NLT else 0
                if t + 1 < NTILE:
                    load_t(t + 1)
                i = t % 2
                x, qc, ac, gc = xb[i], qcb[i], acb[i], gcb[i]
                nkc = NKC if col == 0 else 2
                for qb in range(2):
                    pend = self.attn_pair(st, psp, pTp, po, kcall, vflat, nkc, qc, qb, ocT, prev_epi=pend)
                pend()
                pend = None
                for oc in range(8):
                    pt, ptr = pp[oc % 2], ppr[oc % 2]
                    for hd in range(8):
                        kb.op("pe", lambda e, hd=hd, oc=oc, pt=pt: e.matmul(pt[:, 0:TT], wpc[:, hd, oc * 128:(oc + 1) * 128], ocT[:, hd, :],
                                                                          start=(hd == 0), stop=(hd == 7)),
                              reads=[wpc.r, ocT.r], writes=[ptr])
                    mm = m1[oc % 2]
                    kb.op("dve", lambda e, pt=pt, mm=mm, oc=oc: e.tensor_tensor(mm[:], pt[:, 0:TT], gc[:, oc, :], op=ALU.mult),
                          reads=[ptr, gc.r], writes=[mm.r])
                    kb.op("pool", lambda e, mm=mm, oc=oc: e.tensor_tensor(mrg[:, oc, :], mm[:], ac[:, oc, :], op=ALU.add),
                          reads=[mm.r, ac.r], writes=[mrg.r])
                for oc in range(8):
                    pt, ptr = pp[oc % 2], ppr[oc % 2]
                    for ch in range(8):
                        kb.op("pe", lambda e, ch=ch, oc=oc, pt=pt: e.matmul(pt[:, 0:TT], wo[:, ch, oc * 128:(oc + 1) * 128], mrg[:, ch, :],
                                                                          start=(ch == 0), stop=(ch == 7)),
                              reads=[wo.r, mrg.r], writes=[ptr])
                    kb.op("dve", lambda e, pt=pt, oc=oc: e.scalar_tensor_tensor(
                        out=x[:, oc, :], in0=pt[:, 0:TT], scalar=self.mod("modG", l, 1, oc, col), in1=x[:, oc, :],
                        op0=ALU.mult, op1=ALU.add), reads=[ptr, x.r, c["modG"].r], writes=[x.r])
                kb.dma("sp", xd[:, :, t * TT:(t + 1) * TT], x[:], reads=[x.r], own=x.r)
            kb.phase_end()


def _rope_tables_np(pos0):
    pos = np.arange(pos0, pos0 + TOK)
    row = (pos // 64).astype(np.float32)
    colp = (pos % 64).astype(np.float32)
    half = 32
    inv = (np.float32(10000.0) ** (-np.arange(0, half, 2, dtype=np.float32) / np.float32(half))).astype(np.float32)
    C = np.zeros((128, TOK), np.float32)
    S = np.zeros((128, TOK), np.float32)
    for p in range(128):
        dd = p % 64
        blk, within = dd // 32, dd % 32
        i = within % 16
        first = within < 16
        ang = ((row if blk == 0 else colp) * inv[i]).astype(np.float32)
        C[p] = np.cos(ang)
        S[p] = -np.sin(ang) if first else np.sin(ang)
    return C, S


def _perm_matrix():
    Pm = np.zeros((128, 128), np.float32)
    for m in range(128):
        dd = m % 64
        within = dd % 32
        partner = m + 16 if within < 16 else m - 16
        Pm[partner, m] = 1.0
    return Pm


_QPERM = np.concatenate([np.concatenate([np.arange(j * 64, (j + 1) * 64), np.arange((4 + j) * 64, (5 + j) * 64)])
                         for j in range(4)])


def _prep_common(inp):
    f = lambda a: np.ascontiguousarray(np.asarray(a, dtype=np.float32))
    w_in = f(inp["w_in"]).copy()
    w_in[:, :, 512:1024] = w_in[:, :, 512 + _QPERM]
    w_in[:, :, 1024:1536] = w_in[:, :, 1024 + _QPERM]
    com = {
        "w_ada": f(inp["w_ada"]),
        "b_ada": f(np.asarray(inp["b_ada"]).reshape(DEPTH, 72, 128).transpose(0, 2, 1)),
        "norm_g": f(np.asarray(inp["norm_g"]).reshape(DEPTH, 3, 8, 128).transpose(0, 3, 1, 2)),
        "ident": np.eye(128, dtype=np.float32),
        "permm": _perm_matrix(),
        "qk_g": f(np.tile(np.asarray(inp["qk_g"]), (1, 1, 2)).transpose(0, 2, 1)),
        "w_in": w_in,
        "ffn_w_gate": f(inp["ffn_w_gate"]), "ffn_w_up": f(inp["ffn_w_up"]), "ffn_w_down": f(inp["ffn_w_down"]),
        "sink_a": f(np.asarray(inp["sink_a"]).reshape(DEPTH, 1, 8)),
        "conv_w": f(np.asarray(inp["conv_w"]).reshape(DEPTH, 3, 4, 128).transpose(0, 3, 2, 1)),
        "w_pa": f(inp["w_pa"]), "w_pb": f(inp["w_pb"]), "w_pc": f(inp["w_pc"]),
        "w_gate": f(inp["w_gate"]),
        "b_gate": f(np.asarray(inp["b_gate"]).reshape(DEPTH, 24, 128).transpose(0, 2, 1)),
        "w_o": f(inp["w_o"]),
    }
    return com


def _prep_core(inp, r):
    b, q = r // 4, r % 4
    x = np.asarray(inp["x"], dtype=np.float32)
    ctx = np.asarray(inp["ctx"], dtype=np.float32)
    xT = np.concatenate([x[b, q * TOK:(q + 1) * TOK].T, ctx[b].T], axis=1)
    cvec = np.stack([np.asarray(inp["c"], np.float32)[b], np.asarray(inp["c_ctx"], np.float32)], axis=1)
    cvec = cvec.reshape(8, 128, 2).transpose(1, 0, 2)
    C, S = _rope_tables_np(q * TOK)
    kk = np.arange(128)[:, None]
    qq = np.arange(128)[None, :]
    mprev = np.where(kk >= qq, 0.0, NEG).astype(np.float32)
    mnext = np.where(kk <= qq, 0.0, NEG).astype(np.float32)
    allneg = np.full((128, 128), NEG, np.float32)
    wm = np.stack([np.tile(m, (1, 4)) for m in (mprev, mnext, mprev if q > 0 else allneg, mnext if q < 3 else allneg)], axis=1)
    emask = np.zeros((128, 2), np.float32)
    emask[:, 0] = 1.0 if q > 0 else 0.0
    emask[:, 1] = 1.0 if q < 3 else 0.0
    sel = np.zeros((128, 8), np.float32)
    if q > 0:
        sel[:, q - 1] = 1.0
    if q < 3:
        sel[:, 4 + q + 1] = 1.0
    return {"xin": np.ascontiguousarray(xT), "cvec": np.ascontiguousarray(cvec), "ropeC": C, "ropeS": S,
            "wmask": np.ascontiguousarray(wm), "emask": emask, "sel": sel}


_PROGS = {}


def _get_prog(phases, fused=False, nlayers=1):
    key = (tuple(phases), fused, nlayers)
    if key not in _PROGS:
        _PROGS[key] = Prog(set(phases), fused, nlayers).build()
    return _PROGS[key]


def _run(prog, maps):
    maps = [{k: m[k] for k in prog.in_names} for m in maps]
    res = run_bass_kernel_spmd(prog.nc, maps, core_ids=list(range(NCORE)))
    return res.results


def _layer_slice(com, l, which=None):
    out = {}
    for k, v in com.items():
        if k in ("ident", "permm"):
            out[k] = v
        elif k.startswith("ffn_") and which is not None:
            out[k] = np.ascontiguousarray(v[l:l + 1, which:which + 1])
        else:
            out[k] = np.ascontiguousarray(v[l:l + 1])
    return out


def kernel(**inp):
    com = _prep_common(inp)
    prog = _get_prog(["A", "X", "B1", "B2", "C"], fused=True, nlayers=DEPTH)
    maps = []
    for r in range(NCORE):
        m = dict(com, **_prep_core(inp, r))
        maps.append(m)
    res = _run(prog, maps)
    out = np.zeros((2, SEQ, D), np.float32)
    for r in range(NCORE):
        b, q = r // 4, r % 4
        out[b, q * TOK:(q + 1) * TOK] = res[r]["xw"][:, :TOK].T
    return out
```
